# Optimizing a Trainium2 kernel written in Bass

```python
import math
import jax, jax.numpy as jnp
from jax import lax
import numpy as np

D_MODEL = 1024
BATCH = 32
SEQ = 256
DEPTH = 2
DEC_BATCH = 4
DEC_SEQ = 4096
PAST_LEN = 256

GRID_W = 64
N_HEADS_A = 4
HEAD_DIM_A = 64
V_DIM_A = 2 * HEAD_DIM_A
QK_WIDTH = N_HEADS_A * 2 * HEAD_DIM_A
WIDTH_A = N_HEADS_A * V_DIM_A
N_GROUPS_B = 4
CHUNK = 128
GROUP_DIM_B = 128
WIDTH_B = N_GROUPS_B * GROUP_DIM_B
IN_WIDTH_0 = 2 * QK_WIDTH + WIDTH_A + 2 * WIDTH_B
MIX_WIDTH_0 = WIDTH_A + WIDTH_B
WIDTH_C = D_MODEL
CONV_W = 3
D_FF = 4 * D_MODEL
ROPE_BASE = 10000.0
ROPE_PAIRS = HEAD_DIM_A // 4
LN_EPS = 1e-5
Q_BLOCK = 128
ALPHA = (2 * DEPTH) ** 0.25
BETA = (8 * DEPTH) ** -0.25
LAMBDA_INIT_0 = 0.8 - 0.6 * math.exp(-0.3 * 0)

kernel_name = 'hybrid_diffattn_sgu_shortconv_deepnorm_step'


def layer_norm(x, g, b):
    xf = x.astype(jnp.float32)
    mu = jnp.mean(xf, -1, keepdims=True)
    var = jnp.mean(jnp.square(xf - mu), -1, keepdims=True)
    return ((xf - mu) * lax.rsqrt(var + LN_EPS)).astype(x.dtype) * g + b


def layer_norm_plain(x):
    xf = x.astype(jnp.float32)
    mu = jnp.mean(xf, -1, keepdims=True)
    var = jnp.mean(jnp.square(xf - mu), -1, keepdims=True)
    return ((xf - mu) * lax.rsqrt(var + LN_EPS)).astype(x.dtype)


def rms_norm(x, g):
    xf = x.astype(jnp.float32)
    return (xf * lax.rsqrt(jnp.mean(jnp.square(xf), -1, keepdims=True) + LN_EPS)).astype(x.dtype) * g


def adaln(cvec, w_mod, b_mod):
    m = jax.nn.silu(cvec) @ w_mod + b_mod
    return [t[:, None, :] for t in jnp.split(m, 6, axis=-1)]


def grid_angles(n):
    rows = n // GRID_W
    row = jnp.repeat(jnp.arange(rows, dtype=jnp.float32), GRID_W)
    col = jnp.tile(jnp.arange(GRID_W, dtype=jnp.float32), rows)
    inv = 1.0 / (ROPE_BASE ** (jnp.arange(ROPE_PAIRS, dtype=jnp.float32) / ROPE_PAIRS))
    return row[:, None] * inv, col[:, None] * inv


def rotate(x, ang):
    cos = jnp.cos(ang)[:, None, None, :].astype(x.dtype)
    sin = jnp.sin(ang)[:, None, None, :].astype(x.dtype)
    x1, x2 = jnp.split(x, 2, axis=-1)
    return jnp.concatenate([x1 * cos - x2 * sin, x1 * sin + x2 * cos], axis=-1)


def axial_rope(x, ang_r, ang_c):
    half = HEAD_DIM_A // 2
    return jnp.concatenate([rotate(x[..., :half], ang_r), rotate(x[..., half:], ang_c)], axis=-1)


def diff_lambda(lq1, lk1, lq2, lk2, lambda_init):
    f = jnp.float32
    return (jnp.exp(jnp.sum(lq1.astype(f) * lk1.astype(f)))
            - jnp.exp(jnp.sum(lq2.astype(f) * lk2.astype(f))) + lambda_init)


def diff_attend(q, k, v, lam):
    s = jnp.einsum('bqhid,bkhid->bhiqk', q, k).astype(jnp.float32) * (HEAD_DIM_A ** -0.5)
    p = jax.nn.softmax(s, axis=-1)
    a = p[:, :, 0] - lam * p[:, :, 1]
    return jnp.einsum('bhqk,bkhv->bqhv', a.astype(v.dtype), v)


def blocked_diff_attend(q, k, v, lam):
    b, n = q.shape[:2]
    nb = n // Q_BLOCK
    qb = jnp.moveaxis(q.reshape(b, nb, Q_BLOCK, N_HEADS_A, 2, HEAD_DIM_A), 1, 0)
    ob = lax.map(lambda qi: diff_attend(qi, k, v, lam), qb)
    return jnp.moveaxis(ob, 0, 1).reshape(b, n, N_HEADS_A, V_DIM_A)


def ab_project(h, w_in):
    b, n, _ = h.shape
    q, k, v, u, g = jnp.split(h @ w_in, [QK_WIDTH, 2 * QK_WIDTH, 2 * QK_WIDTH + WIDTH_A,
                                         2 * QK_WIDTH + WIDTH_A + WIDTH_B], axis=-1)
    return (q.reshape(b, n, N_HEADS_A, 2, HEAD_DIM_A), k.reshape(b, n, N_HEADS_A, 2, HEAD_DIM_A),
            v.reshape(b, n, N_HEADS_A, V_DIM_A), u.reshape(b, n, N_GROUPS_B, GROUP_DIM_B),
            g.reshape(b, n, N_GROUPS_B, GROUP_DIM_B))


def chunk_sgu(u, v, sgu_w, sgu_b):
    b, n = u.shape[:2]
    vc = layer_norm_plain(v).reshape(b, n // CHUNK, CHUNK, N_GROUPS_B, GROUP_DIM_B)
    mixed = jnp.einsum('gpq,bcqgd->bcpgd', sgu_w, vc) + sgu_b.T[None, None, :, :, None]
    return u * mixed.reshape(b, n, N_GROUPS_B, GROUP_DIM_B)


def ab_output(attn, u, g, subln_g, sgu_w, sgu_b, w_out):
    b, n = attn.shape[:2]
    a = rms_norm(attn, subln_g) * (1.0 - LAMBDA_INIT_0)
    s = chunk_sgu(u, g, sgu_w, sgu_b)
    return jnp.concatenate([a.reshape(b, n, WIDTH_A), s.reshape(b, n, WIDTH_B)], axis=-1) @ w_out


def short_conv_mixer(h, w_in, conv_w, w_out):
    bg, cg, xt = jnp.split(h @ w_in, 3, axis=-1)
    z = cg * xt
    n = h.shape[1]
    pad = CONV_W // 2
    zp = jnp.pad(z, ((0, 0), (pad, pad), (0, 0)))
    y = zp[:, 0:n] * conv_w[0]
    for j in range(1, CONV_W):
        y = y + zp[:, j:j + n] * conv_w[j]
    return (bg * y) @ w_out


def sq_relu_mlp(h, w1, w2):
    return jnp.square(jax.nn.relu(h @ w1)) @ w2


def setup_inputs(seed: int = 0) -> dict:
    key = jax.random.key(seed)
    ks = iter(jax.random.split(key, 64))
    f = jnp.float32
    d = D_MODEL

    def nrm(shape, scale):
        return jax.random.normal(next(ks), shape, f) * scale

    return {
        'x_prompt': nrm((BATCH, SEQ, d), 1.0),
        'x_sample': nrm((DEC_BATCH, DEC_SEQ, d), 1.0),
        'cache_k0': nrm((DEC_BATCH, PAST_LEN, N_HEADS_A, 2, HEAD_DIM_A), 1.0),
        'cache_v0': nrm((DEC_BATCH, PAST_LEN, N_HEADS_A, V_DIM_A), 1.0),
        'c': nrm((DEC_BATCH, d), 1.0),
        'c_ctx': nrm((d,), 1.0),
        'w_mod0': nrm((d, 6 * d), d ** -0.5),
        'b_mod0': nrm((6 * d,), 0.02),
        'w_in0': nrm((d, IN_WIDTH_0), d ** -0.5),
        'lambda_q1_0': nrm((HEAD_DIM_A,), 0.1),
        'lambda_k1_0': nrm((HEAD_DIM_A,), 0.1),
        'lambda_q2_0': nrm((HEAD_DIM_A,), 0.1),
        'lambda_k2_0': nrm((HEAD_DIM_A,), 0.1),
        'subln_g0': 1.0 + nrm((V_DIM_A,), 0.02),
        'sgu_w0': nrm((N_GROUPS_B, CHUNK, CHUNK), CHUNK ** -0.5),
        'sgu_b0': 1.0 + nrm((N_GROUPS_B, CHUNK), 0.02),
        'w_out0': nrm((MIX_WIDTH_0, d), BETA * MIX_WIDTH_0 ** -0.5),
        'ln_mix_g0': 1.0 + nrm((d,), 0.02),
        'ln_mix_b0': nrm((d,), 0.02),
        'w_ff1_0': nrm((d, D_FF), d ** -0.5),
        'w_ff2_0': nrm((D_FF, d), BETA * D_FF ** -0.5),
        'ln_ff_g0': 1.0 + nrm((d,), 0.02),
        'ln_ff_b0': nrm((d,), 0.02),
        'w_mod1': nrm((d, 6 * d), d ** -0.5),
        'b_mod1': nrm((6 * d,), 0.02),
        'w_in1': nrm((d, 3 * WIDTH_C), d ** -0.5),
        'conv_w1': nrm((CONV_W, WIDTH_C), CONV_W ** -0.5),
        'w_out1': nrm((WIDTH_C, d), BETA * WIDTH_C ** -0.5),
        'ln_mix_g1': 1.0 + nrm((d,), 0.02),
        'ln_mix_b1': nrm((d,), 0.02),
        'w_ff1_1': nrm((d, D_FF), d ** -0.5),
        'w_ff2_1': nrm((D_FF, d), BETA * D_FF ** -0.5),
        'ln_ff_g1': 1.0 + nrm((d,), 0.02),
        'ln_ff_b1': nrm((d,), 0.02),
    }


def reference(x_prompt, x_sample, cache_k0, cache_v0, c, c_ctx,
              w_mod0, b_mod0, w_in0, lambda_q1_0, lambda_k1_0, lambda_q2_0, lambda_k2_0,
              subln_g0, sgu_w0, sgu_b0, w_out0, ln_mix_g0, ln_mix_b0, w_ff1_0, w_ff2_0,
              ln_ff_g0, ln_ff_b0,
              w_mod1, b_mod1, w_in1, conv_w1, w_out1, ln_mix_g1, ln_mix_b1, w_ff1_1, w_ff2_1,
              ln_ff_g1, ln_ff_b1):
    shared = ((w_mod0, b_mod0, ln_mix_g0, ln_mix_b0, w_ff1_0, w_ff2_0, ln_ff_g0, ln_ff_b0),
              (w_mod1, b_mod1, ln_mix_g1, ln_mix_b1, w_ff1_1, w_ff2_1, ln_ff_g1, ln_ff_b1))
    ang_r, ang_c = grid_angles(x_sample.shape[1])
    xp, xs = x_prompt, x_sample
    new_k0 = new_v0 = None
    for layer in range(DEPTH):
        w_mod, b_mod, g_mix, bt_mix, w_ff1, w_ff2, g_ff, bt_ff = shared[layer]
        sh_p, sc_p, ga_p, shf_p, scf_p, gaf_p = adaln(c_ctx[None, :], w_mod, b_mod)
        sh_s, sc_s, ga_s, shf_s, scf_s, gaf_s = adaln(c, w_mod, b_mod)
        hp = xp * (1.0 + sc_p) + sh_p
        hs = xs * (1.0 + sc_s) + sh_s
        if layer % 2 == 0:
            lam = diff_lambda(lambda_q1_0, lambda_k1_0, lambda_q2_0, lambda_k2_0, LAMBDA_INIT_0)
            qp, kp, vp, up, gp = ab_project(hp, w_in0)
            out_p = ab_output(diff_attend(qp, kp, vp, lam), up, gp, subln_g0, sgu_w0, sgu_b0, w_out0)
            new_k0, new_v0 = kp, vp
            qs, ks_, vs, us, gs = ab_project(hs, w_in0)
            qs = axial_rope(qs, ang_r, ang_c)
            k_all = jnp.concatenate([cache_k0, axial_rope(ks_, ang_r, ang_c)], axis=1)
            v_all = jnp.concatenate([cache_v0, vs], axis=1)
            out_s = ab_output(blocked_diff_attend(qs, k_all, v_all, lam), us, gs,
                              subln_g0, sgu_w0, sgu_b0, w_out0)
        else:
            out_p = short_conv_mixer(hp, w_in1, conv_w1, w_out1)
            out_s = short_conv_mixer(hs, w_in1, conv_w1, w_out1)
        xp = layer_norm(ALPHA * xp + ga_p * out_p, g_mix, bt_mix)
        xs = layer_norm(ALPHA * xs + ga_s * out_s, g_mix, bt_mix)
        fp = sq_relu_mlp(xp * (1.0 + scf_p) + shf_p, w_ff1, w_ff2)
        fs = sq_relu_mlp(xs * (1.0 + scf_s) + shf_s, w_ff1, w_ff2)
        xp = layer_norm(ALPHA * xp + gaf_p * fp, g_ff, bt_ff)
        xs = layer_norm(ALPHA * xs + gaf_s * fs, g_ff, bt_ff)
    return (xp, xs, new_k0, new_v0)
```

```python
import math
import os
from contextlib import ExitStack

import numpy as np
import concourse.bass as bass
import concourse.mybir as mybir
from concourse.bass_utils import run_bass_kernel_spmd

F32 = mybir.dt.float32
BF16 = mybir.dt.bfloat16
AF = mybir.ActivationFunctionType
ALU = mybir.AluOpType
AX = mybir.AxisListType

D = 1024
ALPHA = 4.0 ** 0.25
LAMBDA_INIT = 0.8 - 0.6 * math.exp(-0.3 * 0)
LN_EPS = 1e-5
EPS_R = LN_EPS / (ALPHA * ALPHA)
NWIN = 17
DSZ = {F32: 4, BF16: 2}


class Op:
    __slots__ = ("eng", "fn", "deps", "dma", "semkey", "whole", "has_dep", "cnt")

    def __init__(self, eng, fn, dma, semkey, whole):
        self.eng, self.fn, self.dma, self.semkey, self.whole = eng, fn, dma, semkey, whole
        self.deps = set()
        self.has_dep = False
        self.cnt = 0


def _region(ap):
    name = ap.tensor.name
    es = DSZ.get(ap.dtype, 4)
    dims = list(ap.ap)
    off = ap.offset
    if str(ap.space) == "DRAM":
        span = sum((c - 1) * abs(s) for s, c in dims)
        return (name, 0, 1, off * es, (off + span + 1) * es)
    ps, pc = dims[0]
    ps = max(ps, 1)
    p0 = off // ps
    f0 = off % ps
    span = sum((c - 1) * abs(s) for s, c in dims[1:])
    if str(ap.space) == "PSUM":
        return (name, 0, 128, (f0 * es) // 2048 * 2048, ((f0 + span + 1) * es + 2047) // 2048 * 2048)
    return (name, p0, p0 + pc, f0 * es, (f0 + span + 1) * es)


class Prog:
    def __init__(self):
        self.ops = []
        self.recs = {}

    def _rkey(self, idx):
        o = self.ops[idx]
        return (o.eng, o.semkey)

    def _touch(self, idx, ap, write):
        name, p0, p1, lo, hi = _region(ap)
        lst = self.recs.setdefault(name, [])
        deps = self.ops[idx].deps
        keep = []
        for r in lst:
            rp0, rp1, rlo, rhi, w, rd = r
            if rp1 <= p0 or p1 <= rp0 or rhi <= lo or hi <= rlo:
                keep.append(r)
                continue
            if w is not None:
                deps.add(w)
            if write:
                deps.update(rd.values())
                if p0 <= rp0 and rp1 <= p1 and lo <= rlo and rhi <= hi:
                    continue
            keep.append(r)
        if write:
            keep.append([p0, p1, lo, hi, idx, {}])
        else:
            done = False
            for r in keep:
                if r[0] <= p0 and p1 <= r[1] and r[2] <= lo and hi <= r[3]:
                    r[5][self._rkey(idx)] = idx
                    done = True
                    break
            if not done:
                keep.append([p0, p1, lo, hi, None, {self._rkey(idx): idx}])
        self.recs[name] = keep

    def op(self, eng, fn, reads=(), writes=(), dma=False, semkey=None, whole=False):
        idx = len(self.ops)
        self.ops.append(Op(eng, fn, dma, semkey, whole))
        for a in reads:
            if a is not None and not isinstance(a, (int, float)):
                self._touch(idx, a, str(a.space) == "PSUM")
        for a in writes:
            self._touch(idx, a, True)
        o = self.ops[idx]
        o.deps.discard(idx)
        return idx


W_SPECS = [
    ("w_in0", 1024, 2560), ("w_out0", 1024, 1024), ("w_ff1_0", 1024, 4096), ("w_ff2_0", 4096, 1024),
    ("w_in1", 1024, 3072), ("w_out1", 1024, 1024), ("w_ff1_1", 1024, 4096), ("w_ff2_1", 4096, 1024),
]
VEC_IN = [("b_mod0", 6144), ("b_mod1", 6144), ("ln_mix_g0", 1024), ("ln_mix_b0", 1024), ("ln_ff_g0", 1024),
          ("ln_ff_b0", 1024), ("ln_mix_g1", 1024), ("ln_mix_b1", 1024), ("ln_ff_g1", 1024), ("ln_ff_b1", 1024)]


def build_nc(stop=99):
    nc = bass.Bass("TRN2", target_bir_lowering=False)
    P = Prog()
    es = ExitStack()

    def din(name, shape, dt=F32):
        return nc.dram_tensor(name, list(shape), dt, kind="ExternalInput").ap()

    def dout(name, shape, dt=F32):
        return nc.dram_tensor(name, list(shape), dt, kind="ExternalOutput").ap()

    xs = din("xs", [4096, 1024])
    xp = din("xp", [1024, 1024])
    ck = din("ck", [256, 512])
    cv = din("cv", [256, 512])
    cvec = din("cvec", [2, 1024])
    cosT = din("cosT", [128, 4096])
    sinT = din("sinT", [128, 4096])
    c_ident = din("c_ident", [128, 128])
    c_perm = din("c_perm", [128, 128])
    wd = {}
    for name, K, ncol in W_SPECS:
        wd[name] = din(name, [K, ncol])
    wd["w_mod0"] = din("w_mod0", [1024, 6144])
    wd["w_mod1"] = din("w_mod1", [1024, 6144])
    vd = {n: din(n, [ln]) for n, ln in VEC_IN}
    conv_w1 = din("conv_w1", [3, 1024])
    lam_in = {n: din(n, [64]) for n in ("lambda_q1_0", "lambda_k1_0", "lambda_q2_0", "lambda_k2_0")}
    subln_g0 = din("subln_g0", [128])
    sgu_w0 = din("sgu_w0", [4, 128, 128])
    sgu_b0 = din("sgu_b0", [4, 128])

    ys = dout("ys", [NWIN * 128, 1024])
    yp = dout("yp", [1024, 1024])
    nk = dout("nk", [1024, 512])
    nv = dout("nv", [1024, 512])

    blk_of = {}
    nblk = 0
    for name, K, ncol in W_SPECS:
        if K == 1024:
            n = ncol // 512
        else:
            n = 8
        blk_of[name] = (nblk, n)
        nblk += n
    wscr = nc.dram_tensor("wscr", [nblk, 128, 4096], BF16, kind="Internal").ap()

    def sb(name, shape, dt):
        return es.enter_context(nc.sbuf_tensor(name, list(shape), dt))

    KT = sb("KT", [128, 4, 4352], BF16)
    VV = sb("VV", [128, 34, 516], BF16)
    XR = sb("XR", [128, 2, 8, 512], F32)
    XIN = sb("XIN", [128, 2, 1024], F32)
    STG = sb("STG", [128, 2, 1024], F32)
    HT = sb("HT", [128, 8, 512], BF16)
    HX = sb("HX", [128, 8, 2], BF16)
    TAB = sb("TAB", [128, 2, 512], F32)
    T1 = sb("T1", [128, 2, 512], F32)
    T2 = sb("T2", [128, 2, 512], F32)
    T3 = sb("T3", [128, 2, 512], F32)
    QRAW = T2
    RB = T1[:, 0, :].bitcast(BF16).rearrange("p (a n) -> p a n", a=2)
    ARENA = sb("ARENA", [128, 10240], BF16)
    WS = sb("WS", [128, 4, 4096], BF16)
    IDF = sb("IDF", [128, 128], F32)
    IDB = sb("IDB", [128, 128], BF16)
    PERM = sb("PERM", [128, 128], F32)
    WST = sb("WST", [128, 4, 128], BF16)
    GSUB = sb("GSUB", [128, 128], F32)
    VB1 = T2[:, 0, 0:128]
    VB2 = T2[:, 1, 0:128]
    COLS1 = sb("COLS1", [128, 96], F32)
    COLS2 = sb("COLS2", [128, 104], F32)
    SIL = sb("SIL", [128, 8, 2], BF16)
    MODT = sb("MODT", [128, 2, 48, 2], F32)
    NVD = 8
    DCOL = sb("DCOL", [128, 2, 2, NVD, 8], F32)
    BSGU = sb("BSGU", [1, 512], F32)
    ONER = sb("ONER", [1, 128], F32)
    LAMT = T3[:, 0, 0:256].rearrange("p (a b) -> p a b", a=4)
    LAMS = sb("LAMS", [128, 8], F32)
    NEGC = sb("NEGC", [128, 1], F32)
    JUNK = sb("JUNK", [128, 128], F32)
    AN = sb("AN", [128, 4, 128], BF16)
    SM = sb("SM", [128, 8, 8], F32)
    ST = sb("ST", [128, 2, 4, 6], F32)
    MV = sb("MV", [128, 2, 4, 2], F32)
    RS = sb("RS", [128, 2, 8], F32)
    ZSAVE = sb("ZSAVE", [128, 8], F32)
    SMZ = sb("SMZ", [128, 8, 2], F32)
    SGW = T1[:, 0, :].rearrange("p (a b) -> p a b", a=4)

    BLK1 = sb("BLK1", [128, 128], BF16)
    ONESB = sb("ONESB", [128, 128], BF16)
    SQB = sb("SQB", [128, 2, 512], BF16)
    KCOLS = sb("KCOLS", [128, 48], F32)
    QCOLS = sb("QCOLS", [128, 8], F32)
    MAXC = sb("MAXC", [128, 2], F32)
    S1 = sb("S1", [1, 16], F32)

    PS01 = es.enter_context(nc.psum_tensor("PS01", [128, 1024], F32))
    PS23 = es.enter_context(nc.psum_tensor("PS23", [128, 1024], F32))
    BANK = [PS01[:, 0:512], PS01[:, 512:1024], PS23[:, 0:512], PS23[:, 512:1024]]
    BANK += [es.enter_context(nc.psum_tensor(f"B{i}", [128, 512], F32))[:] for i in range(4, 8)]

    QT = ARENA[:, 0:2048].rearrange("p (c n) -> p c n", c=4)
    VC = ARENA[:, 2048:4096].rearrange("p (c n) -> p c n", c=4)
    CAT1 = ARENA[:, 0:4096].rearrange("p (c n) -> p c n", c=8)
    ET = ARENA[:, 4096:6144].rearrange("p (c n) -> p c n", c=4)
    UT = ARENA[:, 6144:10240].bitcast(F32).rearrange("p (c n) -> p c n", c=4)
    HID = ARENA[:, 0:8192].rearrange("p (c n) -> p c n", c=16)
    KTP = KT[:, :, 0:512]

    bank_ctr = [0]

    reserved = set()

    def nb():
        while (bank_ctr[0] % 8) in reserved:
            bank_ctr[0] += 1
        b = BANK[bank_ctr[0] % 8]
        bank_ctr[0] += 1
        return b

    def nb_reserve():
        while (bank_ctr[0] % 8) in reserved:
            bank_ctr[0] += 1
        i = bank_ctr[0] % 8
        bank_ctr[0] += 1
        reserved.add(i)
        return i

    def mm(out, lhsT, rhs, start=True, stop=True, **kw):
        P.op("pe", lambda e: e.matmul(out, lhsT, rhs, start=start, stop=stop, **kw), [lhsT, rhs], [out])

    def tr(out, in_, ident):
        P.op("pe", lambda e: e.transpose(out, in_, ident), [in_, ident], [out])

    def act(out, in_, func, bias=None, scale=None, accum=None):
        kw = {}
        if bias is not None:
            kw["bias"] = bias
        if scale is not None:
            kw["scale"] = scale
        if accum is not None:
            kw["accum_out"] = accum
        wr = [out] + ([accum] if accum is not None else [])
        P.op("act", lambda e: e.activation(out, in_, func, **kw), [in_, bias, scale], wr)

    def ts(eng, out, in0, s1, s2, op0, op1=None):
        if op1 is None:
            P.op(eng, lambda e: e.tensor_scalar(out, in0, s1, None, op0), [in0, s1], [out])
        else:
            P.op(eng, lambda e: e.tensor_scalar(out, in0, s1, s2, op0, op1), [in0, s1, s2], [out])

    def tt(eng, out, in0, in1, op):
        P.op(eng, lambda e: e.tensor_tensor(out, in0, in1, op), [in0, in1], [out])

    def stt(out, in0, scalar, in1, op0, op1):
        P.op("dve", lambda e: e.scalar_tensor_tensor(out, in0, scalar, in1, op0, op1), [in0, scalar, in1], [out])

    def cp(eng, out, in_):
        if eng == "act":
            P.op("act", lambda e: e.activation(out, in_, AF.Copy), [in_], [out])
        else:
            P.op(eng, lambda e: e.tensor_copy(out, in_), [in_], [out])

    def dma(q, out, in_, semkey, whole=False, **kw):
        P.op(q, lambda e: e.dma_start(out=out, in_=in_, **kw), [in_], [out], dma=True, semkey=semkey, whole=whole)

    def memset(eng, ap, val):
        P.op(eng, lambda e: e.memset(ap, val), [], [ap])

    def w_src(name, b):
        K = dict((n, k) for n, k, _ in W_SPECS)[name]
        w = wd[name]
        if K == 1024:
            return w[:, b * 512:(b + 1) * 512].rearrange("(kc p) n -> p kc n", p=128), 8, 512
        hf, mb = b // 4, b % 4
        return (w[hf * 2048:(hf + 1) * 2048, mb * 256:(mb + 1) * 256].rearrange("(kc p) n -> p kc n", p=128),
                16, 256)

    def scr_view(name, b):
        base, n = blk_of[name]
        _, kc, ncol = w_src(name, b)
        return wscr[base + b].rearrange("p (kc n) -> p kc n", kc=kc)

    def cast_all(names):
        for name in names:
            base, n = blk_of[name]
            for b in range(n):
                src, kc, ncol = w_src(name, b)
                dma("pool", scr_view(name, b), src, semkey="cast_" + name, whole=True)

    ws_ctr = [0]

    def wload(name, b):
        s = ws_ctr[0] % 4
        ws_ctr[0] += 1
        _, kc, ncol = w_src(name, b)
        dst = WS[:, s, :].rearrange("p (kc n) -> p kc n", kc=kc)
        dma("sp", dst, scr_view(name, b), semkey=f"ws{s}")
        return dst

    def wload_mod(layer, b):
        s = ws_ctr[0] % 4
        ws_ctr[0] += 1
        dst = WS[:, s, :].rearrange("p (kc n) -> p kc n", kc=8)
        src = wd[f"w_mod{layer}"][:, b * 512:(b + 1) * 512].rearrange("(kc p) n -> p kc n", p=128)
        dma("pool", dst, src, semkey=f"wm{s}")
        return dst

    def vrows(ap1d, n):
        return ap1d.rearrange("(c p) -> c p", p=128)

    dma("sp", IDF[:], c_ident, "setup", whole=True)
    dma("sp", PERM[:], c_perm, "setup", whole=True)
    dma("sp", VB1[0:48, :], vrows(vd["b_mod0"], 48), "setup", whole=True)
    dma("sp", VB1[48:96, :], vrows(vd["b_mod1"], 48), "setup", whole=True)
    r = 0
    vb2_off = {}
    for l in range(2):
        for nm in ("ln_mix_g", "ln_mix_b", "ln_ff_g", "ln_ff_b"):
            dma("sp", VB2[r:r + 8, :], vrows(vd[f"{nm}{l}"], 8), "setup", whole=True)
            vb2_off[f"{nm}{l}"] = r
            r += 8
    dma("sp", VB2[r:r + 24, :], conv_w1.rearrange("t (c p) -> (t c) p", p=128), "setup", whole=True)
    vb2_off["conv"] = r
    r += 24
    dma("sp", VB2[r:r + 16, :], cvec.rearrange("b (c p) -> (b c) p", p=128), "setup", whole=True)
    vb2_off["cvec"] = r
    r += 16
    assert r == 104
    for i, n in enumerate(("lambda_q1_0", "lambda_k1_0", "lambda_q2_0", "lambda_k2_0")):
        dma("sp", LAMT[:, i, :], lam_in[n].partition_broadcast(128), "setup", whole=True)
    dma("sp", GSUB[:], subln_g0.partition_broadcast(128), "setup", whole=True)
    dma("sp", SGW, sgu_w0.rearrange("g p q -> p g q"), "setup", whole=True)
    dma("sp", BSGU[:], sgu_b0.rearrange("g p -> (g p)").partition_broadcast(1), "setup", whole=True)


    memset("dve", ONER[:], 1.0)
    memset("dve", NEGC[:], 0.0)
    memset("dve", ZSAVE[:], 0.0)
    memset("dve", VV[:].rearrange("p k (h c) -> p k h c", h=4)[:, :, :, 128:129], 1.0)
    cp("dve", IDB[:], IDF[:])
    memset("dve", BLK1[:], 0.0)
    memset("dve", ONESB[:], 1.0 / 1024.0)
    memset("dve", BLK1[0:64, 0:64], 1.0)
    memset("dve", BLK1[64:128, 64:128], 1.0)

    b0 = nb()
    tr(b0[:, 0:96], VB1[0:96, :], IDF[0:96, 0:96])
    cp("dve", COLS1[:], b0[:, 0:96])
    b1 = nb()
    tr(b1[:, 0:104], VB2[0:104, :], IDF[0:104, 0:104])
    cp("dve", COLS2[:], b1[:, 0:104])

    def col2(name, c):
        o = vb2_off[name] + c
        return COLS2[:, o:o + 1]

    co = vb2_off["cvec"]
    for cd in range(2):
        act(SIL[:, :, cd], COLS2[:, co + cd * 8: co + cd * 8 + 8], AF.Silu)

    bw = nb()
    for g in range(4):
        tr(bw[:, g * 128:(g + 1) * 128], SGW[:, g, :], IDF[:])
    cp("dve", WST[:].rearrange("p g q -> p (g q)"), bw[:, 0:512])

    tt("dve", LAMT[:, 0, :], LAMT[:, 0, :], LAMT[:, 1, :], ALU.mult)
    tt("dve", LAMT[:, 2, :], LAMT[:, 2, :], LAMT[:, 3, :], ALU.mult)
    P.op("dve", lambda e: e.reduce_sum(LAMS[:, 0:1], LAMT[:, 0, :], AX.X), [LAMT[:, 0, :]], [LAMS[:, 0:1]])
    P.op("dve", lambda e: e.reduce_sum(LAMS[:, 1:2], LAMT[:, 2, :], AX.X), [LAMT[:, 2, :]], [LAMS[:, 1:2]])
    act(LAMS[:, 2:4], LAMS[:, 0:2], AF.Exp)
    tt("dve", LAMS[:, 4:5], LAMS[:, 3:4], LAMS[:, 2:3], ALU.subtract)
    ts("dve", LAMS[:, 5:6], LAMS[:, 4:5], -LAMBDA_INIT, None, ALU.add)
    NEGLAM = LAMS[:, 5:6]
    ts("dve", GSUB[:], GSUB[:], 1.0 - LAMBDA_INIT, None, ALU.mult)

    def modulation(layer, blocks, derive):
        bm = nb()
        for b in blocks:
            wb = wload_mod(layer, b)
            for j4 in range(4):
                j = b * 4 + j4
                for kc in range(8):
                    mm(bm[:, 2 * j:2 * j + 2], wb[:, kc, j4 * 128:(j4 + 1) * 128], SIL[:, kc, :],
                       start=(kc == 0), stop=(kc == 7))
        j0, j1 = blocks[0] * 4, blocks[-1] * 4 + 4
        bmv = bm[:, 0:96].rearrange("p (j c) -> p j c", c=2)
        for cd in range(2):
            tt("dve", MODT[:, layer, j0:j1, cd], bmv[:, j0:j1, cd], COLS1[:, layer * 48 + j0:layer * 48 + j1],
               ALU.add)
        for cd in range(2):
            M = lambda w: MODT[:, layer, w * 8:(w + 1) * 8, cd]
            Dv = lambda k: DCOL[:, layer, cd, k, :]
            gm = COLS2[:, vb2_off[f"ln_mix_g{layer}"]: vb2_off[f"ln_mix_g{layer}"] + 8]
            bmx = COLS2[:, vb2_off[f"ln_mix_b{layer}"]: vb2_off[f"ln_mix_b{layer}"] + 8]
            if "a" in derive:
                ts("dve", Dv(0), M(1), 1.0, None, ALU.add)
                cp("dve", Dv(1), M(0))
            if "b" in derive:
                ts("dve", Dv(2), M(2), 1.0 / ALPHA, None, ALU.mult)
                ts("dve", Dv(6), M(4), 1.0, None, ALU.add)
                tt("dve", Dv(3), gm, Dv(6), ALU.mult)
                tt("dve", Dv(4), bmx, Dv(6), ALU.mult)
                tt("dve", Dv(4), Dv(4), M(3), ALU.add)
                ts("dve", Dv(5), M(5), 1.0 / ALPHA, None, ALU.mult)

    def dc(layer, cd, k, c):
        return DCOL[:, layer, cd, k, c:c + 1]

    modulation(0, [0, 1, 2, 3], "a")
    cast_all(["w_in0"])

    xin_ctr = [0]

    def load_tile(src_rows, cd, layer, dstx, dsth, tt_):
        k = xin_ctr[0] % 2
        xin_ctr[0] += 1
        dma("sp", XIN[:, k, :], src_rows, semkey=f"xin{k}")
        for half in range(2):
            bk = nb()
            for cc in range(4):
                c = half * 4 + cc
                tr(bk[:, cc * 128:(cc + 1) * 128], XIN[:, k, c * 128:(c + 1) * 128], IDF[:])
            if dstx is not None and os.environ.get("DBG_NOX") != "1":
                if os.environ.get("DBG_NOX") == "2":
                    for cc in range(4):
                        cp("dve", dstx[:, half * 4 + cc, tt_ * 128:(tt_ + 1) * 128], bk[:, cc * 128:(cc + 1) * 128])
                else:
                    cp("dve", dstx[:, half * 4:half * 4 + 4, tt_ * 128:(tt_ + 1) * 128],
                       bk[:, 0:512].rearrange("p (c n) -> p c n", c=4))
            for cc in range(4):
                c = half * 4 + cc
                act(dsth[:, c, tt_ * 128:(tt_ + 1) * 128], bk[:, cc * 128:(cc + 1) * 128], AF.Identity,
                    bias=dc(layer, cd, 1, c), scale=dc(layer, cd, 0, c))

    def proj_fm(blk, m4, inT, N, bank):
        for kc in range(8):
            mm(bank[:, 0:N], blk[:, kc, m4 * 128:(m4 + 1) * 128], inT[:, kc, 0:N], start=(kc == 0), stop=(kc == 7))

    def proj_tm(blk, inT, tt_, bank):
        for kc in range(8):
            mm(bank[:, 0:512], inT[:, kc, tt_ * 128:(tt_ + 1) * 128], blk[:, kc, 0:512], start=(kc == 0),
               stop=(kc == 7))

    sq_ctr = [0]

    def norm_update(src, N, dstcol):
        k = sq_ctr[0] % 2
        sq_ctr[0] += 1
        act(SQB[:, k, 0:N], src, AF.Square)
        bkn = nb()
        mm(bkn[:, 0:N], BLK1[:], SQB[:, k, 0:N])
        P.op("dve", lambda e: e.reduce_max(dstcol, bkn[:, 0:N], AX.X), [bkn[:, 0:N]], [dstcol])

    def cols_max(cols, dst):
        P.op("dve", lambda e: e.reduce_max(dst, cols, AX.X), [cols], [dst])

    def bound_finalize():
        for j in range(2):
            bkt = nb()
            tr(bkt[0:1, 0:128], MAXC[:, j:j + 1], IDF[:])
            P.op("dve", (lambda e, bkt=bkt, j=j: e.reduce_max(S1[0:1, j:j + 1], bkt[0:1, 0:128], AX.X)),
                 [bkt[0:1, 0:128]], [S1[0:1, j:j + 1]])
        tt("dve", S1[0:1, 2:3], S1[0:1, 0:1], S1[0:1, 1:2], ALU.mult)
        act(S1[0:1, 3:4], S1[0:1, 2:3], AF.Ln)
        act(S1[0:1, 4:5], S1[0:1, 3:4], AF.Exp, scale=0.5)
        ts("dve", S1[0:1, 5:6], S1[0:1, 4:5], -1.01 / 8.0, None, ALU.mult)
        ts("dve", S1[0:1, 6:7], S1[0:1, 4:5], -1.01 / 8.0, None, ALU.mult)

    def bound_broadcast():
        bkb = nb()
        mm(bkb[:, 0:2], ONER[0:1, :], S1[0:1, 5:7])
        cp("dve", NEGC[:, 0:1], bkb[:, 0:1])

    qr_ctr = [0]

    def rope(bank, N, dst, normcol=None):
        k = qr_ctr[0] % 2
        qr_ctr[0] += 1
        cp("act", QRAW[:, k, 0:N], bank[:, 0:N])
        if normcol is not None:
            norm_update(QRAW[:, k, 0:N], N, normcol)
        b2 = nb()
        mm(b2[:, 0:N], PERM[:], QRAW[:, k, 0:N])
        tt("dve", T1[:, k, 0:N], QRAW[:, k, 0:N], TAB[:, 0, 0:N], ALU.mult)
        tt("dve", T3[:, k, 0:N], b2[:, 0:N], TAB[:, 1, 0:N], ALU.mult)
        tt("dve", dst, T1[:, k, 0:N], T3[:, k, 0:N], ALU.add)

    def sgu_norm_tile(bank, tt_):
        k = tt_ % 2
        for gi in range(4):
            P.op("dve", (lambda e, gi=gi: e.bn_stats(ST[:, k, gi, :], bank[:, gi * 128:(gi + 1) * 128])),
                 [bank[:, gi * 128:(gi + 1) * 128]], [ST[:, k, gi, :]])
        for gi in range(4):
            P.op("dve", (lambda e, gi=gi: e.bn_aggr(MV[:, k, gi, :], ST[:, k, gi, :])), [ST[:, k, gi, :]],
                 [MV[:, k, gi, :]])
        act(RS[:, k, 0:4], MV[:, k, :, 1], AF.Ln, bias=LN_EPS)
        act(RS[:, k, 4:8], RS[:, k, 0:4], AF.Exp, scale=-0.5)
        for gi in range(4):
            ts("dve", VC[:, tt_, gi * 128:(gi + 1) * 128], bank[:, gi * 128:(gi + 1) * 128],
               MV[:, k, gi, 0:1], RS[:, k, 4 + gi:5 + gi], ALU.subtract, ALU.mult)

    SBK = [[BANK[0], BANK[1]], [BANK[2], BANK[3]]]
    OBK = [BANK[4], BANK[5], BANK[6]]
    MISCB = BANK[7].bitcast(BF16)
    SPAIR = [PS01, PS23]

    deferred = []
    T2f = T2[:].rearrange("p a n -> p (a n)")
    T3f = T3[:].rearrange("p a n -> p (a n)")
    AEP = T3[:, 1, :].rearrange("p (s n) -> p s n", s=4)

    def osacc(j):
        return T2f[:, j * 129:(j + 1) * 129] if j < 6 else T3f[:, (j - 6) * 129:(j - 5) * 129]

    def run_deferred(n=None):
        k = 0
        while deferred and (n is None or k < n):
            deferred.pop(0)()
            k += 1

    def attention(qT, q0, Nq, ktiles, cat):
        nsub = Nq // 128
        nacc = 2 * nsub
        nk_ = len(ktiles)
        hooks = {4, 9, 14, 19, 24}
        for h in range(4):
            accs = {}
            first_in_bank = {}
            for sub in range(nsub):
                for i in range(2):
                    j = sub * 2 + i
                    bkk = OBK[j // 3]
                    accs[(sub, i)] = bkk[:, (j % 3) * 129:(j % 3) * 129 + 129]
                    first_in_bank[(sub, i)] = (j % 3 == 0)
            for kt in range(nk_ + 1):
                if kt < nk_:
                    Kh = ktiles[kt][0](h)
                    for i in range(2):
                        sbank = SBK[kt % 2][i]
                        mm(sbank[:, 0:Nq], Kh[i * 64:(i + 1) * 64, :], qT[i * 64:(i + 1) * 64, h, q0:q0 + Nq])
                    act(ET[:, (kt % 2) * 2:(kt % 2) * 2 + 2, 0:Nq],
                        SPAIR[kt % 2][:, :].rearrange("p (i n) -> p i n", i=2)[:, :, 0:Nq],
                        AF.Exp, bias=NEGC[:, 0:1], scale=0.125)
                if kt >= 1:
                    k1 = kt - 1
                    Vh = ktiles[k1][1](h)
                    for sub in range(nsub):
                        for i in range(2):
                            mm(accs[(sub, i)], ET[:, (k1 % 2) * 2 + i, sub * 128:(sub + 1) * 128], Vh,
                               start=(k1 == 0 and first_in_bank[(sub, i)]), stop=(k1 == nk_ - 1),
                               skip_group_check=True)
                if kt in hooks:
                    run_deferred(1)
            run_deferred()
            for b_ in range((nacc + 2) // 3):
                n_in = min(3, nacc - 3 * b_)
                dst = T2f[:, b_ * 387:b_ * 387 + n_in * 129] if b_ < 2 else T3f[:, 0:n_in * 129]
                cp("dve", dst, OBK[b_][:, 0:n_in * 129])
            p_ = h % 2
            R = SM[:, p_ * 2, :]
            SS = SM[:, p_ * 2 + 1, 0:4]
            LNV = SM[:, 4 + p_, 0:4]
            RSTD = SM[:, 4 + p_, 4:8]

            def stage1(nsub=nsub, nacc=nacc, R=R, SS=SS):
                n6 = min(nacc, 6)
                lcol = T2f[:, 0:n6 * 129].rearrange("p (j c) -> p j c", c=129)[:, :, 128:129]
                rout = R[:, 0:n6].rearrange("p (j o) -> p j o", o=1)
                P.op("dve", lambda e: e.reciprocal(rout, lcol), [lcol], [rout])
                if nacc > 6:
                    lcol2 = T3f[:, 0:258].rearrange("p (j c) -> p j c", c=129)[:, :, 128:129]
                    rout2 = R[:, 6:8].rearrange("p (j o) -> p j o", o=1)
                    P.op("dve", lambda e: e.reciprocal(rout2, lcol2), [lcol2], [rout2])
                Rv = R[:, 0:nacc].rearrange("p (s i) -> p s i", i=2)
                ts("dve", Rv[:, :, 1], Rv[:, :, 1], NEGLAM, None, ALU.mult)
                for sub in range(nsub):
                    ts("dve", AEP[:, sub, :], osacc(2 * sub)[:, 0:128], R[:, 2 * sub:2 * sub + 1], None, ALU.mult)
                    stt(AEP[:, sub, :], osacc(2 * sub + 1)[:, 0:128], R[:, 2 * sub + 1:2 * sub + 2], AEP[:, sub, :],
                        ALU.mult, ALU.add)
                    P.op("dve", (lambda e, sub=sub: e.scalar_tensor_tensor(JUNK[:], AEP[:, sub, :], 1.0, AEP[:, sub, :],
                                                                           ALU.mult, ALU.mult, accum_out=SS[:, sub:sub + 1])),
                         [AEP[:, sub, :]], [JUNK[:], SS[:, sub:sub + 1]])

            def stage2(nsub=nsub, SS=SS, LNV=LNV, RSTD=RSTD):
                act(LNV[:, 0:nsub], SS[:, 0:nsub], AF.Ln, bias=LN_EPS, scale=1.0 / 128.0)
                act(RSTD[:, 0:nsub], LNV[:, 0:nsub], AF.Exp, scale=-0.5)

            def stage3(nsub=nsub, RSTD=RSTD):
                for sub in range(nsub):
                    stt(AN[:, sub, :], AEP[:, sub, :], RSTD[:, sub:sub + 1], GSUB[:], ALU.mult, ALU.mult)

            def stage4(nsub=nsub):
                for sub in range(nsub):
                    tr(MISCB[:, sub * 128:(sub + 1) * 128], AN[:, sub, :], IDB[:])

            def stage5(h=h, q0=q0, Nq=Nq, cat=cat):
                cp("dve", cat[:, h, q0:q0 + Nq], MISCB[:, 0:Nq])

            deferred.extend([stage1, stage2, stage3, stage4, stage5])

    def sgu_mix(N, cat):
        ntile = N // 128
        for gi in range(4):
            bk = nb()
            for t_ in range(ntile):
                mm(bk[:, t_ * 128:(t_ + 1) * 128], VC[:, t_, gi * 128:(gi + 1) * 128], WST[:, gi, :],
                   start=True, stop=False)
                mm(bk[:, t_ * 128:(t_ + 1) * 128], ONER[0:1, :], BSGU[0:1, gi * 128:(gi + 1) * 128],
                   start=False, stop=True)
            tt("dve", cat[:, 4 + gi, 0:N], bk[:, 0:N], UT[:, gi, 0:N], ALU.mult)

    pending_release = []

    class StatAcc:
        def __init__(self, XS, N):
            self.XS, self.N = XS, N
            while pending_release:
                pending_release.pop().release()
            self.im, self.ie = nb_reserve(), nb_reserve()
            self.Bm, self.Be = BANK[self.im], BANK[self.ie]
            self.pending = []
            self.n = 0

        def add(self, c):
            self.pending.append(c)
            if len(self.pending) > 1:
                self._emit(self.pending.pop(0))

        def _emit(self, c):
            N, XS = self.N, self.XS
            k = self.n % 2
            act(SQB[:, k, 0:N], XS[:, c, 0:N], AF.Square)
            cp("pool", RB[:, k, 0:N], XS[:, c, 0:N])
            mm(self.Bm[:, 0:N], ONESB[:], RB[:, k, 0:N], start=(self.n == 0), stop=(self.n == 7))
            mm(self.Be[:, 0:N], ONESB[:], SQB[:, k, 0:N], start=(self.n == 0), stop=(self.n == 7))
            self.n += 1

        def finish(self):
            while self.pending:
                self._emit(self.pending.pop(0))
            assert self.n == 8

        def release(self):
            reserved.discard(self.im)
            reserved.discard(self.ie)

    def out_proj(name, cat, XS, N, layer, cd):
        st = StatAcc(XS, N)
        for ob in range(2):
            blk = wload(name, ob)
            for m4 in range(4):
                m = ob * 4 + m4
                bk = nb()
                proj_fm(blk, m4, cat, N, bk)
                stt(XS[:, m, 0:N], bk[:, 0:N], dc(layer, cd, 2, m), XS[:, m, 0:N], ALU.mult, ALU.add)
                st.add(m)
        return st

    def layernorm(XS, N, layer, cd, which, want_h, st):
        gname = f"ln_{which}_g{layer}"
        bname = f"ln_{which}_b{layer}"
        st.finish()
        Bm, Be = st.Bm, st.Be
        cp("act", T2[:, 0, 0:N], Bm[:, 0:N])
        tt("dve", T2[:, 1, 0:N], T2[:, 0, 0:N], Bm[:, 0:N], ALU.mult)
        tt("dve", T2[:, 1, 0:N], Be[:, 0:N], T2[:, 1, 0:N], ALU.subtract)
        act(T2[:, 1, 0:N], T2[:, 1, 0:N], AF.Ln, bias=EPS_R)
        act(Be[:, 0:N], T2[:, 1, 0:N], AF.Exp, scale=-0.5)
        for c in range(8):
            tt("dve", XS[:, c, 0:N], XS[:, c, 0:N], Bm[:, 0:N], ALU.subtract)
            tt("dve", XS[:, c, 0:N], XS[:, c, 0:N], Be[:, 0:N], ALU.mult)
            if want_h:
                act(HT[:, c, 0:N], XS[:, c, 0:N], AF.Identity, bias=dc(layer, cd, 4, c), scale=dc(layer, cd, 3, c))
            act(XS[:, c, 0:N], XS[:, c, 0:N], AF.Identity, bias=col2(bname, c), scale=col2(gname, c))
        pending_release.append(st)

    t_ctr = [0]

    def ffn(XS, N, layer, cd):
        n1, n2 = f"w_ff1_{layer}", f"w_ff2_{layer}"
        st = None
        for hf in range(2):
            if hf == 1:
                st = StatAcc(XS, N)
            for j4 in range(4):
                blk = wload(n1, hf * 4 + j4)
                bks = [nb() for _ in range(4)]
                for kc in range(8):
                    for m4 in range(4):
                        mm(bks[m4][:, 0:N], blk[:, kc, m4 * 128:(m4 + 1) * 128], HT[:, kc, 0:N], start=(kc == 0),
                           stop=(kc == 7))
                for m4 in range(4):
                    bk = bks[m4]
                    k = t_ctr[0] % 2
                    t_ctr[0] += 1
                    cp("act", T1[:, k, 0:N], bk[:, 0:N])
                    stt(HID[:, j4 * 4 + m4, 0:N], bk[:, 0:N], 0.0, T1[:, k, 0:N], ALU.max, ALU.mult)
            for mb in range(4):
                blk = wload(n2, hf * 4 + mb)
                for m2 in range(2):
                    m = mb * 2 + m2
                    bk = nb()
                    for kc in range(16):
                        mm(bk[:, 0:N], blk[:, kc, m2 * 128:(m2 + 1) * 128], HID[:, kc, 0:N], start=(kc == 0),
                           stop=(kc == 15))
                    stt(XS[:, m, 0:N], bk[:, 0:N], dc(layer, cd, 5, m), XS[:, m, 0:N], ALU.mult, ALU.add)
                    if hf == 1:
                        st.add(m)
        return st

    stg_ctr = [0]

    def store_tiles(XS, ntile, dst_rows_fn):
        for t_ in range(ntile):
            k = stg_ctr[0] % 2
            stg_ctr[0] += 1
            for half in range(2):
                bk = nb()
                for cc in range(4):
                    tr(bk[:, cc * 128:(cc + 1) * 128], XS[:, half * 4 + cc, t_ * 128:(t_ + 1) * 128], IDF[:])
                cp("act" if half == 0 else "dve", STG[:, k, half * 512:(half + 1) * 512], bk[:, 0:512])
            dma("pool", dst_rows_fn(t_), STG[:, k, :], semkey=f"stg{k}")

    def layer0_group(src_fn, ntile, cd, slot, sample, tab0, ktiles_fn, kvout_fn):
        N = ntile * 128
        XS = XR[:, slot]
        for t_ in range(ntile):
            load_tile(src_fn(t_), cd, 0, XS, HT, t_)
        if sample:
            dma("sp", TAB[:, 0, 0:N], cosT[:, tab0:tab0 + N], semkey="tabc")
            dma("sp", TAB[:, 1, 0:N], sinT[:, tab0:tab0 + N], semkey="tabs")
        if stop < 3.02:
            return
        blk = wload("w_in0", 0)
        for m4 in range(4):
            bk = nb()
            proj_fm(blk, m4, HT, N, bk)
            if sample:
                rope(bk, N, QT[:, m4, 0:N], QCOLS[:, m4:m4 + 1])
            else:
                cp("act", QT[:, m4, 0:N], bk[:, 0:N])
                norm_update(QT[:, m4, 0:N], N, QCOLS[:, m4:m4 + 1])
        cols_max(QCOLS[:, 0:4], MAXC[:, 0:1])
        if not sample:
            blk = wload("w_in0", 1)
            for m4 in range(4):
                bk = nb()
                proj_fm(blk, m4, HT, N, bk)
                cp("act", KTP[:, m4, 0:N], bk[:, 0:N])
                norm_update(KTP[:, m4, 0:N], N, KCOLS[:, m4:m4 + 1])
            cols_max(KCOLS[:, 0:4], MAXC[:, 1:2])
            blkv = wload("w_in0", 2)
            for t_ in range(ntile):
                k = stg_ctr[0] % 2
                stg_ctr[0] += 1
                bk = nb()
                proj_tm(blk, HT, t_, bk)
                cp("dve", STG[:, k, 0:512], bk[:, 0:512])
                bk2 = nb()
                proj_tm(blkv, HT, t_, bk2)
                cp("act", STG[:, k, 512:1024], bk2[:, 0:512])
                cp("dve", VV[:, t_, :].rearrange("p (h c) -> p h c", h=4)[:, :, 0:128],
                   bk2[:, 0:512].rearrange("p (h c) -> p h c", h=4))
                kd, vd_ = kvout_fn(t_)
                dma("pool", kd, STG[:, k, 0:512], semkey=f"stg{k}")
                dma("pool", vd_, STG[:, k, 512:1024], semkey=f"stg{k}")
        if stop < 3.03:
            return
        blk = wload("w_in0", 3)
        for m4 in range(4):
            bk = nb()
            proj_fm(blk, m4, HT, N, bk)
            cp("act" if m4 % 2 == 0 else "dve", UT[:, m4, 0:N], bk[:, 0:N])
        bound_finalize()
        if stop < 3.04:
            return
        blk = wload("w_in0", 4)
        for t_ in range(ntile):
            bk = nb()
            proj_tm(blk, HT, t_, bk)
            if stop >= 3.05:
                sgu_norm_tile(bk, t_)
        if stop < 3.2:
            return
        bound_broadcast()
        for (q0, Nq, kts) in ktiles_fn(N):
            attention(QT, q0, Nq, kts, HT)
        run_deferred()
        if stop < 3.3:
            return
        sgu_mix(N, HT)
        if stop < 3.4:
            return
        st = out_proj("w_out0", HT, XS, N, 0, cd)
        layernorm(XS, N, 0, cd, "mix", True, st)
        st = ffn(XS, N, 0, cd)
        layernorm(XS, N, 0, cd, "ff", False, st)

    def layer1_group(ntile, cd, slot, segs, xnext, use_prev, dst_rows_fn):
        N = ntile * 128
        XS = XR[:, slot]
        for c in range(8):
            act(HT[:, c, 0:N], XS[:, c, 0:N], AF.Identity, bias=dc(1, cd, 1, c), scale=dc(1, cd, 0, c))
        if xnext is not None:
            for c in range(8):
                act(HX[:, c, 0:1], xnext[:, c, 0:1], AF.Identity, bias=dc(1, cd, 1, c), scale=dc(1, cd, 0, c))
        for half in range(2):
            bgb = wload("w_in1", 0 + half)
            cgb = wload("w_in1", 2 + half)
            xtb = wload("w_in1", 4 + half)
            for m4 in range(4):
                m = half * 4 + m4
                k = m % 2
                bC, bX, bB = nb(), nb(), nb()
                proj_fm(cgb, m4, HT, N, bC)
                proj_fm(xtb, m4, HT, N, bX)
                proj_fm(bgb, m4, HT, N, bB)
                cp("act", T1[:, k, 0:N], bC[:, 0:N])
                Z = T3[:, k, 0:N]
                C = T2[:, k, 0:N]
                tt("dve", Z, bX[:, 0:N], T1[:, k, 0:N], ALU.mult)
                ts("dve", C, Z, col2("conv", 8 + m), None, ALU.mult)
                for (a, b) in segs:
                    stt(C[:, a + 1:b], Z[:, a:b - 1], col2("conv", 0 + m), C[:, a + 1:b], ALU.mult, ALU.add)
                    stt(C[:, a:b - 1], Z[:, a + 1:b], col2("conv", 16 + m), C[:, a:b - 1], ALU.mult, ALU.add)
                if use_prev:
                    stt(C[:, 0:1], ZSAVE[:, m:m + 1], col2("conv", 0 + m), C[:, 0:1], ALU.mult, ALU.add)
                    cp("dve", ZSAVE[:, m:m + 1], Z[:, N - 1:N])
                if xnext is not None:
                    bH = nb()
                    for kc in range(8):
                        mm(bH[:, 0:1], cgb[:, kc, m4 * 128:(m4 + 1) * 128], HX[:, kc, 0:1], start=(kc == 0),
                           stop=(kc == 7))
                    for kc in range(8):
                        mm(bH[:, 2:3], xtb[:, kc, m4 * 128:(m4 + 1) * 128], HX[:, kc, 0:1], start=(kc == 0),
                           stop=(kc == 7), skip_group_check=True)
                    cp("act", SMZ[:, m, 0:1], bH[:, 0:1])
                    tt("dve", SMZ[:, m, 1:2], bH[:, 2:3], SMZ[:, m, 0:1], ALU.mult)
                    stt(C[:, N - 1:N], SMZ[:, m, 1:2], col2("conv", 16 + m), C[:, N - 1:N], ALU.mult, ALU.add)
                tt("dve", CAT1[:, m, 0:N], bB[:, 0:N], C, ALU.mult)
        st = out_proj("w_out1", CAT1, XS, N, 1, cd)
        layernorm(XS, N, 1, cd, "mix", True, st)
        st = ffn(XS, N, 1, cd)
        layernorm(XS, N, 1, cd, "ff", False, st)
        store_tiles(XS, ntile, dst_rows_fn)

    HTB = [HT, CAT1]

    def kv_load(g, t_):
        tile_i = g * 4 + t_
        load_tile(xs[tile_i * 128:(tile_i + 1) * 128, :], 0, 0, None, HTB[g % 2], t_)

    def kv_tab(g):
        dma("sp", TAB[:, 0, :], cosT[:, g * 512:(g + 1) * 512], semkey="tabc")
        dma("sp", TAB[:, 1, :], sinT[:, g * 512:(g + 1) * 512], semkey="tabs")

    NKV = 8 if stop >= 2 else 0
    if NKV:
        for t_ in range(4):
            kv_load(0, t_)
        kv_tab(0)
    for kvg in range(NKV):
        H = HTB[kvg % 2]
        blk = wload("w_in0", 1)
        for m4 in range(4):
            bk = nb()
            proj_fm(blk, m4, H, 512, bk)
            rope(bk, 512, KT[:, m4, kvg * 512:(kvg + 1) * 512], KCOLS[:, kvg * 4 + m4:kvg * 4 + m4 + 1])
            if kvg + 1 < NKV:
                kv_load(kvg + 1, m4)
        if kvg + 1 < NKV:
            kv_tab(kvg + 1)
        blk = wload("w_in0", 2)
        for t_ in range(4):
            bk = nb()
            proj_tm(blk, H, t_, bk)
            cp("act" if t_ % 2 == 0 else "dve",
               VV[:, kvg * 4 + t_, :].rearrange("p (h c) -> p h c", h=4)[:, :, 0:128],
               bk[:, 0:512].rearrange("p (h c) -> p h c", h=4))
    for j in range(2 if stop >= 2 else 0):
        k = xin_ctr[0] % 2
        xin_ctr[0] += 1
        dma("sp", XIN[:, k, 0:512], ck[j * 128:(j + 1) * 128, :], semkey=f"xin{k}")
        bk = nb()
        for m4 in range(4):
            tr(bk[:, m4 * 128:(m4 + 1) * 128], XIN[:, k, m4 * 128:(m4 + 1) * 128], IDF[:])
        cp("dve", KT[:, :, (32 + j) * 128:(33 + j) * 128], bk[:, 0:512].rearrange("p (c n) -> p c n", c=4))
        for m4 in range(4):
            norm_update(KT[:, m4, (32 + j) * 128:(33 + j) * 128], 128, KCOLS[:, 32 + j * 4 + m4:33 + j * 4 + m4])
        dma("pool", VV[:, 32 + j, :].rearrange("p (h c) -> p h c", h=4)[:, :, 0:128],
            cv[j * 128:(j + 1) * 128, :].rearrange("p (h c) -> p h c", h=4), semkey=f"cvld{j}")

    if stop >= 2:
        cols_max(KCOLS[:, 0:40], MAXC[:, 1:2])
    modulation(0, [4, 5, 6, 7, 8, 9, 10, 11], "b")
    cast_all(["w_out0", "w_ff1_0", "w_ff2_0"])
    cast_all(["w_in1", "w_out1", "w_ff1_1", "w_ff2_1"])

    def sample_ktiles(N):
        kts = []
        for kt in range(34):
            kts.append(((lambda h, kt=kt: KT[:, h, kt * 128:(kt + 1) * 128]),
                        (lambda h, kt=kt: VV[:, kt, h * 129:(h + 1) * 129])))
        return [(0, N, kts)]

    groups = [(0, 4), (4, 4), (8, 4), (12, 3), (15, 2)]

    def s_l0(g):
        t0, nt = groups[g]
        layer0_group(lambda t_: xs[(t0 + t_) * 128:(t0 + t_ + 1) * 128, :], nt, 0, g % 2, True, t0 * 128,
                     sample_ktiles, None)

    def s_l1(g):
        t0, nt = groups[g]
        xnext = XR[:, (g + 1) % 2] if g + 1 < len(groups) else None
        layer1_group(nt, 0, g % 2, [(0, nt * 128)], xnext, True,
                     lambda t_: ys[(t0 + t_) * 128:(t0 + t_ + 1) * 128, :])

    if stop >= 3:
        s_l0(0)
    if stop >= 4:
        for g in range(1, len(groups)):
            s_l0(g)
            if g == 1:
                modulation(1, list(range(12)), "ab")
            s_l1(g - 1)
        s_l1(len(groups) - 1)

    def prompt_ktiles(N):
        res = []
        for bi in range(2):
            kts = []
            for j in range(2):
                kt = bi * 2 + j
                kts.append(((lambda h, kt=kt: KTP[:, h, kt * 128:(kt + 1) * 128]),
                            (lambda h, kt=kt: VV[:, kt, h * 129:(h + 1) * 129])))
            res.append((bi * 256, 256, kts))
        return res

    for pg in range(2 if stop >= 5 else 0):
        r0 = pg * 512
        layer0_group(lambda t_: xp[r0 + t_ * 128: r0 + (t_ + 1) * 128, :], 4, 1, 0, False, 0, prompt_ktiles,
                     lambda t_: (nk[r0 + t_ * 128: r0 + (t_ + 1) * 128, :], nv[r0 + t_ * 128: r0 + (t_ + 1) * 128, :]))
        layer1_group(4, 1, 0, [(0, 256), (256, 512)], None, False,
                     lambda t_: yp[r0 + t_ * 128: r0 + (t_ + 1) * 128, :])

    ops = P.ops
    for o in ops:
        for d in o.deps:
            ops[d].has_dep = True
    ENG = ["pe", "act", "dve", "pool", "sp"]
    semkeys = sorted({o.semkey for o in ops if o.dma})
    sems = {}
    for e_ in ENG:
        sems[e_] = es.enter_context(nc.semaphore("s_" + e_))
    for k in semkeys:
        sems["d_" + k] = es.enter_context(nc.semaphore("d_" + k))
    ecount = {e_: 0 for e_ in ENG}
    dcount = {k: 0 for k in semkeys}
    for o in ops:
        if o.dma:
            dcount[o.semkey] += 16
            o.cnt = dcount[o.semkey]
        elif o.has_dep:
            ecount[o.eng] += 1
            o.cnt = ecount[o.eng]
    dtotal = dict(dcount)
    out_keys = [k for k in semkeys if k.startswith("stg")]

    block = es.enter_context(nc.Block())

    def emit_engine(ename, e):
        waited = {}
        for o in ops:
            if o.eng != ename:
                continue
            need = {}
            for d in o.deps:
                p = ops[d]
                if p.dma:
                    sk = "d_" + p.semkey
                    val = dtotal[p.semkey] if p.whole else p.cnt
                else:
                    if p.eng == "pe" and ename == "pe":
                        continue
                    sk = p.eng
                    val = p.cnt
                if val > need.get(sk, 0):
                    need[sk] = val
            for sk, val in need.items():
                if waited.get(sk, 0) >= val:
                    continue
                e.wait_ge(sems[sk], val)
                waited[sk] = val
            ins = o.fn(e)
            if o.dma:
                ins.then_inc(sems["d_" + o.semkey], 16)
            elif o.has_dep:
                ins.then_inc(sems[ename], 1)
        if ename == "pool":
            for k in out_keys:
                e.wait_ge(sems["d_" + k], dtotal[k])

    @block.tensor
    def _(e):
        emit_engine("pe", e)

    @block.scalar
    def _(e):
        emit_engine("act", e)

    @block.vector
    def _(e):
        emit_engine("dve", e)

    @block.gpsimd
    def _(e):
        emit_engine("pool", e)

    @block.sync
    def _(e):
        emit_engine("sp", e)

    es.close()
    return nc


_NC_CACHE = {}


def _rope_tables(order):
    pos = (np.asarray(order)[:, None] * 128 + np.arange(128)[None, :]).reshape(-1)
    row = (pos // 64).astype(np.float32)
    col = (pos % 64).astype(np.float32)
    inv = (1.0 / (np.float32(10000.0) ** (np.arange(16, dtype=np.float32) / np.float32(16)))).astype(np.float32)
    ang = [row[:, None] * inv[None, :], col[:, None] * inv[None, :]]
    cosT = np.zeros((128, 4096), np.float32)
    sinT = np.zeros((128, 4096), np.float32)
    for p in range(128):
        pm = p % 64
        s = pm // 32
        j = (pm % 32) // 16
        f = pm % 16
        cosT[p] = np.cos(ang[s][:, f])
        sinT[p] = np.sin(ang[s][:, f]) * (-1.0 if j == 0 else 1.0)
    return cosT, sinT


def kernel(**inp):
    f = lambda a: np.ascontiguousarray(np.asarray(a, dtype=np.float32))
    x_prompt, x_sample = f(inp["x_prompt"]), f(inp["x_sample"])
    cache_k0, cache_v0 = f(inp["cache_k0"]), f(inp["cache_v0"])
    c, c_ctx = f(inp["c"]), f(inp["c_ctx"])
    if "nc" not in _NC_CACHE:
        _NC_CACHE["nc"] = build_nc()
    nc = _NC_CACHE["nc"]
    ident = np.eye(128, dtype=np.float32)
    perm = np.zeros((128, 128), np.float32)
    for m in range(128):
        perm[m ^ 16, m] = 1.0
    shared = {"c_ident": ident, "c_perm": perm}
    for name, _, _ in W_SPECS:
        shared[name] = f(inp[name])
    for name in ("w_mod0", "w_mod1", "conv_w1", "subln_g0", "sgu_w0", "sgu_b0", "lambda_q1_0", "lambda_k1_0",
                 "lambda_q2_0", "lambda_k2_0"):
        shared[name] = f(inp[name])
    for name, _ in VEC_IN:
        shared[name] = f(inp[name])
    in_maps = []
    orders = []
    for core in range(8):
        b, half = core // 2, core % 2
        win = list(range(0, 17)) if half == 0 else list(range(15, 32))
        others = [t for t in range(32) if t not in win]
        order = win + others
        orders.append(order)
        xt = x_sample[b].reshape(32, 128, 1024)[order].reshape(4096, 1024)
        cosT, sinT = _rope_tables(order)
        m = dict(shared)
        m["xs"] = np.ascontiguousarray(xt)
        m["xp"] = np.ascontiguousarray(x_prompt[4 * core:4 * core + 4].reshape(1024, 1024))
        m["ck"] = np.ascontiguousarray(cache_k0[b].reshape(256, 512))
        m["cv"] = np.ascontiguousarray(cache_v0[b].reshape(256, 512))
        m["cvec"] = np.ascontiguousarray(np.stack([c[b], c_ctx], 0))
        m["cosT"] = cosT
        m["sinT"] = sinT
        in_maps.append(m)
    res = run_bass_kernel_spmd(nc, in_maps, core_ids=list(range(8)))
    y_prompt = np.zeros((32, 256, 1024), np.float32)
    y_sample = np.zeros((4, 4096, 1024), np.float32)
    new_k = np.zeros((32, 256, 4, 2, 64), np.float32)
    new_v = np.zeros((32, 256, 4, 128), np.float32)
    for core in range(8):
        r = res.results[core]
        b, half = core // 2, core % 2
        ysc = np.asarray(r["ys"])
        if half == 0:
            y_sample[b, 0:2048] = ysc[0:2048]
        else:
            y_sample[b, 2048:4096] = ysc[128:2176]
        y_prompt[4 * core:4 * core + 4] = np.asarray(r["yp"]).reshape(4, 256, 1024)
        new_k[4 * core:4 * core + 4] = np.asarray(r["nk"]).reshape(4, 256, 4, 2, 64)
        new_v[4 * core:4 * core + 4] = np.asarray(r["nv"]).reshape(4, 256, 4, 128)
    return (y_prompt, y_sample, new_k, new_v)
```

```python
import math
import os
from contextlib import ExitStack

import numpy as np
import concourse.bass as bass
import concourse.mybir as mybir
from concourse.bass_utils import run_bass_kernel_spmd

F32 = mybir.dt.float32
BF16 = mybir.dt.bfloat16
AF = mybir.ActivationFunctionType
ALU = mybir.AluOpType
AX = mybir.AxisListType

D = 1024
ALPHA = 4.0 ** 0.25
LAMBDA_INIT = 0.8 - 0.6 * math.exp(-0.3 * 0)
LN_EPS = 1e-5
EPS_R = LN_EPS / (ALPHA * ALPHA)
NWIN = 17
DSZ = {F32: 4, BF16: 2}


class Op:
    __slots__ = ("eng", "fn", "deps", "dma", "semkey", "whole", "has_dep", "cnt")

    def __init__(self, eng, fn, dma, semkey, whole):
        self.eng, self.fn, self.dma, self.semkey, self.whole = eng, fn, dma, semkey, whole
        self.deps = set()
        self.has_dep = False
        self.cnt = 0


def _region(ap):
    name = ap.tensor.name
    es = DSZ.get(ap.dtype, 4)
    dims = list(ap.ap)
    off = ap.offset
    if str(ap.space) == "DRAM":
        span = sum((c - 1) * abs(s) for s, c in dims)
        return (name, 0, 1, off * es, (off + span + 1) * es)
    ps, pc = dims[0]
    ps = max(ps, 1)
    p0 = off // ps
    f0 = off % ps
    span = sum((c - 1) * abs(s) for s, c in dims[1:])
    if str(ap.space) == "PSUM":
        return (name, 0, 128, (f0 * es) // 2048 * 2048, ((f0 + span + 1) * es + 2047) // 2048 * 2048)
    return (name, p0, p0 + pc, f0 * es, (f0 + span + 1) * es)


class Prog:
    def __init__(self):
        self.ops = []
        self.recs = {}

    def _rkey(self, idx):
        o = self.ops[idx]
        return (o.eng, o.semkey)

    def _touch(self, idx, ap, write):
        name, p0, p1, lo, hi = _region(ap)
        lst = self.recs.setdefault(name, [])
        deps = self.ops[idx].deps
        keep = []
        for r in lst:
            rp0, rp1, rlo, rhi, w, rd = r
            if rp1 <= p0 or p1 <= rp0 or rhi <= lo or hi <= rlo:
                keep.append(r)
                continue
            if w is not None:
                deps.add(w)
            if write:
                deps.update(rd.values())
                if p0 <= rp0 and rp1 <= p1 and lo <= rlo and rhi <= hi:
                    continue
            keep.append(r)
        if write:
            keep.append([p0, p1, lo, hi, idx, {}])
        else:
            done = False
            for r in keep:
                if r[0] <= p0 and p1 <= r[1] and r[2] <= lo and hi <= r[3]:
                    r[5][self._rkey(idx)] = idx
                    done = True
                    break
            if not done:
                keep.append([p0, p1, lo, hi, None, {self._rkey(idx): idx}])
        self.recs[name] = keep

    def op(self, eng, fn, reads=(), writes=(), dma=False, semkey=None, whole=False):
        idx = len(self.ops)
        self.ops.append(Op(eng, fn, dma, semkey, whole))
        for a in reads:
            if a is not None and not isinstance(a, (int, float)):
                self._touch(idx, a, str(a.space) == "PSUM")
        for a in writes:
            self._touch(idx, a, True)
        o = self.ops[idx]
        o.deps.discard(idx)
        return idx


W_SPECS = [
    ("w_in0", 1024, 2560), ("w_out0", 1024, 1024), ("w_ff1_0", 1024, 4096), ("w_ff2_0", 4096, 1024),
    ("w_in1", 1024, 3072), ("w_out1", 1024, 1024), ("w_ff1_1", 1024, 4096), ("w_ff2_1", 4096, 1024),
]
VEC_IN = [("b_mod0", 6144), ("b_mod1", 6144), ("ln_mix_g0", 1024), ("ln_mix_b0", 1024), ("ln_ff_g0", 1024),
          ("ln_ff_b0", 1024), ("ln_mix_g1", 1024), ("ln_mix_b1", 1024), ("ln_ff_g1", 1024), ("ln_ff_b1", 1024)]


def build_nc(stop=99):
    nc = bass.Bass("TRN2", target_bir_lowering=False)
    P = Prog()
    es = ExitStack()

    def din(name, shape, dt=F32):
        return nc.dram_tensor(name, list(shape), dt, kind="ExternalInput").ap()

    def dout(name, shape, dt=F32):
        return nc.dram_tensor(name, list(shape), dt, kind="ExternalOutput").ap()

    xs = din("xs", [4096, 1024])
    xp = din("xp", [1024, 1024])
    ck = din("ck", [256, 512])
    cv = din("cv", [256, 512])
    cvec = din("cvec", [2, 1024])
    cosT = din("cosT", [128, 4096])
    sinT = din("sinT", [128, 4096])
    c_ident = din("c_ident", [128, 128])
    c_perm = din("c_perm", [128, 128])
    wd = {}
    for name, K, ncol in W_SPECS:
        wd[name] = din(name, [K, ncol])
    wd["w_mod0"] = din("w_mod0", [1024, 6144])
    wd["w_mod1"] = din("w_mod1", [1024, 6144])
    vd = {n: din(n, [ln]) for n, ln in VEC_IN}
    conv_w1 = din("conv_w1", [3, 1024])
    lam_in = {n: din(n, [64]) for n in ("lambda_q1_0", "lambda_k1_0", "lambda_q2_0", "lambda_k2_0")}
    subln_g0 = din("subln_g0", [128])
    sgu_w0 = din("sgu_w0", [4, 128, 128])
    sgu_b0 = din("sgu_b0", [4, 128])

    ys = dout("ys", [NWIN * 128, 1024])
    yp = dout("yp", [1024, 1024])
    nk = dout("nk", [1024, 512])
    nv = dout("nv", [1024, 512])

    blk_of = {}
    nblk = 0
    for name, K, ncol in W_SPECS:
        if K == 1024:
            n = ncol // 512
        else:
            n = 8
        blk_of[name] = (nblk, n)
        nblk += n
    wscr = nc.dram_tensor("wscr", [nblk, 128, 4096], BF16, kind="Internal").ap()

    def sb(name, shape, dt):
        return es.enter_context(nc.sbuf_tensor(name, list(shape), dt))

    KT = sb("KT", [128, 4, 4352], BF16)
    VV = sb("VV", [128, 34, 516], BF16)
    XR = sb("XR", [128, 2, 8, 512], F32)
    XIN = sb("XIN", [128, 2, 1024], F32)
    STG = sb("STG", [128, 2, 1024], F32)
    HT = sb("HT", [128, 8, 512], BF16)
    HX = sb("HX", [128, 8, 2], BF16)
    TAB = sb("TAB", [128, 2, 512], F32)
    T1 = sb("T1", [128, 2, 512], F32)
    T2 = sb("T2", [128, 2, 512], F32)
    T3 = sb("T3", [128, 2, 512], F32)
    QRAW = T2
    RB = T1[:, 0, :].bitcast(BF16).rearrange("p (a n) -> p a n", a=2)
    ARENA = sb("ARENA", [128, 10240], BF16)
    WS = sb("WS", [128, 4, 4096], BF16)
    IDF = sb("IDF", [128, 128], F32)
    IDB = sb("IDB", [128, 128], BF16)
    PERM = sb("PERM", [128, 128], F32)
    WST = sb("WST", [128, 4, 128], BF16)
    GSUB = sb("GSUB", [128, 128], F32)
    VB1 = T2[:, 0, 0:128]
    VB2 = T2[:, 1, 0:128]
    COLS1 = sb("COLS1", [128, 96], F32)
    COLS2 = sb("COLS2", [128, 104], F32)
    SIL = sb("SIL", [128, 8, 2], BF16)
    MODT = sb("MODT", [128, 2, 48, 2], F32)
    NVD = 8
    DCOL = sb("DCOL", [128, 2, 2, NVD, 8], F32)
    BSGU = sb("BSGU", [1, 512], F32)
    ONER = sb("ONER", [1, 128], F32)
    LAMT = T3[:, 0, 0:256].rearrange("p (a b) -> p a b", a=4)
    LAMS = sb("LAMS", [128, 8], F32)
    NEGC = sb("NEGC", [128, 1], F32)
    JUNK = sb("JUNK", [128, 128], F32)
    AN = sb("AN", [128, 4, 128], BF16)
    SM = sb("SM", [128, 8, 8], F32)
    ST = sb("ST", [128, 2, 4, 6], F32)
    MV = sb("MV", [128, 2, 4, 2], F32)
    RS = sb("RS", [128, 2, 8], F32)
    ZSAVE = sb("ZSAVE", [128, 8], F32)
    SMZ = sb("SMZ", [128, 8, 2], F32)
    SGW = T1[:, 0, :].rearrange("p (a b) -> p a b", a=4)

    BLK1 = sb("BLK1", [128, 128], BF16)
    ONESB = sb("ONESB", [128, 128], BF16)
    SQB = sb("SQB", [128, 2, 512], BF16)
    KCOLS = sb("KCOLS", [128, 48], F32)
    QCOLS = sb("QCOLS", [128, 8], F32)
    MAXC = sb("MAXC", [128, 2], F32)
    S1 = sb("S1", [1, 16], F32)

    PS01 = es.enter_context(nc.psum_tensor("PS01", [128, 1024], F32))
    PS23 = es.enter_context(nc.psum_tensor("PS23", [128, 1024], F32))
    BANK = [PS01[:, 0:512], PS01[:, 512:1024], PS23[:, 0:512], PS23[:, 512:1024]]
    BANK += [es.enter_context(nc.psum_tensor(f"B{i}", [128, 512], F32))[:] for i in range(4, 8)]

    QT = ARENA[:, 0:2048].rearrange("p (c n) -> p c n", c=4)
    VC = ARENA[:, 2048:4096].rearrange("p (c n) -> p c n", c=4)
    CAT1 = ARENA[:, 0:4096].rearrange("p (c n) -> p c n", c=8)
    ET = ARENA[:, 4096:6144].rearrange("p (c n) -> p c n", c=4)
    UT = ARENA[:, 6144:10240].bitcast(F32).rearrange("p (c n) -> p c n", c=4)
    HID = ARENA[:, 0:8192].rearrange("p (c n) -> p c n", c=16)
    KTP = KT[:, :, 0:512]

    bank_ctr = [0]

    reserved = set()

    def nb():
        while (bank_ctr[0] % 8) in reserved:
            bank_ctr[0] += 1
        b = BANK[bank_ctr[0] % 8]
        bank_ctr[0] += 1
        return b

    def nb_reserve():
        while (bank_ctr[0] % 8) in reserved:
            bank_ctr[0] += 1
        i = bank_ctr[0] % 8
        bank_ctr[0] += 1
        reserved.add(i)
        return i

    def mm(out, lhsT, rhs, start=True, stop=True, **kw):
        P.op("pe", lambda e: e.matmul(out, lhsT, rhs, start=start, stop=stop, **kw), [lhsT, rhs], [out])

    def tr(out, in_, ident):
        P.op("pe", lambda e: e.transpose(out, in_, ident), [in_, ident], [out])

    def act(out, in_, func, bias=None, scale=None, accum=None):
        kw = {}
        if bias is not None:
            kw["bias"] = bias
        if scale is not None:
            kw["scale"] = scale
        if accum is not None:
            kw["accum_out"] = accum
        wr = [out] + ([accum] if accum is not None else [])
        P.op("act", lambda e: e.activation(out, in_, func, **kw), [in_, bias, scale], wr)

    def ts(eng, out, in0, s1, s2, op0, op1=None):
        if op1 is None:
            P.op(eng, lambda e: e.tensor_scalar(out, in0, s1, None, op0), [in0, s1], [out])
        else:
            P.op(eng, lambda e: e.tensor_scalar(out, in0, s1, s2, op0, op1), [in0, s1, s2], [out])

    def tt(eng, out, in0, in1, op):
        P.op(eng, lambda e: e.tensor_tensor(out, in0, in1, op), [in0, in1], [out])

    def stt(out, in0, scalar, in1, op0, op1):
        P.op("dve", lambda e: e.scalar_tensor_tensor(out, in0, scalar, in1, op0, op1), [in0, scalar, in1], [out])

    def cp(eng, out, in_):
        if eng == "act":
            P.op("act", lambda e: e.activation(out, in_, AF.Copy), [in_], [out])
        else:
            P.op(eng, lambda e: e.tensor_copy(out, in_), [in_], [out])

    def dma(q, out, in_, semkey, whole=False, **kw):
        P.op(q, lambda e: e.dma_start(out=out, in_=in_, **kw), [in_], [out], dma=True, semkey=semkey, whole=whole)

    def memset(eng, ap, val):
        P.op(eng, lambda e: e.memset(ap, val), [], [ap])

    def w_src(name, b):
        K = dict((n, k) for n, k, _ in W_SPECS)[name]
        w = wd[name]
        if K == 1024:
            return w[:, b * 512:(b + 1) * 512].rearrange("(kc p) n -> p kc n", p=128), 8, 512
        hf, mb = b // 4, b % 4
        return (w[hf * 2048:(hf + 1) * 2048, mb * 256:(mb + 1) * 256].rearrange("(kc p) n -> p kc n", p=128),
                16, 256)

    def scr_view(name, b):
        base, n = blk_of[name]
        _, kc, ncol = w_src(name, b)
        return wscr[base + b].rearrange("p (kc n) -> p kc n", kc=kc)

    def cast_all(names):
        for name in names:
            base, n = blk_of[name]
            for b in range(n):
                src, kc, ncol = w_src(name, b)
                dma("pool", scr_view(name, b), src, semkey="cast_" + name, whole=True)

    ws_ctr = [0]

    def wload(name, b):
        s = ws_ctr[0] % 4
        ws_ctr[0] += 1
        _, kc, ncol = w_src(name, b)
        dst = WS[:, s, :].rearrange("p (kc n) -> p kc n", kc=kc)
        dma("sp", dst, scr_view(name, b), semkey=f"ws{s}")
        return dst

    def wload_mod(layer, b):
        s = ws_ctr[0] % 4
        ws_ctr[0] += 1
        dst = WS[:, s, :].rearrange("p (kc n) -> p kc n", kc=8)
        src = wd[f"w_mod{layer}"][:, b * 512:(b + 1) * 512].rearrange("(kc p) n -> p kc n", p=128)
        dma("pool", dst, src, semkey=f"wm{s}")
        return dst

    def vrows(ap1d, n):
        return ap1d.rearrange("(c p) -> c p", p=128)

    dma("sp", IDF[:], c_ident, "setup", whole=True)
    dma("sp", PERM[:], c_perm, "setup", whole=True)
    dma("sp", VB1[0:48, :], vrows(vd["b_mod0"], 48), "setup", whole=True)
    dma("sp", VB1[48:96, :], vrows(vd["b_mod1"], 48), "setup", whole=True)
    r = 0
    vb2_off = {}
    for l in range(2):
        for nm in ("ln_mix_g", "ln_mix_b", "ln_ff_g", "ln_ff_b"):
            dma("sp", VB2[r:r + 8, :], vrows(vd[f"{nm}{l}"], 8), "setup", whole=True)
            vb2_off[f"{nm}{l}"] = r
            r += 8
    dma("sp", VB2[r:r + 24, :], conv_w1.rearrange("t (c p) -> (t c) p", p=128), "setup", whole=True)
    vb2_off["conv"] = r
    r += 24
    dma("sp", VB2[r:r + 16, :], cvec.rearrange("b (c p) -> (b c) p", p=128), "setup", whole=True)
    vb2_off["cvec"] = r
    r += 16
    assert r == 104
    for i, n in enumerate(("lambda_q1_0", "lambda_k1_0", "lambda_q2_0", "lambda_k2_0")):
        dma("sp", LAMT[:, i, :], lam_in[n].partition_broadcast(128), "setup", whole=True)
    dma("sp", GSUB[:], subln_g0.partition_broadcast(128), "setup", whole=True)
    dma("sp", SGW, sgu_w0.rearrange("g p q -> p g q"), "setup", whole=True)
    dma("sp", BSGU[:], sgu_b0.rearrange("g p -> (g p)").partition_broadcast(1), "setup", whole=True)


    memset("dve", ONER[:], 1.0)
    memset("dve", NEGC[:], 0.0)
    memset("dve", ZSAVE[:], 0.0)
    memset("dve", VV[:].rearrange("p k (h c) -> p k h c", h=4)[:, :, :, 128:129], 1.0)
    cp("dve", IDB[:], IDF[:])
    memset("dve", BLK1[:], 0.0)
    memset("dve", ONESB[:], 1.0 / 1024.0)
    memset("dve", BLK1[0:64, 0:64], 1.0)
    memset("dve", BLK1[64:128, 64:128], 1.0)

    b0 = nb()
    tr(b0[:, 0:96], VB1[0:96, :], IDF[0:96, 0:96])
    cp("dve", COLS1[:], b0[:, 0:96])
    b1 = nb()
    tr(b1[:, 0:104], VB2[0:104, :], IDF[0:104, 0:104])
    cp("dve", COLS2[:], b1[:, 0:104])

    def col2(name, c):
        o = vb2_off[name] + c
        return COLS2[:, o:o + 1]

    co = vb2_off["cvec"]
    for cd in range(2):
        act(SIL[:, :, cd], COLS2[:, co + cd * 8: co + cd * 8 + 8], AF.Silu)

    bw = nb()
    for g in range(4):
        tr(bw[:, g * 128:(g + 1) * 128], SGW[:, g, :], IDF[:])
    cp("dve", WST[:].rearrange("p g q -> p (g q)"), bw[:, 0:512])

    tt("dve", LAMT[:, 0, :], LAMT[:, 0, :], LAMT[:, 1, :], ALU.mult)
    tt("dve", LAMT[:, 2, :], LAMT[:, 2, :], LAMT[:, 3, :], ALU.mult)
    P.op("dve", lambda e: e.reduce_sum(LAMS[:, 0:1], LAMT[:, 0, :], AX.X), [LAMT[:, 0, :]], [LAMS[:, 0:1]])
    P.op("dve", lambda e: e.reduce_sum(LAMS[:, 1:2], LAMT[:, 2, :], AX.X), [LAMT[:, 2, :]], [LAMS[:, 1:2]])
    act(LAMS[:, 2:4], LAMS[:, 0:2], AF.Exp)
    tt("dve", LAMS[:, 4:5], LAMS[:, 3:4], LAMS[:, 2:3], ALU.subtract)
    ts("dve", LAMS[:, 5:6], LAMS[:, 4:5], -LAMBDA_INIT, None, ALU.add)
    NEGLAM = LAMS[:, 5:6]
    ts("dve", GSUB[:], GSUB[:], 1.0 - LAMBDA_INIT, None, ALU.mult)

    def modulation(layer, blocks, derive):
        bm = nb()
        for b in blocks:
            wb = wload_mod(layer, b)
            for j4 in range(4):
                j = b * 4 + j4
                for kc in range(8):
                    mm(bm[:, 2 * j:2 * j + 2], wb[:, kc, j4 * 128:(j4 + 1) * 128], SIL[:, kc, :],
                       start=(kc == 0), stop=(kc == 7))
        j0, j1 = blocks[0] * 4, blocks[-1] * 4 + 4
        bmv = bm[:, 0:96].rearrange("p (j c) -> p j c", c=2)
        for cd in range(2):
            tt("dve", MODT[:, layer, j0:j1, cd], bmv[:, j0:j1, cd], COLS1[:, layer * 48 + j0:layer * 48 + j1],
               ALU.add)
        for cd in range(2):
            M = lambda w: MODT[:, layer, w * 8:(w + 1) * 8, cd]
            Dv = lambda k: DCOL[:, layer, cd, k, :]
            gm = COLS2[:, vb2_off[f"ln_mix_g{layer}"]: vb2_off[f"ln_mix_g{layer}"] + 8]
            bmx = COLS2[:, vb2_off[f"ln_mix_b{layer}"]: vb2_off[f"ln_mix_b{layer}"] + 8]
            if "a" in derive:
                ts("dve", Dv(0), M(1), 1.0, None, ALU.add)
                cp("dve", Dv(1), M(0))
            if "b" in derive:
                ts("dve", Dv(2), M(2), 1.0 / ALPHA, None, ALU.mult)
                ts("dve", Dv(6), M(4), 1.0, None, ALU.add)
                tt("dve", Dv(3), gm, Dv(6), ALU.mult)
                tt("dve", Dv(4), bmx, Dv(6), ALU.mult)
                tt("dve", Dv(4), Dv(4), M(3), ALU.add)
                ts("dve", Dv(5), M(5), 1.0 / ALPHA, None, ALU.mult)

    def dc(layer, cd, k, c):
        return DCOL[:, layer, cd, k, c:c + 1]

    modulation(0, [0, 1, 2, 3], "a")
    cast_all(["w_in0"])

    xin_ctr = [0]

    def load_tile(src_rows, cd, layer, dstx, dsth, tt_):
        k = xin_ctr[0] % 2
        xin_ctr[0] += 1
        dma("sp", XIN[:, k, :], src_rows, semkey=f"xin{k}")
        for half in range(2):
            bk = nb()
            for cc in range(4):
                c = half * 4 + cc
                tr(bk[:, cc * 128:(cc + 1) * 128], XIN[:, k, c * 128:(c + 1) * 128], IDF[:])
            if dstx is not None and os.environ.get("DBG_NOX") != "1":
                if os.environ.get("DBG_NOX") == "2":
                    for cc in range(4):
                        cp("dve", dstx[:, half * 4 + cc, tt_ * 128:(tt_ + 1) * 128], bk[:, cc * 128:(cc + 1) * 128])
                else:
                    cp("dve", dstx[:, half * 4:half * 4 + 4, tt_ * 128:(tt_ + 1) * 128],
                       bk[:, 0:512].rearrange("p (c n) -> p c n", c=4))
            for cc in range(4):
                c = half * 4 + cc
                act(dsth[:, c, tt_ * 128:(tt_ + 1) * 128], bk[:, cc * 128:(cc + 1) * 128], AF.Identity,
                    bias=dc(layer, cd, 1, c), scale=dc(layer, cd, 0, c))

    def proj_fm(blk, m4, inT, N, bank):
        for kc in range(8):
            mm(bank[:, 0:N], blk[:, kc, m4 * 128:(m4 + 1) * 128], inT[:, kc, 0:N], start=(kc == 0), stop=(kc == 7))

    def proj_tm(blk, inT, tt_, bank):
        for kc in range(8):
            mm(bank[:, 0:512], inT[:, kc, tt_ * 128:(tt_ + 1) * 128], blk[:, kc, 0:512], start=(kc == 0),
               stop=(kc == 7))

    sq_ctr = [0]

    def norm_update(src, N, dstcol):
        k = sq_ctr[0] % 2
        sq_ctr[0] += 1
        act(SQB[:, k, 0:N], src, AF.Square)
        bkn = nb()
        mm(bkn[:, 0:N], BLK1[:], SQB[:, k, 0:N])
        P.op("dve", lambda e: e.reduce_max(dstcol, bkn[:, 0:N], AX.X), [bkn[:, 0:N]], [dstcol])

    def cols_max(cols, dst):
        P.op("dve", lambda e: e.reduce_max(dst, cols, AX.X), [cols], [dst])

    def bound_finalize():
        for j in range(2):
            bkt = nb()
            tr(bkt[0:1, 0:128], MAXC[:, j:j + 1], IDF[:])
            P.op("dve", (lambda e, bkt=bkt, j=j: e.reduce_max(S1[0:1, j:j + 1], bkt[0:1, 0:128], AX.X)),
                 [bkt[0:1, 0:128]], [S1[0:1, j:j + 1]])
        tt("dve", S1[0:1, 2:3], S1[0:1, 0:1], S1[0:1, 1:2], ALU.mult)
        act(S1[0:1, 3:4], S1[0:1, 2:3], AF.Ln)
        act(S1[0:1, 4:5], S1[0:1, 3:4], AF.Exp, scale=0.5)
        ts("dve", S1[0:1, 5:6], S1[0:1, 4:5], -1.01 / 8.0, None, ALU.mult)
        ts("dve", S1[0:1, 6:7], S1[0:1, 4:5], -1.01 / 8.0, None, ALU.mult)

    def bound_broadcast():
        bkb = nb()
        mm(bkb[:, 0:2], ONER[0:1, :], S1[0:1, 5:7])
        cp("dve", NEGC[:, 0:1], bkb[:, 0:1])

    qr_ctr = [0]

    def rope(bank, N, dst, normcol=None):
        k = qr_ctr[0] % 2
        qr_ctr[0] += 1
        cp("act", QRAW[:, k, 0:N], bank[:, 0:N])
        if normcol is not None:
            norm_update(QRAW[:, k, 0:N], N, normcol)
        b2 = nb()
        mm(b2[:, 0:N], PERM[:], QRAW[:, k, 0:N])
        tt("dve", T1[:, k, 0:N], QRAW[:, k, 0:N], TAB[:, 0, 0:N], ALU.mult)
        tt("dve", T3[:, k, 0:N], b2[:, 0:N], TAB[:, 1, 0:N], ALU.mult)
        tt("dve", dst, T1[:, k, 0:N], T3[:, k, 0:N], ALU.add)

    def sgu_norm_tile(bank, tt_):
        k = tt_ % 2
        for gi in range(4):
            P.op("dve", (lambda e, gi=gi: e.bn_stats(ST[:, k, gi, :], bank[:, gi * 128:(gi + 1) * 128])),
                 [bank[:, gi * 128:(gi + 1) * 128]], [ST[:, k, gi, :]])
        for gi in range(4):
            P.op("dve", (lambda e, gi=gi: e.bn_aggr(MV[:, k, gi, :], ST[:, k, gi, :])), [ST[:, k, gi, :]],
                 [MV[:, k, gi, :]])
        act(RS[:, k, 0:4], MV[:, k, :, 1], AF.Ln, bias=LN_EPS)
        act(RS[:, k, 4:8], RS[:, k, 0:4], AF.Exp, scale=-0.5)
        for gi in range(4):
            ts("dve", VC[:, tt_, gi * 128:(gi + 1) * 128], bank[:, gi * 128:(gi + 1) * 128],
               MV[:, k, gi, 0:1], RS[:, k, 4 + gi:5 + gi], ALU.subtract, ALU.mult)

    SBK = [[BANK[0], BANK[1]], [BANK[2], BANK[3]]]
    OBK = [BANK[4], BANK[5], BANK[6]]
    MISCB = BANK[7].bitcast(BF16)
    SPAIR = [PS01, PS23]

    deferred = []
    T2f = T2[:].rearrange("p a n -> p (a n)")
    T3f = T3[:].rearrange("p a n -> p (a n)")
    AEP = T3[:, 1, :].rearrange("p (s n) -> p s n", s=4)

    def osacc(j):
        return T2f[:, j * 129:(j + 1) * 129] if j < 6 else T3f[:, (j - 6) * 129:(j - 5) * 129]

    def run_deferred(n=None):
        k = 0
        while deferred and (n is None or k < n):
            deferred.pop(0)()
            k += 1

    def attention(qT, q0, Nq, ktiles, cat):
        nsub = Nq // 128
        nacc = 2 * nsub
        nk_ = len(ktiles)
        hooks = {4, 9, 14, 19, 24}
        for h in range(4):
            accs = {}
            first_in_bank = {}
            for sub in range(nsub):
                for i in range(2):
                    j = sub * 2 + i
                    bkk = OBK[j // 3]
                    accs[(sub, i)] = bkk[:, (j % 3) * 129:(j % 3) * 129 + 129]
                    first_in_bank[(sub, i)] = (j % 3 == 0)
            for kt in range(nk_ + 1):
                if kt < nk_:
                    Kh = ktiles[kt][0](h)
                    for i in range(2):
                        sbank = SBK[kt % 2][i]
                        mm(sbank[:, 0:Nq], Kh[i * 64:(i + 1) * 64, :], qT[i * 64:(i + 1) * 64, h, q0:q0 + Nq])
                    act(ET[:, (kt % 2) * 2:(kt % 2) * 2 + 2, 0:Nq],
                        SPAIR[kt % 2][:, :].rearrange("p (i n) -> p i n", i=2)[:, :, 0:Nq],
                        AF.Exp, bias=NEGC[:, 0:1], scale=0.125)
                if kt >= 1:
                    k1 = kt - 1
                    Vh = ktiles[k1][1](h)
                    for sub in range(nsub):
                        for i in range(2):
                            mm(accs[(sub, i)], ET[:, (k1 % 2) * 2 + i, sub * 128:(sub + 1) * 128], Vh,
                               start=(k1 == 0 and first_in_bank[(sub, i)]), stop=(k1 == nk_ - 1),
                               skip_group_check=True)
                if kt in hooks:
                    run_deferred(1)
            run_deferred()
            for b_ in range((nacc + 2) // 3):
                n_in = min(3, nacc - 3 * b_)
                dst = T2f[:, b_ * 387:b_ * 387 + n_in * 129] if b_ < 2 else T3f[:, 0:n_in * 129]
                cp("dve", dst, OBK[b_][:, 0:n_in * 129])
            p_ = h % 2
            R = SM[:, p_ * 2, :]
            SS = SM[:, p_ * 2 + 1, 0:4]
            LNV = SM[:, 4 + p_, 0:4]
            RSTD = SM[:, 4 + p_, 4:8]

            def stage1(nsub=nsub, nacc=nacc, R=R, SS=SS):
                n6 = min(nacc, 6)
                lcol = T2f[:, 0:n6 * 129].rearrange("p (j c) -> p j c", c=129)[:, :, 128:129]
                rout = R[:, 0:n6].rearrange("p (j o) -> p j o", o=1)
                P.op("dve", lambda e: e.reciprocal(rout, lcol), [lcol], [rout])
                if nacc > 6:
                    lcol2 = T3f[:, 0:258].rearrange("p (j c) -> p j c", c=129)[:, :, 128:129]
                    rout2 = R[:, 6:8].rearrange("p (j o) -> p j o", o=1)
                    P.op("dve", lambda e: e.reciprocal(rout2, lcol2), [lcol2], [rout2])
                Rv = R[:, 0:nacc].rearrange("p (s i) -> p s i", i=2)
                ts("dve", Rv[:, :, 1], Rv[:, :, 1], NEGLAM, None, ALU.mult)
                for sub in range(nsub):
                    ts("dve", AEP[:, sub, :], osacc(2 * sub)[:, 0:128], R[:, 2 * sub:2 * sub + 1], None, ALU.mult)
                    stt(AEP[:, sub, :], osacc(2 * sub + 1)[:, 0:128], R[:, 2 * sub + 1:2 * sub + 2], AEP[:, sub, :],
                        ALU.mult, ALU.add)
                    P.op("dve", (lambda e, sub=sub: e.scalar_tensor_tensor(JUNK[:], AEP[:, sub, :], 1.0, AEP[:, sub, :],
                                                                           ALU.mult, ALU.mult, accum_out=SS[:, sub:sub + 1])),
                         [AEP[:, sub, :]], [JUNK[:], SS[:, sub:sub + 1]])

            def stage2(nsub=nsub, SS=SS, LNV=LNV, RSTD=RSTD):
                act(LNV[:, 0:nsub], SS[:, 0:nsub], AF.Ln, bias=LN_EPS, scale=1.0 / 128.0)
                act(RSTD[:, 0:nsub], LNV[:, 0:nsub], AF.Exp, scale=-0.5)

            def stage3(nsub=nsub, RSTD=RSTD):
                for sub in range(nsub):
                    stt(AN[:, sub, :], AEP[:, sub, :], RSTD[:, sub:sub + 1], GSUB[:], ALU.mult, ALU.mult)

            def stage4(nsub=nsub):
                for sub in range(nsub):
                    tr(MISCB[:, sub * 128:(sub + 1) * 128], AN[:, sub, :], IDB[:])

            def stage5(h=h, q0=q0, Nq=Nq, cat=cat):
                cp("dve", cat[:, h, q0:q0 + Nq], MISCB[:, 0:Nq])

            deferred.extend([stage1, stage2, stage3, stage4, stage5])

    def sgu_mix(N, cat):
        ntile = N // 128
        for gi in range(4):
            bk = nb()
            for t_ in range(ntile):
                mm(bk[:, t_ * 128:(t_ + 1) * 128], VC[:, t_, gi * 128:(gi + 1) * 128], WST[:, gi, :],
                   start=True, stop=False)
                mm(bk[:, t_ * 128:(t_ + 1) * 128], ONER[0:1, :], BSGU[0:1, gi * 128:(gi + 1) * 128],
                   start=False, stop=True)
            tt("dve", cat[:, 4 + gi, 0:N], bk[:, 0:N], UT[:, gi, 0:N], ALU.mult)

    pending_release = []

    class StatAcc:
        def __init__(self, XS, N):
            self.XS, self.N = XS, N
            while pending_release:
                pending_release.pop().release()
            self.im, self.ie = nb_reserve(), nb_reserve()
            self.Bm, self.Be = BANK[self.im], BANK[self.ie]
            self.pending = []
            self.n = 0

        def add(self, c):
            self.pending.append(c)
            if len(self.pending) > 1:
                self._emit(self.pending.pop(0))

        def _emit(self, c):
            N, XS = self.N, self.XS
            k = self.n % 2
            act(SQB[:, k, 0:N], XS[:, c, 0:N], AF.Square)
            cp("act", RB[:, k, 0:N], XS[:, c, 0:N])
            mm(self.Bm[:, 0:N], ONESB[:], RB[:, k, 0:N], start=(self.n == 0), stop=(self.n == 7))
            mm(self.Be[:, 0:N], ONESB[:], SQB[:, k, 0:N], start=(self.n == 0), stop=(self.n == 7))
            self.n += 1

        def finish(self):
            while self.pending:
                self._emit(self.pending.pop(0))
            assert self.n == 8

        def release(self):
            reserved.discard(self.im)
            reserved.discard(self.ie)

    def out_proj(name, cat, XS, N, layer, cd):
        st = StatAcc(XS, N)
        for ob in range(2):
            blk = wload(name, ob)
            for m4 in range(4):
                m = ob * 4 + m4
                bk = nb()
                proj_fm(blk, m4, cat, N, bk)
                stt(XS[:, m, 0:N], bk[:, 0:N], dc(layer, cd, 2, m), XS[:, m, 0:N], ALU.mult, ALU.add)
                st.add(m)
        return st

    def layernorm(XS, N, layer, cd, which, want_h, st):
        gname = f"ln_{which}_g{layer}"
        bname = f"ln_{which}_b{layer}"
        st.finish()
        Bm, Be = st.Bm, st.Be
        cp("act", T2[:, 0, 0:N], Bm[:, 0:N])
        tt("dve", T2[:, 1, 0:N], T2[:, 0, 0:N], Bm[:, 0:N], ALU.mult)
        tt("dve", T2[:, 1, 0:N], Be[:, 0:N], T2[:, 1, 0:N], ALU.subtract)
        act(T2[:, 1, 0:N], T2[:, 1, 0:N], AF.Ln, bias=EPS_R)
        act(Be[:, 0:N], T2[:, 1, 0:N], AF.Exp, scale=-0.5)
        for c in range(8):
            tt("dve", XS[:, c, 0:N], XS[:, c, 0:N], Bm[:, 0:N], ALU.subtract)
            tt("dve", XS[:, c, 0:N], XS[:, c, 0:N], Be[:, 0:N], ALU.mult)
            if want_h:
                act(HT[:, c, 0:N], XS[:, c, 0:N], AF.Identity, bias=dc(layer, cd, 4, c), scale=dc(layer, cd, 3, c))
            act(XS[:, c, 0:N], XS[:, c, 0:N], AF.Identity, bias=col2(bname, c), scale=col2(gname, c))
        pending_release.append(st)

    t_ctr = [0]

    def ffn(XS, N, layer, cd):
        n1, n2 = f"w_ff1_{layer}", f"w_ff2_{layer}"
        st = None
        for hf in range(2):
            if hf == 1:
                st = StatAcc(XS, N)
            for j4 in range(4):
                blk = wload(n1, hf * 4 + j4)
                first = (hf == 0 and j4 == 0)
                if first:
                    bks = [nb() for _ in range(4)]
                    for kc in range(8):
                        for m4 in range(4):
                            mm(bks[m4][:, 0:N], blk[:, kc, m4 * 128:(m4 + 1) * 128], HT[:, kc, 0:N],
                               start=(kc == 0), stop=(kc == 7))
                for m4 in range(4):
                    if first:
                        bk = bks[m4]
                    else:
                        bk = nb()
                        proj_fm(blk, m4, HT, N, bk)
                    k = t_ctr[0] % 2
                    t_ctr[0] += 1
                    cp("act", T1[:, k, 0:N], bk[:, 0:N])
                    stt(HID[:, j4 * 4 + m4, 0:N], bk[:, 0:N], 0.0, T1[:, k, 0:N], ALU.max, ALU.mult)
            for mb in range(4):
                blk = wload(n2, hf * 4 + mb)
                for m2 in range(2):
                    m = mb * 2 + m2
                    bk = nb()
                    for kc in range(16):
                        mm(bk[:, 0:N], blk[:, kc, m2 * 128:(m2 + 1) * 128], HID[:, kc, 0:N], start=(kc == 0),
                           stop=(kc == 15))
                    stt(XS[:, m, 0:N], bk[:, 0:N], dc(layer, cd, 5, m), XS[:, m, 0:N], ALU.mult, ALU.add)
                    if hf == 1:
                        st.add(m)
        return st

    stg_ctr = [0]

    def store_tiles(XS, ntile, dst_rows_fn):
        for t_ in range(ntile):
            k = stg_ctr[0] % 2
            stg_ctr[0] += 1
            for half in range(2):
                bk = nb()
                for cc in range(4):
                    tr(bk[:, cc * 128:(cc + 1) * 128], XS[:, half * 4 + cc, t_ * 128:(t_ + 1) * 128], IDF[:])
                cp("act" if half == 0 else "dve", STG[:, k, half * 512:(half + 1) * 512], bk[:, 0:512])
            dma("pool", dst_rows_fn(t_), STG[:, k, :], semkey=f"stg{k}")

    def layer0_group(src_fn, ntile, cd, slot, sample, tab0, ktiles_fn, kvout_fn):
        N = ntile * 128
        XS = XR[:, slot]
        for t_ in range(ntile):
            load_tile(src_fn(t_), cd, 0, XS, HT, t_)
        if sample:
            dma("sp", TAB[:, 0, 0:N], cosT[:, tab0:tab0 + N], semkey="tabc")
            dma("sp", TAB[:, 1, 0:N], sinT[:, tab0:tab0 + N], semkey="tabs")
        if stop < 3.02:
            return
        blk = wload("w_in0", 0)
        for m4 in range(4):
            bk = nb()
            proj_fm(blk, m4, HT, N, bk)
            if sample:
                rope(bk, N, QT[:, m4, 0:N], QCOLS[:, m4:m4 + 1])
            else:
                cp("act", QT[:, m4, 0:N], bk[:, 0:N])
                norm_update(QT[:, m4, 0:N], N, QCOLS[:, m4:m4 + 1])
        cols_max(QCOLS[:, 0:4], MAXC[:, 0:1])
        if not sample:
            blk = wload("w_in0", 1)
            for m4 in range(4):
                bk = nb()
                proj_fm(blk, m4, HT, N, bk)
                cp("act", KTP[:, m4, 0:N], bk[:, 0:N])
                norm_update(KTP[:, m4, 0:N], N, KCOLS[:, m4:m4 + 1])
            cols_max(KCOLS[:, 0:4], MAXC[:, 1:2])
            blkv = wload("w_in0", 2)
            for t_ in range(ntile):
                k = stg_ctr[0] % 2
                stg_ctr[0] += 1
                bk = nb()
                proj_tm(blk, HT, t_, bk)
                cp("dve", STG[:, k, 0:512], bk[:, 0:512])
                bk2 = nb()
                proj_tm(blkv, HT, t_, bk2)
                cp("act", STG[:, k, 512:1024], bk2[:, 0:512])
                cp("dve", VV[:, t_, :].rearrange("p (h c) -> p h c", h=4)[:, :, 0:128],
                   bk2[:, 0:512].rearrange("p (h c) -> p h c", h=4))
                kd, vd_ = kvout_fn(t_)
                dma("pool", kd, STG[:, k, 0:512], semkey=f"stg{k}")
                dma("pool", vd_, STG[:, k, 512:1024], semkey=f"stg{k}")
        if stop < 3.03:
            return
        blk = wload("w_in0", 3)
        for m4 in range(4):
            bk = nb()
            proj_fm(blk, m4, HT, N, bk)
            cp("act" if m4 % 2 == 0 else "dve", UT[:, m4, 0:N], bk[:, 0:N])
        bound_finalize()
        if stop < 3.04:
            return
        blk = wload("w_in0", 4)
        for t_ in range(ntile):
            bk = nb()
            proj_tm(blk, HT, t_, bk)
            if stop >= 3.05:
                sgu_norm_tile(bk, t_)
        if stop < 3.2:
            return
        bound_broadcast()
        for (q0, Nq, kts) in ktiles_fn(N):
            attention(QT, q0, Nq, kts, HT)
        run_deferred()
        if stop < 3.3:
            return
        sgu_mix(N, HT)
        if stop < 3.4:
            return
        st = out_proj("w_out0", HT, XS, N, 0, cd)
        layernorm(XS, N, 0, cd, "mix", True, st)
        st = ffn(XS, N, 0, cd)
        layernorm(XS, N, 0, cd, "ff", False, st)

    def layer1_group(ntile, cd, slot, segs, xnext, use_prev, dst_rows_fn):
        N = ntile * 128
        XS = XR[:, slot]
        for c in range(8):
            act(HT[:, c, 0:N], XS[:, c, 0:N], AF.Identity, bias=dc(1, cd, 1, c), scale=dc(1, cd, 0, c))
        if xnext is not None:
            for c in range(8):
                act(HX[:, c, 0:1], xnext[:, c, 0:1], AF.Identity, bias=dc(1, cd, 1, c), scale=dc(1, cd, 0, c))
        for half in range(2):
            bgb = wload("w_in1", 0 + half)
            cgb = wload("w_in1", 2 + half)
            xtb = wload("w_in1", 4 + half)
            for m4 in range(4):
                m = half * 4 + m4
                k = m % 2
                bC, bX, bB = nb(), nb(), nb()
                proj_fm(cgb, m4, HT, N, bC)
                proj_fm(xtb, m4, HT, N, bX)
                proj_fm(bgb, m4, HT, N, bB)
                cp("act", T1[:, k, 0:N], bC[:, 0:N])
                Z = T3[:, k, 0:N]
                C = T2[:, k, 0:N]
                tt("dve", Z, bX[:, 0:N], T1[:, k, 0:N], ALU.mult)
                ts("dve", C, Z, col2("conv", 8 + m), None, ALU.mult)
                for (a, b) in segs:
                    stt(C[:, a + 1:b], Z[:, a:b - 1], col2("conv", 0 + m), C[:, a + 1:b], ALU.mult, ALU.add)
                    stt(C[:, a:b - 1], Z[:, a + 1:b], col2("conv", 16 + m), C[:, a:b - 1], ALU.mult, ALU.add)
                if use_prev:
                    stt(C[:, 0:1], ZSAVE[:, m:m + 1], col2("conv", 0 + m), C[:, 0:1], ALU.mult, ALU.add)
                    cp("dve", ZSAVE[:, m:m + 1], Z[:, N - 1:N])
                if xnext is not None:
                    bH = nb()
                    for kc in range(8):
                        mm(bH[:, 0:1], cgb[:, kc, m4 * 128:(m4 + 1) * 128], HX[:, kc, 0:1], start=(kc == 0),
                           stop=(kc == 7))
                    for kc in range(8):
                        mm(bH[:, 2:3], xtb[:, kc, m4 * 128:(m4 + 1) * 128], HX[:, kc, 0:1], start=(kc == 0),
                           stop=(kc == 7), skip_group_check=True)
                    cp("act", SMZ[:, m, 0:1], bH[:, 0:1])
                    tt("dve", SMZ[:, m, 1:2], bH[:, 2:3], SMZ[:, m, 0:1], ALU.mult)
                    stt(C[:, N - 1:N], SMZ[:, m, 1:2], col2("conv", 16 + m), C[:, N - 1:N], ALU.mult, ALU.add)
                tt("dve", CAT1[:, m, 0:N], bB[:, 0:N], C, ALU.mult)
        st = out_proj("w_out1", CAT1, XS, N, 1, cd)
        layernorm(XS, N, 1, cd, "mix", True, st)
        st = ffn(XS, N, 1, cd)
        layernorm(XS, N, 1, cd, "ff", False, st)
        store_tiles(XS, ntile, dst_rows_fn)

    HTB = [HT, CAT1]

    def kv_load(g, t_):
        tile_i = g * 4 + t_
        load_tile(xs[tile_i * 128:(tile_i + 1) * 128, :], 0, 0, None, HTB[g % 2], t_)

    def kv_tab(g):
        dma("sp", TAB[:, 0, :], cosT[:, g * 512:(g + 1) * 512], semkey="tabc")
        dma("sp", TAB[:, 1, :], sinT[:, g * 512:(g + 1) * 512], semkey="tabs")

    NKV = 8 if stop >= 2 else 0
    if NKV:
        for t_ in range(4):
            kv_load(0, t_)
        kv_tab(0)
    for kvg in range(NKV):
        H = HTB[kvg % 2]
        blk = wload("w_in0", 1)
        for m4 in range(4):
            bk = nb()
            proj_fm(blk, m4, H, 512, bk)
            rope(bk, 512, KT[:, m4, kvg * 512:(kvg + 1) * 512], KCOLS[:, kvg * 4 + m4:kvg * 4 + m4 + 1])
            if kvg + 1 < NKV:
                kv_load(kvg + 1, m4)
        if kvg + 1 < NKV:
            kv_tab(kvg + 1)
        blk = wload("w_in0", 2)
        for t_ in range(4):
            bk = nb()
            proj_tm(blk, H, t_, bk)
            cp("act" if t_ % 2 == 0 else "dve",
               VV[:, kvg * 4 + t_, :].rearrange("p (h c) -> p h c", h=4)[:, :, 0:128],
               bk[:, 0:512].rearrange("p (h c) -> p h c", h=4))
    for j in range(2 if stop >= 2 else 0):
        k = xin_ctr[0] % 2
        xin_ctr[0] += 1
        dma("sp", XIN[:, k, 0:512], ck[j * 128:(j + 1) * 128, :], semkey=f"xin{k}")
        bk = nb()
        for m4 in range(4):
            tr(bk[:, m4 * 128:(m4 + 1) * 128], XIN[:, k, m4 * 128:(m4 + 1) * 128], IDF[:])
        cp("dve", KT[:, :, (32 + j) * 128:(33 + j) * 128], bk[:, 0:512].rearrange("p (c n) -> p c n", c=4))
        for m4 in range(4):
            norm_update(KT[:, m4, (32 + j) * 128:(33 + j) * 128], 128, KCOLS[:, 32 + j * 4 + m4:33 + j * 4 + m4])
        dma("pool", VV[:, 32 + j, :].rearrange("p (h c) -> p h c", h=4)[:, :, 0:128],
            cv[j * 128:(j + 1) * 128, :].rearrange("p (h c) -> p h c", h=4), semkey=f"cvld{j}")

    if stop >= 2:
        cols_max(KCOLS[:, 0:40], MAXC[:, 1:2])
    modulation(0, [4, 5, 6, 7, 8, 9, 10, 11], "b")
    cast_all(["w_out0", "w_ff1_0", "w_ff2_0"])

    def sample_ktiles(N):
        kts = []
        for kt in range(34):
            kts.append(((lambda h, kt=kt: KT[:, h, kt * 128:(kt + 1) * 128]),
                        (lambda h, kt=kt: VV[:, kt, h * 129:(h + 1) * 129])))
        return [(0, N, kts)]

    groups = [(0, 4), (4, 4), (8, 4), (12, 3), (15, 2)]

    def s_l0(g):
        t0, nt = groups[g]
        layer0_group(lambda t_: xs[(t0 + t_) * 128:(t0 + t_ + 1) * 128, :], nt, 0, g % 2, True, t0 * 128,
                     sample_ktiles, None)

    def s_l1(g):
        t0, nt = groups[g]
        xnext = XR[:, (g + 1) % 2] if g + 1 < len(groups) else None
        layer1_group(nt, 0, g % 2, [(0, nt * 128)], xnext, True,
                     lambda t_: ys[(t0 + t_) * 128:(t0 + t_ + 1) * 128, :])

    if stop >= 3:
        s_l0(0)
    cast_all(["w_in1", "w_out1", "w_ff1_1", "w_ff2_1"])
    if stop >= 4:
        for g in range(1, len(groups)):
            s_l0(g)
            if g == 1:
                modulation(1, list(range(12)), "ab")
            s_l1(g - 1)
        s_l1(len(groups) - 1)

    def prompt_ktiles(N):
        res = []
        for bi in range(2):
            kts = []
            for j in range(2):
                kt = bi * 2 + j
                kts.append(((lambda h, kt=kt: KTP[:, h, kt * 128:(kt + 1) * 128]),
                            (lambda h, kt=kt: VV[:, kt, h * 129:(h + 1) * 129])))
            res.append((bi * 256, 256, kts))
        return res

    for pg in range(2 if stop >= 5 else 0):
        r0 = pg * 512
        layer0_group(lambda t_: xp[r0 + t_ * 128: r0 + (t_ + 1) * 128, :], 4, 1, 0, False, 0, prompt_ktiles,
                     lambda t_: (nk[r0 + t_ * 128: r0 + (t_ + 1) * 128, :], nv[r0 + t_ * 128: r0 + (t_ + 1) * 128, :]))
        layer1_group(4, 1, 0, [(0, 256), (256, 512)], None, False,
                     lambda t_: yp[r0 + t_ * 128: r0 + (t_ + 1) * 128, :])

    ops = P.ops
    for o in ops:
        for d in o.deps:
            ops[d].has_dep = True
    ENG = ["pe", "act", "dve", "pool", "sp"]
    semkeys = sorted({o.semkey for o in ops if o.dma})
    sems = {}
    for e_ in ENG:
        sems[e_] = es.enter_context(nc.semaphore("s_" + e_))
    for k in semkeys:
        sems["d_" + k] = es.enter_context(nc.semaphore("d_" + k))
    ecount = {e_: 0 for e_ in ENG}
    dcount = {k: 0 for k in semkeys}
    for o in ops:
        if o.dma:
            dcount[o.semkey] += 16
            o.cnt = dcount[o.semkey]
        elif o.has_dep:
            ecount[o.eng] += 1
            o.cnt = ecount[o.eng]
    dtotal = dict(dcount)
    out_keys = [k for k in semkeys if k.startswith("stg")]

    block = es.enter_context(nc.Block())

    def emit_engine(ename, e):
        waited = {}
        for o in ops:
            if o.eng != ename:
                continue
            need = {}
            for d in o.deps:
                p = ops[d]
                if p.dma:
                    sk = "d_" + p.semkey
                    val = dtotal[p.semkey] if p.whole else p.cnt
                else:
                    if p.eng == "pe" and ename == "pe":
                        continue
                    sk = p.eng
                    val = p.cnt
                if val > need.get(sk, 0):
                    need[sk] = val
            for sk, val in need.items():
                if waited.get(sk, 0) >= val:
                    continue
                e.wait_ge(sems[sk], val)
                waited[sk] = val
            ins = o.fn(e)
            if o.dma:
                ins.then_inc(sems["d_" + o.semkey], 16)
            elif o.has_dep:
                ins.then_inc(sems[ename], 1)
        if ename == "pool":
            for k in out_keys:
                e.wait_ge(sems["d_" + k], dtotal[k])

    @block.tensor
    def _(e):
        emit_engine("pe", e)

    @block.scalar
    def _(e):
        emit_engine("act", e)

    @block.vector
    def _(e):
        emit_engine("dve", e)

    @block.gpsimd
    def _(e):
        emit_engine("pool", e)

    @block.sync
    def _(e):
        emit_engine("sp", e)

    es.close()
    return nc


_NC_CACHE = {}


def _rope_tables(order):
    pos = (np.asarray(order)[:, None] * 128 + np.arange(128)[None, :]).reshape(-1)
    row = (pos // 64).astype(np.float32)
    col = (pos % 64).astype(np.float32)
    inv = (1.0 / (np.float32(10000.0) ** (np.arange(16, dtype=np.float32) / np.float32(16)))).astype(np.float32)
    ang = [row[:, None] * inv[None, :], col[:, None] * inv[None, :]]
    cosT = np.zeros((128, 4096), np.float32)
    sinT = np.zeros((128, 4096), np.float32)
    for p in range(128):
        pm = p % 64
        s = pm // 32
        j = (pm % 32) // 16
        f = pm % 16
        cosT[p] = np.cos(ang[s][:, f])
        sinT[p] = np.sin(ang[s][:, f]) * (-1.0 if j == 0 else 1.0)
    return cosT, sinT


def kernel(**inp):
    f = lambda a: np.ascontiguousarray(np.asarray(a, dtype=np.float32))
    x_prompt, x_sample = f(inp["x_prompt"]), f(inp["x_sample"])
    cache_k0, cache_v0 = f(inp["cache_k0"]), f(inp["cache_v0"])
    c, c_ctx = f(inp["c"]), f(inp["c_ctx"])
    if "nc" not in _NC_CACHE:
        _NC_CACHE["nc"] = build_nc()
    nc = _NC_CACHE["nc"]
    ident = np.eye(128, dtype=np.float32)
    perm = np.zeros((128, 128), np.float32)
    for m in range(128):
        perm[m ^ 16, m] = 1.0
    shared = {"c_ident": ident, "c_perm": perm}
    for name, _, _ in W_SPECS:
        shared[name] = f(inp[name])
    for name in ("w_mod0", "w_mod1", "conv_w1", "subln_g0", "sgu_w0", "sgu_b0", "lambda_q1_0", "lambda_k1_0",
                 "lambda_q2_0", "lambda_k2_0"):
        shared[name] = f(inp[name])
    for name, _ in VEC_IN:
        shared[name] = f(inp[name])
    in_maps = []
    orders = []
    for core in range(8):
        b, half = core // 2, core % 2
        win = list(range(0, 17)) if half == 0 else list(range(15, 32))
        others = [t for t in range(32) if t not in win]
        order = win + others
        orders.append(order)
        xt = x_sample[b].reshape(32, 128, 1024)[order].reshape(4096, 1024)
        cosT, sinT = _rope_tables(order)
        m = dict(shared)
        m["xs"] = np.ascontiguousarray(xt)
        m["xp"] = np.ascontiguousarray(x_prompt[4 * core:4 * core + 4].reshape(1024, 1024))
        m["ck"] = np.ascontiguousarray(cache_k0[b].reshape(256, 512))
        m["cv"] = np.ascontiguousarray(cache_v0[b].reshape(256, 512))
        m["cvec"] = np.ascontiguousarray(np.stack([c[b], c_ctx], 0))
        m["cosT"] = cosT
        m["sinT"] = sinT
        in_maps.append(m)
    res = run_bass_kernel_spmd(nc, in_maps, core_ids=list(range(8)))
    y_prompt = np.zeros((32, 256, 1024), np.float32)
    y_sample = np.zeros((4, 4096, 1024), np.float32)
    new_k = np.zeros((32, 256, 4, 2, 64), np.float32)
    new_v = np.zeros((32, 256, 4, 128), np.float32)
    for core in range(8):
        r = res.results[core]
        b, half = core // 2, core % 2
        ysc = np.asarray(r["ys"])
        if half == 0:
            y_sample[b, 0:2048] = ysc[0:2048]
        else:
            y_sample[b, 2048:4096] = ysc[128:2176]
        y_prompt[4 * core:4 * core + 4] = np.asarray(r["yp"]).reshape(4, 256, 1024)
        new_k[4 * core:4 * core + 4] = np.asarray(r["nk"]).reshape(4, 256, 4, 2, 64)
        new_v[4 * core:4 * core + 4] = np.asarray(r["nv"]).reshape(4, 256, 4, 128)
    return (y_prompt, y_sample, new_k, new_v)
```

```python
import math
import os
from contextlib import ExitStack

import numpy as np
import concourse.bass as bass
import concourse.mybir as mybir
from concourse.bass_utils import run_bass_kernel_spmd

F32 = mybir.dt.float32
BF16 = mybir.dt.bfloat16
AF = mybir.ActivationFunctionType
ALU = mybir.AluOpType
AX = mybir.AxisListType

D = 1024
ALPHA = 4.0 ** 0.25
LAMBDA_INIT = 0.8 - 0.6 * math.exp(-0.3 * 0)
LN_EPS = 1e-5
EPS_R = LN_EPS / (ALPHA * ALPHA)
NWIN = 17
DSZ = {F32: 4, BF16: 2}


class Op:
    __slots__ = ("eng", "fn", "deps", "dma", "semkey", "whole", "has_dep", "cnt")

    def __init__(self, eng, fn, dma, semkey, whole):
        self.eng, self.fn, self.dma, self.semkey, self.whole = eng, fn, dma, semkey, whole
        self.deps = set()
        self.has_dep = False
        self.cnt = 0


def _region(ap):
    name = ap.tensor.name
    es = DSZ.get(ap.dtype, 4)
    dims = list(ap.ap)
    off = ap.offset
    if str(ap.space) == "DRAM":
        span = sum((c - 1) * abs(s) for s, c in dims)
        return (name, 0, 1, off * es, (off + span + 1) * es)
    ps, pc = dims[0]
    ps = max(ps, 1)
    p0 = off // ps
    f0 = off % ps
    span = sum((c - 1) * abs(s) for s, c in dims[1:])
    if str(ap.space) == "PSUM":
        return (name, 0, 128, (f0 * es) // 2048 * 2048, ((f0 + span + 1) * es + 2047) // 2048 * 2048)
    return (name, p0, p0 + pc, f0 * es, (f0 + span + 1) * es)


class Prog:
    def __init__(self):
        self.ops = []
        self.recs = {}

    def _rkey(self, idx):
        o = self.ops[idx]
        return (o.eng, o.semkey)

    def _touch(self, idx, ap, write):
        name, p0, p1, lo, hi = _region(ap)
        lst = self.recs.setdefault(name, [])
        deps = self.ops[idx].deps
        keep = []
        for r in lst:
            rp0, rp1, rlo, rhi, w, rd = r
            if rp1 <= p0 or p1 <= rp0 or rhi <= lo or hi <= rlo:
                keep.append(r)
                continue
            if w is not None:
                deps.add(w)
            if write:
                deps.update(rd.values())
                if p0 <= rp0 and rp1 <= p1 and lo <= rlo and rhi <= hi:
                    continue
            keep.append(r)
        if write:
            keep.append([p0, p1, lo, hi, idx, {}])
        else:
            done = False
            for r in keep:
                if r[0] <= p0 and p1 <= r[1] and r[2] <= lo and hi <= r[3]:
                    r[5][self._rkey(idx)] = idx
                    done = True
                    break
            if not done:
                keep.append([p0, p1, lo, hi, None, {self._rkey(idx): idx}])
        self.recs[name] = keep

    def op(self, eng, fn, reads=(), writes=(), dma=False, semkey=None, whole=False):
        idx = len(self.ops)
        self.ops.append(Op(eng, fn, dma, semkey, whole))
        for a in reads:
            if a is not None and not isinstance(a, (int, float)):
                self._touch(idx, a, str(a.space) == "PSUM")
        for a in writes:
            self._touch(idx, a, True)
        o = self.ops[idx]
        o.deps.discard(idx)
        return idx


W_SPECS = [
    ("w_in0", 1024, 2560), ("w_out0", 1024, 1024), ("w_ff1_0", 1024, 4096), ("w_ff2_0", 4096, 1024),
    ("w_in1", 1024, 3072), ("w_out1", 1024, 1024), ("w_ff1_1", 1024, 4096), ("w_ff2_1", 4096, 1024),
]
VEC_IN = [("b_mod0", 6144), ("b_mod1", 6144), ("ln_mix_g0", 1024), ("ln_mix_b0", 1024), ("ln_ff_g0", 1024),
          ("ln_ff_b0", 1024), ("ln_mix_g1", 1024), ("ln_mix_b1", 1024), ("ln_ff_g1", 1024), ("ln_ff_b1", 1024)]


def build_nc(stop=99):
    nc = bass.Bass("TRN2", target_bir_lowering=False)
    P = Prog()
    es = ExitStack()

    def din(name, shape, dt=F32):
        return nc.dram_tensor(name, list(shape), dt, kind="ExternalInput").ap()

    def dout(name, shape, dt=F32):
        return nc.dram_tensor(name, list(shape), dt, kind="ExternalOutput").ap()

    xs = din("xs", [4096, 1024])
    xp = din("xp", [1024, 1024])
    ck = din("ck", [256, 512])
    cv = din("cv", [256, 512])
    cvec = din("cvec", [2, 1024])
    cosT = din("cosT", [128, 4096])
    sinT = din("sinT", [128, 4096])
    c_ident = din("c_ident", [128, 128])
    c_perm = din("c_perm", [128, 128])
    wd = {}
    for name, K, ncol in W_SPECS:
        wd[name] = din(name, [K, ncol])
    wd["w_mod0"] = din("w_mod0", [1024, 6144])
    wd["w_mod1"] = din("w_mod1", [1024, 6144])
    vd = {n: din(n, [ln]) for n, ln in VEC_IN}
    conv_w1 = din("conv_w1", [3, 1024])
    lam_in = {n: din(n, [64]) for n in ("lambda_q1_0", "lambda_k1_0", "lambda_q2_0", "lambda_k2_0")}
    subln_g0 = din("subln_g0", [128])
    sgu_w0 = din("sgu_w0", [4, 128, 128])
    sgu_b0 = din("sgu_b0", [4, 128])

    ys = dout("ys", [NWIN * 128, 1024])
    yp = dout("yp", [1024, 1024])
    nk = dout("nk", [1024, 512])
    nv = dout("nv", [1024, 512])

    blk_of = {}
    nblk = 0
    for name, K, ncol in W_SPECS:
        if K == 1024:
            n = ncol // 512
        else:
            n = 8
        blk_of[name] = (nblk, n)
        nblk += n
    wscr = nc.dram_tensor("wscr", [nblk, 128, 4096], BF16, kind="Internal").ap()

    def sb(name, shape, dt):
        return es.enter_context(nc.sbuf_tensor(name, list(shape), dt))

    KT = sb("KT", [128, 4, 4352], BF16)
    VV = sb("VV", [128, 34, 516], BF16)
    XR = sb("XR", [128, 2, 8, 512], F32)
    XIN = sb("XIN", [128, 2, 1024], F32)
    STG = sb("STG", [128, 2, 1024], F32)
    HT = sb("HT", [128, 8, 512], BF16)
    HX = sb("HX", [128, 8, 2], BF16)
    TAB = sb("TAB", [128, 2, 512], F32)
    T1 = sb("T1", [128, 2, 512], F32)
    T2 = sb("T2", [128, 2, 512], F32)
    T3 = sb("T3", [128, 2, 512], F32)
    QRAW = T2
    RB = T1[:, 0, :].bitcast(BF16).rearrange("p (a n) -> p a n", a=2)
    ARENA = sb("ARENA", [128, 10240], BF16)
    WS = sb("WS", [128, 4, 4096], BF16)
    IDF = sb("IDF", [128, 128], F32)
    IDB = sb("IDB", [128, 128], BF16)
    PERM = sb("PERM", [128, 128], F32)
    WST = sb("WST", [128, 4, 128], BF16)
    GSUB = sb("GSUB", [128, 128], F32)
    VB1 = T2[:, 0, 0:128]
    VB2 = T2[:, 1, 0:128]
    COLS1 = sb("COLS1", [128, 96], F32)
    COLS2 = sb("COLS2", [128, 104], F32)
    SIL = sb("SIL", [128, 8, 2], BF16)
    MODT = sb("MODT", [128, 2, 48, 2], F32)
    NVD = 8
    DCOL = sb("DCOL", [128, 2, 2, NVD, 8], F32)
    BSGU = sb("BSGU", [1, 512], F32)
    ONER = sb("ONER", [1, 128], F32)
    LAMT = T3[:, 0, 0:256].rearrange("p (a b) -> p a b", a=4)
    LAMS = sb("LAMS", [128, 8], F32)
    NEGC = sb("NEGC", [128, 1], F32)
    JUNK = sb("JUNK", [128, 128], F32)
    AN = sb("AN", [128, 4, 128], BF16)
    SM = sb("SM", [128, 8, 8], F32)
    ST = sb("ST", [128, 2, 4, 6], F32)
    MV = sb("MV", [128, 2, 4, 2], F32)
    RS = sb("RS", [128, 2, 8], F32)
    ZSAVE = sb("ZSAVE", [128, 8], F32)
    SMZ = sb("SMZ", [128, 8, 2], F32)
    SGW = T1[:, 0, :].rearrange("p (a b) -> p a b", a=4)

    BLK1 = sb("BLK1", [128, 128], BF16)
    ONESB = sb("ONESB", [128, 128], BF16)
    SQB = sb("SQB", [128, 2, 512], BF16)
    KCOLS = sb("KCOLS", [128, 48], F32)
    QCOLS = sb("QCOLS", [128, 8], F32)
    MAXC = sb("MAXC", [128, 2], F32)
    S1 = sb("S1", [1, 16], F32)

    PS01 = es.enter_context(nc.psum_tensor("PS01", [128, 1024], F32))
    PS23 = es.enter_context(nc.psum_tensor("PS23", [128, 1024], F32))
    BANK = [PS01[:, 0:512], PS01[:, 512:1024], PS23[:, 0:512], PS23[:, 512:1024]]
    BANK += [es.enter_context(nc.psum_tensor(f"B{i}", [128, 512], F32))[:] for i in range(4, 8)]

    QT = ARENA[:, 0:2048].rearrange("p (c n) -> p c n", c=4)
    VC = ARENA[:, 2048:4096].rearrange("p (c n) -> p c n", c=4)
    CAT1 = ARENA[:, 0:4096].rearrange("p (c n) -> p c n", c=8)
    ET = ARENA[:, 4096:6144].rearrange("p (c n) -> p c n", c=4)
    UT = ARENA[:, 6144:10240].bitcast(F32).rearrange("p (c n) -> p c n", c=4)
    HID = ARENA[:, 0:8192].rearrange("p (c n) -> p c n", c=16)
    KTP = KT[:, :, 0:512]

    bank_ctr = [0]

    reserved = set()

    def nb():
        while (bank_ctr[0] % 8) in reserved:
            bank_ctr[0] += 1
        b = BANK[bank_ctr[0] % 8]
        bank_ctr[0] += 1
        return b

    def nb_reserve():
        while (bank_ctr[0] % 8) in reserved:
            bank_ctr[0] += 1
        i = bank_ctr[0] % 8
        bank_ctr[0] += 1
        reserved.add(i)
        return i

    def mm(out, lhsT, rhs, start=True, stop=True, **kw):
        P.op("pe", lambda e: e.matmul(out, lhsT, rhs, start=start, stop=stop, **kw), [lhsT, rhs], [out])

    def tr(out, in_, ident):
        P.op("pe", lambda e: e.transpose(out, in_, ident), [in_, ident], [out])

    def act(out, in_, func, bias=None, scale=None, accum=None):
        kw = {}
        if bias is not None:
            kw["bias"] = bias
        if scale is not None:
            kw["scale"] = scale
        if accum is not None:
            kw["accum_out"] = accum
        wr = [out] + ([accum] if accum is not None else [])
        P.op("act", lambda e: e.activation(out, in_, func, **kw), [in_, bias, scale], wr)

    def ts(eng, out, in0, s1, s2, op0, op1=None):
        if op1 is None:
            P.op(eng, lambda e: e.tensor_scalar(out, in0, s1, None, op0), [in0, s1], [out])
        else:
            P.op(eng, lambda e: e.tensor_scalar(out, in0, s1, s2, op0, op1), [in0, s1, s2], [out])

    def tt(eng, out, in0, in1, op):
        P.op(eng, lambda e: e.tensor_tensor(out, in0, in1, op), [in0, in1], [out])

    def stt(out, in0, scalar, in1, op0, op1):
        P.op("dve", lambda e: e.scalar_tensor_tensor(out, in0, scalar, in1, op0, op1), [in0, scalar, in1], [out])

    def cp(eng, out, in_):
        if eng == "act":
            P.op("act", lambda e: e.activation(out, in_, AF.Copy), [in_], [out])
        else:
            P.op(eng, lambda e: e.tensor_copy(out, in_), [in_], [out])

    def dma(q, out, in_, semkey, whole=False, **kw):
        P.op(q, lambda e: e.dma_start(out=out, in_=in_, **kw), [in_], [out], dma=True, semkey=semkey, whole=whole)

    def memset(eng, ap, val):
        P.op(eng, lambda e: e.memset(ap, val), [], [ap])

    def w_src(name, b):
        K = dict((n, k) for n, k, _ in W_SPECS)[name]
        w = wd[name]
        if K == 1024:
            return w[:, b * 512:(b + 1) * 512].rearrange("(kc p) n -> p kc n", p=128), 8, 512
        hf, mb = b // 4, b % 4
        return (w[hf * 2048:(hf + 1) * 2048, mb * 256:(mb + 1) * 256].rearrange("(kc p) n -> p kc n", p=128),
                16, 256)

    def scr_view(name, b):
        base, n = blk_of[name]
        _, kc, ncol = w_src(name, b)
        return wscr[base + b].rearrange("p (kc n) -> p kc n", kc=kc)

    def cast_all(names):
        for name in names:
            base, n = blk_of[name]
            for b in range(n):
                src, kc, ncol = w_src(name, b)
                dma("pool", scr_view(name, b), src, semkey="cast_" + name, whole=True)

    ws_ctr = [0]

    def wload(name, b):
        s = ws_ctr[0] % 4
        ws_ctr[0] += 1
        _, kc, ncol = w_src(name, b)
        dst = WS[:, s, :].rearrange("p (kc n) -> p kc n", kc=kc)
        dma("sp", dst, scr_view(name, b), semkey=f"ws{s}")
        return dst

    def wload_mod(layer, b):
        s = ws_ctr[0] % 4
        ws_ctr[0] += 1
        dst = WS[:, s, :].rearrange("p (kc n) -> p kc n", kc=8)
        src = wd[f"w_mod{layer}"][:, b * 512:(b + 1) * 512].rearrange("(kc p) n -> p kc n", p=128)
        dma("pool", dst, src, semkey=f"wm{s}")
        return dst

    def vrows(ap1d, n):
        return ap1d.rearrange("(c p) -> c p", p=128)

    dma("sp", IDF[:], c_ident, "setup", whole=True)
    dma("sp", PERM[:], c_perm, "setup", whole=True)
    dma("sp", VB1[0:48, :], vrows(vd["b_mod0"], 48), "setup", whole=True)
    dma("sp", VB1[48:96, :], vrows(vd["b_mod1"], 48), "setup", whole=True)
    r = 0
    vb2_off = {}
    for l in range(2):
        for nm in ("ln_mix_g", "ln_mix_b", "ln_ff_g", "ln_ff_b"):
            dma("sp", VB2[r:r + 8, :], vrows(vd[f"{nm}{l}"], 8), "setup", whole=True)
            vb2_off[f"{nm}{l}"] = r
            r += 8
    dma("sp", VB2[r:r + 24, :], conv_w1.rearrange("t (c p) -> (t c) p", p=128), "setup", whole=True)
    vb2_off["conv"] = r
    r += 24
    dma("sp", VB2[r:r + 16, :], cvec.rearrange("b (c p) -> (b c) p", p=128), "setup", whole=True)
    vb2_off["cvec"] = r
    r += 16
    assert r == 104
    for i, n in enumerate(("lambda_q1_0", "lambda_k1_0", "lambda_q2_0", "lambda_k2_0")):
        dma("sp", LAMT[:, i, :], lam_in[n].partition_broadcast(128), "setup", whole=True)
    dma("sp", GSUB[:], subln_g0.partition_broadcast(128), "setup", whole=True)
    dma("sp", SGW, sgu_w0.rearrange("g p q -> p g q"), "setup", whole=True)
    dma("sp", BSGU[:], sgu_b0.rearrange("g p -> (g p)").partition_broadcast(1), "setup", whole=True)


    memset("dve", ONER[:], 1.0)
    memset("dve", NEGC[:], 0.0)
    memset("dve", ZSAVE[:], 0.0)
    memset("dve", VV[:].rearrange("p k (h c) -> p k h c", h=4)[:, :, :, 128:129], 1.0)
    cp("dve", IDB[:], IDF[:])
    memset("dve", BLK1[:], 0.0)
    memset("dve", ONESB[:], 1.0 / 1024.0)
    memset("dve", BLK1[0:64, 0:64], 1.0)
    memset("dve", BLK1[64:128, 64:128], 1.0)

    b0 = nb()
    tr(b0[:, 0:96], VB1[0:96, :], IDF[0:96, 0:96])
    cp("dve", COLS1[:], b0[:, 0:96])
    b1 = nb()
    tr(b1[:, 0:104], VB2[0:104, :], IDF[0:104, 0:104])
    cp("dve", COLS2[:], b1[:, 0:104])

    def col2(name, c):
        o = vb2_off[name] + c
        return COLS2[:, o:o + 1]

    co = vb2_off["cvec"]
    for cd in range(2):
        act(SIL[:, :, cd], COLS2[:, co + cd * 8: co + cd * 8 + 8], AF.Silu)

    bw = nb()
    for g in range(4):
        tr(bw[:, g * 128:(g + 1) * 128], SGW[:, g, :], IDF[:])
    cp("dve", WST[:].rearrange("p g q -> p (g q)"), bw[:, 0:512])

    tt("dve", LAMT[:, 0, :], LAMT[:, 0, :], LAMT[:, 1, :], ALU.mult)
    tt("dve", LAMT[:, 2, :], LAMT[:, 2, :], LAMT[:, 3, :], ALU.mult)
    P.op("dve", lambda e: e.reduce_sum(LAMS[:, 0:1], LAMT[:, 0, :], AX.X), [LAMT[:, 0, :]], [LAMS[:, 0:1]])
    P.op("dve", lambda e: e.reduce_sum(LAMS[:, 1:2], LAMT[:, 2, :], AX.X), [LAMT[:, 2, :]], [LAMS[:, 1:2]])
    act(LAMS[:, 2:4], LAMS[:, 0:2], AF.Exp)
    tt("dve", LAMS[:, 4:5], LAMS[:, 3:4], LAMS[:, 2:3], ALU.subtract)
    ts("dve", LAMS[:, 5:6], LAMS[:, 4:5], -LAMBDA_INIT, None, ALU.add)
    NEGLAM = LAMS[:, 5:6]
    ts("dve", GSUB[:], GSUB[:], 1.0 - LAMBDA_INIT, None, ALU.mult)

    def modulation(layer, blocks, derive):
        bm = nb()
        for b in blocks:
            wb = wload_mod(layer, b)
            for j4 in range(4):
                j = b * 4 + j4
                for kc in range(8):
                    mm(bm[:, 2 * j:2 * j + 2], wb[:, kc, j4 * 128:(j4 + 1) * 128], SIL[:, kc, :],
                       start=(kc == 0), stop=(kc == 7))
        j0, j1 = blocks[0] * 4, blocks[-1] * 4 + 4
        bmv = bm[:, 0:96].rearrange("p (j c) -> p j c", c=2)
        for cd in range(2):
            tt("dve", MODT[:, layer, j0:j1, cd], bmv[:, j0:j1, cd], COLS1[:, layer * 48 + j0:layer * 48 + j1],
               ALU.add)
        for cd in range(2):
            M = lambda w: MODT[:, layer, w * 8:(w + 1) * 8, cd]
            Dv = lambda k: DCOL[:, layer, cd, k, :]
            gm = COLS2[:, vb2_off[f"ln_mix_g{layer}"]: vb2_off[f"ln_mix_g{layer}"] + 8]
            bmx = COLS2[:, vb2_off[f"ln_mix_b{layer}"]: vb2_off[f"ln_mix_b{layer}"] + 8]
            if "a" in derive:
                ts("dve", Dv(0), M(1), 1.0, None, ALU.add)
                cp("dve", Dv(1), M(0))
            if "b" in derive:
                ts("dve", Dv(2), M(2), 1.0 / ALPHA, None, ALU.mult)
                ts("dve", Dv(6), M(4), 1.0, None, ALU.add)
                tt("dve", Dv(3), gm, Dv(6), ALU.mult)
                tt("dve", Dv(4), bmx, Dv(6), ALU.mult)
                tt("dve", Dv(4), Dv(4), M(3), ALU.add)
                ts("dve", Dv(5), M(5), 1.0 / ALPHA, None, ALU.mult)

    def dc(layer, cd, k, c):
        return DCOL[:, layer, cd, k, c:c + 1]

    modulation(0, [0, 1, 2, 3], "a")
    cast_all(["w_in0"])

    xin_ctr = [0]

    def load_tile(src_rows, cd, layer, dstx, dsth, tt_):
        k = xin_ctr[0] % 2
        xin_ctr[0] += 1
        dma("sp", XIN[:, k, :], src_rows, semkey=f"xin{k}")
        for half in range(2):
            bk = nb()
            for cc in range(4):
                c = half * 4 + cc
                tr(bk[:, cc * 128:(cc + 1) * 128], XIN[:, k, c * 128:(c + 1) * 128], IDF[:])
            if dstx is not None and os.environ.get("DBG_NOX") != "1":
                if os.environ.get("DBG_NOX") == "2":
                    for cc in range(4):
                        cp("dve", dstx[:, half * 4 + cc, tt_ * 128:(tt_ + 1) * 128], bk[:, cc * 128:(cc + 1) * 128])
                else:
                    cp("dve", dstx[:, half * 4:half * 4 + 4, tt_ * 128:(tt_ + 1) * 128],
                       bk[:, 0:512].rearrange("p (c n) -> p c n", c=4))
            for cc in range(4):
                c = half * 4 + cc
                act(dsth[:, c, tt_ * 128:(tt_ + 1) * 128], bk[:, cc * 128:(cc + 1) * 128], AF.Identity,
                    bias=dc(layer, cd, 1, c), scale=dc(layer, cd, 0, c))

    def proj_fm(blk, m4, inT, N, bank):
        for kc in range(8):
            mm(bank[:, 0:N], blk[:, kc, m4 * 128:(m4 + 1) * 128], inT[:, kc, 0:N], start=(kc == 0), stop=(kc == 7))

    def proj_tm(blk, inT, tt_, bank):
        for kc in range(8):
            mm(bank[:, 0:512], inT[:, kc, tt_ * 128:(tt_ + 1) * 128], blk[:, kc, 0:512], start=(kc == 0),
               stop=(kc == 7))

    sq_ctr = [0]

    def norm_update(src, N, dstcol):
        k = sq_ctr[0] % 2
        sq_ctr[0] += 1
        act(SQB[:, k, 0:N], src, AF.Square)
        bkn = nb()
        mm(bkn[:, 0:N], BLK1[:], SQB[:, k, 0:N])
        P.op("dve", lambda e: e.reduce_max(dstcol, bkn[:, 0:N], AX.X), [bkn[:, 0:N]], [dstcol])

    def cols_max(cols, dst):
        P.op("dve", lambda e: e.reduce_max(dst, cols, AX.X), [cols], [dst])

    def bound_finalize():
        for j in range(2):
            bkt = nb()
            tr(bkt[0:1, 0:128], MAXC[:, j:j + 1], IDF[:])
            P.op("dve", (lambda e, bkt=bkt, j=j: e.reduce_max(S1[0:1, j:j + 1], bkt[0:1, 0:128], AX.X)),
                 [bkt[0:1, 0:128]], [S1[0:1, j:j + 1]])
        tt("dve", S1[0:1, 2:3], S1[0:1, 0:1], S1[0:1, 1:2], ALU.mult)
        act(S1[0:1, 3:4], S1[0:1, 2:3], AF.Ln)
        act(S1[0:1, 4:5], S1[0:1, 3:4], AF.Exp, scale=0.5)
        ts("dve", S1[0:1, 5:6], S1[0:1, 4:5], -1.01 / 8.0, None, ALU.mult)
        ts("dve", S1[0:1, 6:7], S1[0:1, 4:5], -1.01 / 8.0, None, ALU.mult)

    def bound_broadcast():
        bkb = nb()
        mm(bkb[:, 0:2], ONER[0:1, :], S1[0:1, 5:7])
        cp("dve", NEGC[:, 0:1], bkb[:, 0:1])

    qr_ctr = [0]

    def rope(bank, N, dst, normcol=None):
        k = qr_ctr[0] % 2
        qr_ctr[0] += 1
        cp("act", QRAW[:, k, 0:N], bank[:, 0:N])
        if normcol is not None:
            norm_update(QRAW[:, k, 0:N], N, normcol)
        b2 = nb()
        mm(b2[:, 0:N], PERM[:], QRAW[:, k, 0:N])
        tt("dve", T1[:, k, 0:N], QRAW[:, k, 0:N], TAB[:, 0, 0:N], ALU.mult)
        tt("dve", T3[:, k, 0:N], b2[:, 0:N], TAB[:, 1, 0:N], ALU.mult)
        tt("dve", dst, T1[:, k, 0:N], T3[:, k, 0:N], ALU.add)

    def sgu_norm_tile(bank, tt_):
        k = tt_ % 2
        for gi in range(4):
            P.op("dve", (lambda e, gi=gi: e.bn_stats(ST[:, k, gi, :], bank[:, gi * 128:(gi + 1) * 128])),
                 [bank[:, gi * 128:(gi + 1) * 128]], [ST[:, k, gi, :]])
        for gi in range(4):
            P.op("dve", (lambda e, gi=gi: e.bn_aggr(MV[:, k, gi, :], ST[:, k, gi, :])), [ST[:, k, gi, :]],
                 [MV[:, k, gi, :]])
        act(RS[:, k, 0:4], MV[:, k, :, 1], AF.Ln, bias=LN_EPS)
        act(RS[:, k, 4:8], RS[:, k, 0:4], AF.Exp, scale=-0.5)
        for gi in range(4):
            ts("dve", VC[:, tt_, gi * 128:(gi + 1) * 128], bank[:, gi * 128:(gi + 1) * 128],
               MV[:, k, gi, 0:1], RS[:, k, 4 + gi:5 + gi], ALU.subtract, ALU.mult)

    SBK = [[BANK[0], BANK[1]], [BANK[2], BANK[3]]]
    OBK = [BANK[4], BANK[5], BANK[6]]
    MISCB = BANK[7].bitcast(BF16)
    SPAIR = [PS01, PS23]

    deferred = []
    T2f = T2[:].rearrange("p a n -> p (a n)")
    T3f = T3[:].rearrange("p a n -> p (a n)")
    AEP = T3[:, 1, :].rearrange("p (s n) -> p s n", s=4)

    def osacc(j):
        return T2f[:, j * 129:(j + 1) * 129] if j < 6 else T3f[:, (j - 6) * 129:(j - 5) * 129]

    def run_deferred(n=None):
        k = 0
        while deferred and (n is None or k < n):
            deferred.pop(0)()
            k += 1

    def attention(qT, q0, Nq, ktiles, cat):
        nsub = Nq // 128
        nacc = 2 * nsub
        nk_ = len(ktiles)
        hooks = {4, 9, 14, 19, 24}
        for h in range(4):
            accs = {}
            first_in_bank = {}
            for sub in range(nsub):
                for i in range(2):
                    j = sub * 2 + i
                    bkk = OBK[j // 3]
                    accs[(sub, i)] = bkk[:, (j % 3) * 129:(j % 3) * 129 + 129]
                    first_in_bank[(sub, i)] = (j % 3 == 0)
            for kt in range(nk_ + 1):
                if kt < nk_:
                    Kh = ktiles[kt][0](h)
                    for i in range(2):
                        sbank = SBK[kt % 2][i]
                        mm(sbank[:, 0:Nq], Kh[i * 64:(i + 1) * 64, :], qT[i * 64:(i + 1) * 64, h, q0:q0 + Nq])
                    act(ET[:, (kt % 2) * 2:(kt % 2) * 2 + 2, 0:Nq],
                        SPAIR[kt % 2][:, :].rearrange("p (i n) -> p i n", i=2)[:, :, 0:Nq],
                        AF.Exp, bias=NEGC[:, 0:1], scale=0.125)
                if kt >= 1:
                    k1 = kt - 1
                    Vh = ktiles[k1][1](h)
                    for sub in range(nsub):
                        for i in range(2):
                            mm(accs[(sub, i)], ET[:, (k1 % 2) * 2 + i, sub * 128:(sub + 1) * 128], Vh,
                               start=(k1 == 0 and first_in_bank[(sub, i)]), stop=(k1 == nk_ - 1),
                               skip_group_check=True)
                if kt in hooks:
                    run_deferred(1)
            run_deferred()
            for b_ in range((nacc + 2) // 3):
                n_in = min(3, nacc - 3 * b_)
                dst = T2f[:, b_ * 387:b_ * 387 + n_in * 129] if b_ < 2 else T3f[:, 0:n_in * 129]
                cp("dve", dst, OBK[b_][:, 0:n_in * 129])
            p_ = h % 2
            R = SM[:, p_ * 2, :]
            SS = SM[:, p_ * 2 + 1, 0:4]
            LNV = SM[:, 4 + p_, 0:4]
            RSTD = SM[:, 4 + p_, 4:8]

            def stage1(nsub=nsub, nacc=nacc, R=R, SS=SS):
                n6 = min(nacc, 6)
                lcol = T2f[:, 0:n6 * 129].rearrange("p (j c) -> p j c", c=129)[:, :, 128:129]
                rout = R[:, 0:n6].rearrange("p (j o) -> p j o", o=1)
                P.op("dve", lambda e: e.reciprocal(rout, lcol), [lcol], [rout])
                if nacc > 6:
                    lcol2 = T3f[:, 0:258].rearrange("p (j c) -> p j c", c=129)[:, :, 128:129]
                    rout2 = R[:, 6:8].rearrange("p (j o) -> p j o", o=1)
                    P.op("dve", lambda e: e.reciprocal(rout2, lcol2), [lcol2], [rout2])
                Rv = R[:, 0:nacc].rearrange("p (s i) -> p s i", i=2)
                ts("dve", Rv[:, :, 1], Rv[:, :, 1], NEGLAM, None, ALU.mult)
                for sub in range(nsub):
                    ts("dve", AEP[:, sub, :], osacc(2 * sub)[:, 0:128], R[:, 2 * sub:2 * sub + 1], None, ALU.mult)
                    stt(AEP[:, sub, :], osacc(2 * sub + 1)[:, 0:128], R[:, 2 * sub + 1:2 * sub + 2], AEP[:, sub, :],
                        ALU.mult, ALU.add)
                    P.op("dve", (lambda e, sub=sub: e.scalar_tensor_tensor(JUNK[:], AEP[:, sub, :], 1.0, AEP[:, sub, :],
                                                                           ALU.mult, ALU.mult, accum_out=SS[:, sub:sub + 1])),
                         [AEP[:, sub, :]], [JUNK[:], SS[:, sub:sub + 1]])

            def stage2(nsub=nsub, SS=SS, LNV=LNV, RSTD=RSTD):
                act(LNV[:, 0:nsub], SS[:, 0:nsub], AF.Ln, bias=LN_EPS, scale=1.0 / 128.0)
                act(RSTD[:, 0:nsub], LNV[:, 0:nsub], AF.Exp, scale=-0.5)

            def stage3(nsub=nsub, RSTD=RSTD):
                for sub in range(nsub):
                    stt(AN[:, sub, :], AEP[:, sub, :], RSTD[:, sub:sub + 1], GSUB[:], ALU.mult, ALU.mult)

            def stage4(nsub=nsub):
                for sub in range(nsub):
                    tr(MISCB[:, sub * 128:(sub + 1) * 128], AN[:, sub, :], IDB[:])

            def stage5(h=h, q0=q0, Nq=Nq, cat=cat):
                cp("dve", cat[:, h, q0:q0 + Nq], MISCB[:, 0:Nq])

            deferred.extend([stage1, stage2, stage3, stage4, stage5])

    def sgu_mix(N, cat):
        ntile = N // 128
        for gi in range(4):
            bk = nb()
            for t_ in range(ntile):
                mm(bk[:, t_ * 128:(t_ + 1) * 128], VC[:, t_, gi * 128:(gi + 1) * 128], WST[:, gi, :],
                   start=True, stop=False)
                mm(bk[:, t_ * 128:(t_ + 1) * 128], ONER[0:1, :], BSGU[0:1, gi * 128:(gi + 1) * 128],
                   start=False, stop=True)
            tt("dve", cat[:, 4 + gi, 0:N], bk[:, 0:N], UT[:, gi, 0:N], ALU.mult)

    pending_release = []

    class StatAcc:
        def __init__(self, XS, N):
            self.XS, self.N = XS, N
            while pending_release:
                pending_release.pop().release()
            self.im, self.ie = nb_reserve(), nb_reserve()
            self.Bm, self.Be = BANK[self.im], BANK[self.ie]
            self.pending = []
            self.n = 0

        def add(self, c):
            self.pending.append(c)
            if len(self.pending) > 1:
                self._emit(self.pending.pop(0))

        def _emit(self, c):
            N, XS = self.N, self.XS
            k = self.n % 2
            act(SQB[:, k, 0:N], XS[:, c, 0:N], AF.Square)
            cp("act", RB[:, k, 0:N], XS[:, c, 0:N])
            mm(self.Bm[:, 0:N], ONESB[:], RB[:, k, 0:N], start=(self.n == 0), stop=(self.n == 7))
            mm(self.Be[:, 0:N], ONESB[:], SQB[:, k, 0:N], start=(self.n == 0), stop=(self.n == 7))
            self.n += 1

        def finish(self):
            while self.pending:
                self._emit(self.pending.pop(0))
            assert self.n == 8

        def release(self):
            reserved.discard(self.im)
            reserved.discard(self.ie)

    def out_proj(name, cat, XS, N, layer, cd):
        st = StatAcc(XS, N)
        for ob in range(2):
            blk = wload(name, ob)
            for m4 in range(4):
                m = ob * 4 + m4
                bk = nb()
                proj_fm(blk, m4, cat, N, bk)
                stt(XS[:, m, 0:N], bk[:, 0:N], dc(layer, cd, 2, m), XS[:, m, 0:N], ALU.mult, ALU.add)
                st.add(m)
        return st

    def layernorm(XS, N, layer, cd, which, want_h, st):
        gname = f"ln_{which}_g{layer}"
        bname = f"ln_{which}_b{layer}"
        st.finish()
        Bm, Be = st.Bm, st.Be
        cp("act", T2[:, 0, 0:N], Bm[:, 0:N])
        tt("dve", T2[:, 1, 0:N], T2[:, 0, 0:N], Bm[:, 0:N], ALU.mult)
        tt("dve", T2[:, 1, 0:N], Be[:, 0:N], T2[:, 1, 0:N], ALU.subtract)
        act(T2[:, 1, 0:N], T2[:, 1, 0:N], AF.Ln, bias=EPS_R)
        act(Be[:, 0:N], T2[:, 1, 0:N], AF.Exp, scale=-0.5)
        for c in range(8):
            tt("dve", XS[:, c, 0:N], XS[:, c, 0:N], Bm[:, 0:N], ALU.subtract)
            tt("dve", XS[:, c, 0:N], XS[:, c, 0:N], Be[:, 0:N], ALU.mult)
            if want_h:
                act(HT[:, c, 0:N], XS[:, c, 0:N], AF.Identity, bias=dc(layer, cd, 4, c), scale=dc(layer, cd, 3, c))
            act(XS[:, c, 0:N], XS[:, c, 0:N], AF.Identity, bias=col2(bname, c), scale=col2(gname, c))
        pending_release.append(st)

    t_ctr = [0]

    def ffn(XS, N, layer, cd):
        n1, n2 = f"w_ff1_{layer}", f"w_ff2_{layer}"
        st = None
        for hf in range(2):
            if hf == 1:
                st = StatAcc(XS, N)
            for j4 in range(4):
                blk = wload(n1, hf * 4 + j4)
                first = (hf == 0 and j4 == 0)
                if first:
                    bks = [nb() for _ in range(4)]
                    for kc in range(8):
                        for m4 in range(4):
                            mm(bks[m4][:, 0:N], blk[:, kc, m4 * 128:(m4 + 1) * 128], HT[:, kc, 0:N],
                               start=(kc == 0), stop=(kc == 7))
                for m4 in range(4):
                    if first:
                        bk = bks[m4]
                    else:
                        bk = nb()
                        proj_fm(blk, m4, HT, N, bk)
                    k = t_ctr[0] % 2
                    t_ctr[0] += 1
                    cp("act", T1[:, k, 0:N], bk[:, 0:N])
                    stt(HID[:, j4 * 4 + m4, 0:N], bk[:, 0:N], 0.0, T1[:, k, 0:N], ALU.max, ALU.mult)
            for mb in range(4):
                blk = wload(n2, hf * 4 + mb)
                for m2 in range(2):
                    m = mb * 2 + m2
                    bk = nb()
                    for kc in range(16):
                        mm(bk[:, 0:N], blk[:, kc, m2 * 128:(m2 + 1) * 128], HID[:, kc, 0:N], start=(kc == 0),
                           stop=(kc == 15))
                    stt(XS[:, m, 0:N], bk[:, 0:N], dc(layer, cd, 5, m), XS[:, m, 0:N], ALU.mult, ALU.add)
                    if hf == 1:
                        st.add(m)
        return st

    stg_ctr = [0]

    def store_tiles(XS, ntile, dst_rows_fn):
        for t_ in range(ntile):
            k = stg_ctr[0] % 2
            stg_ctr[0] += 1
            for half in range(2):
                bk = nb()
                for cc in range(4):
                    tr(bk[:, cc * 128:(cc + 1) * 128], XS[:, half * 4 + cc, t_ * 128:(t_ + 1) * 128], IDF[:])
                cp("act" if half == 0 else "dve", STG[:, k, half * 512:(half + 1) * 512], bk[:, 0:512])
            dma("pool", dst_rows_fn(t_), STG[:, k, :], semkey=f"stg{k}")

    def layer0_group(src_fn, ntile, cd, slot, sample, tab0, ktiles_fn, kvout_fn, pre_ln2=None):
        N = ntile * 128
        XS = XR[:, slot]
        for t_ in range(ntile):
            load_tile(src_fn(t_), cd, 0, XS, HT, t_)
        if sample:
            dma("sp", TAB[:, 0, 0:N], cosT[:, tab0:tab0 + N], semkey="tabc")
            dma("sp", TAB[:, 1, 0:N], sinT[:, tab0:tab0 + N], semkey="tabs")
        if stop < 3.02:
            return
        blk = wload("w_in0", 0)
        for m4 in range(4):
            bk = nb()
            proj_fm(blk, m4, HT, N, bk)
            if sample:
                rope(bk, N, QT[:, m4, 0:N], QCOLS[:, m4:m4 + 1])
            else:
                cp("act", QT[:, m4, 0:N], bk[:, 0:N])
                norm_update(QT[:, m4, 0:N], N, QCOLS[:, m4:m4 + 1])
        cols_max(QCOLS[:, 0:4], MAXC[:, 0:1])
        if not sample:
            blk = wload("w_in0", 1)
            for m4 in range(4):
                bk = nb()
                proj_fm(blk, m4, HT, N, bk)
                cp("act", KTP[:, m4, 0:N], bk[:, 0:N])
                norm_update(KTP[:, m4, 0:N], N, KCOLS[:, m4:m4 + 1])
            cols_max(KCOLS[:, 0:4], MAXC[:, 1:2])
            blkv = wload("w_in0", 2)
            for t_ in range(ntile):
                k = stg_ctr[0] % 2
                stg_ctr[0] += 1
                bk = nb()
                proj_tm(blk, HT, t_, bk)
                cp("dve", STG[:, k, 0:512], bk[:, 0:512])
                bk2 = nb()
                proj_tm(blkv, HT, t_, bk2)
                cp("act", STG[:, k, 512:1024], bk2[:, 0:512])
                cp("dve", VV[:, t_, :].rearrange("p (h c) -> p h c", h=4)[:, :, 0:128],
                   bk2[:, 0:512].rearrange("p (h c) -> p h c", h=4))
                kd, vd_ = kvout_fn(t_)
                dma("pool", kd, STG[:, k, 0:512], semkey=f"stg{k}")
                dma("pool", vd_, STG[:, k, 512:1024], semkey=f"stg{k}")
        if stop < 3.03:
            return
        blk = wload("w_in0", 3)
        for m4 in range(4):
            bk = nb()
            proj_fm(blk, m4, HT, N, bk)
            cp("act" if m4 % 2 == 0 else "dve", UT[:, m4, 0:N], bk[:, 0:N])
        bound_finalize()
        if stop < 3.04:
            return
        blk = wload("w_in0", 4)
        for t_ in range(ntile):
            bk = nb()
            proj_tm(blk, HT, t_, bk)
            if stop >= 3.05:
                sgu_norm_tile(bk, t_)
        if stop < 3.2:
            return
        bound_broadcast()
        for (q0, Nq, kts) in ktiles_fn(N):
            attention(QT, q0, Nq, kts, HT)
        run_deferred()
        if stop < 3.3:
            return
        sgu_mix(N, HT)
        if stop < 3.4:
            return
        st = out_proj("w_out0", HT, XS, N, 0, cd)
        layernorm(XS, N, 0, cd, "mix", True, st)
        st = ffn(XS, N, 0, cd)
        if pre_ln2 is not None:
            pre_ln2()
        layernorm(XS, N, 0, cd, "ff", False, st)

    def l1_prologue(ntile, cd, slot):
        N = ntile * 128
        XS = XR[:, slot]
        for c in range(8):
            act(HT[:, c, 0:N], XS[:, c, 0:N], AF.Identity, bias=dc(1, cd, 1, c), scale=dc(1, cd, 0, c))

    def layer1_group(ntile, cd, slot, segs, xnext, use_prev, dst_rows_fn, skip_prologue=False):
        N = ntile * 128
        XS = XR[:, slot]
        if not skip_prologue:
            l1_prologue(ntile, cd, slot)
        if xnext is not None:
            for c in range(8):
                act(HX[:, c, 0:1], xnext[:, c, 0:1], AF.Identity, bias=dc(1, cd, 1, c), scale=dc(1, cd, 0, c))
        for half in range(2):
            bgb = wload("w_in1", 0 + half)
            cgb = wload("w_in1", 2 + half)
            xtb = wload("w_in1", 4 + half)
            for m4 in range(4):
                m = half * 4 + m4
                k = m % 2
                bC, bX, bB = nb(), nb(), nb()
                if half == 0 and m4 == 0:
                    for kc in range(8):
                        for (wb_, bk_) in ((cgb, bC), (xtb, bX), (bgb, bB)):
                            mm(bk_[:, 0:N], wb_[:, kc, 0:128], HT[:, kc, 0:N], start=(kc == 0), stop=(kc == 7))
                else:
                    proj_fm(cgb, m4, HT, N, bC)
                    proj_fm(xtb, m4, HT, N, bX)
                    proj_fm(bgb, m4, HT, N, bB)
                cp("act", T1[:, k, 0:N], bC[:, 0:N])
                Z = T3[:, k, 0:N]
                C = T2[:, k, 0:N]
                tt("dve", Z, bX[:, 0:N], T1[:, k, 0:N], ALU.mult)
                ts("dve", C, Z, col2("conv", 8 + m), None, ALU.mult)
                for (a, b) in segs:
                    stt(C[:, a + 1:b], Z[:, a:b - 1], col2("conv", 0 + m), C[:, a + 1:b], ALU.mult, ALU.add)
                    stt(C[:, a:b - 1], Z[:, a + 1:b], col2("conv", 16 + m), C[:, a:b - 1], ALU.mult, ALU.add)
                if use_prev:
                    stt(C[:, 0:1], ZSAVE[:, m:m + 1], col2("conv", 0 + m), C[:, 0:1], ALU.mult, ALU.add)
                    cp("dve", ZSAVE[:, m:m + 1], Z[:, N - 1:N])
                if xnext is not None:
                    bH = nb()
                    for kc in range(8):
                        mm(bH[:, 0:1], cgb[:, kc, m4 * 128:(m4 + 1) * 128], HX[:, kc, 0:1], start=(kc == 0),
                           stop=(kc == 7))
                    for kc in range(8):
                        mm(bH[:, 2:3], xtb[:, kc, m4 * 128:(m4 + 1) * 128], HX[:, kc, 0:1], start=(kc == 0),
                           stop=(kc == 7), skip_group_check=True)
                    cp("act", SMZ[:, m, 0:1], bH[:, 0:1])
                    tt("dve", SMZ[:, m, 1:2], bH[:, 2:3], SMZ[:, m, 0:1], ALU.mult)
                    stt(C[:, N - 1:N], SMZ[:, m, 1:2], col2("conv", 16 + m), C[:, N - 1:N], ALU.mult, ALU.add)
                tt("dve", CAT1[:, m, 0:N], bB[:, 0:N], C, ALU.mult)
        st = out_proj("w_out1", CAT1, XS, N, 1, cd)
        layernorm(XS, N, 1, cd, "mix", True, st)
        st = ffn(XS, N, 1, cd)
        layernorm(XS, N, 1, cd, "ff", False, st)
        store_tiles(XS, ntile, dst_rows_fn)

    HTB = [HT, CAT1]

    def kv_load(g, t_):
        tile_i = g * 4 + t_
        load_tile(xs[tile_i * 128:(tile_i + 1) * 128, :], 0, 0, None, HTB[g % 2], t_)

    def kv_tab(g):
        dma("sp", TAB[:, 0, :], cosT[:, g * 512:(g + 1) * 512], semkey="tabc")
        dma("sp", TAB[:, 1, :], sinT[:, g * 512:(g + 1) * 512], semkey="tabs")

    NKV = 8 if stop >= 2 else 0
    if NKV:
        for t_ in range(4):
            kv_load(0, t_)
        kv_tab(0)
    if NKV:
        blk_k = wload("w_in0", 1)
        blk_v = wload("w_in0", 2)
    for kvg in range(NKV):
        H = HTB[kvg % 2]
        blk = blk_k
        for m4 in range(4):
            bk = nb()
            proj_fm(blk, m4, H, 512, bk)
            rope(bk, 512, KT[:, m4, kvg * 512:(kvg + 1) * 512], KCOLS[:, kvg * 4 + m4:kvg * 4 + m4 + 1])
            if kvg + 1 < NKV:
                kv_load(kvg + 1, m4)
        if kvg + 1 < NKV:
            kv_tab(kvg + 1)
        blk = blk_v
        for t_ in range(4):
            bk = nb()
            proj_tm(blk, H, t_, bk)
            cp("act" if t_ % 2 == 0 else "dve",
               VV[:, kvg * 4 + t_, :].rearrange("p (h c) -> p h c", h=4)[:, :, 0:128],
               bk[:, 0:512].rearrange("p (h c) -> p h c", h=4))
    for j in range(2 if stop >= 2 else 0):
        k = xin_ctr[0] % 2
        xin_ctr[0] += 1
        dma("sp", XIN[:, k, 0:512], ck[j * 128:(j + 1) * 128, :], semkey=f"xin{k}")
        bk = nb()
        for m4 in range(4):
            tr(bk[:, m4 * 128:(m4 + 1) * 128], XIN[:, k, m4 * 128:(m4 + 1) * 128], IDF[:])
        cp("dve", KT[:, :, (32 + j) * 128:(33 + j) * 128], bk[:, 0:512].rearrange("p (c n) -> p c n", c=4))
        for m4 in range(4):
            norm_update(KT[:, m4, (32 + j) * 128:(33 + j) * 128], 128, KCOLS[:, 32 + j * 4 + m4:33 + j * 4 + m4])
        dma("pool", VV[:, 32 + j, :].rearrange("p (h c) -> p h c", h=4)[:, :, 0:128],
            cv[j * 128:(j + 1) * 128, :].rearrange("p (h c) -> p h c", h=4), semkey=f"cvld{j}")

    if stop >= 2:
        cols_max(KCOLS[:, 0:40], MAXC[:, 1:2])
    modulation(0, [4, 5, 6, 7, 8, 9, 10, 11], "b")
    cast_all(["w_out0", "w_ff1_0", "w_ff2_0"])

    def sample_ktiles(N):
        kts = []
        for kt in range(34):
            kts.append(((lambda h, kt=kt: KT[:, h, kt * 128:(kt + 1) * 128]),
                        (lambda h, kt=kt: VV[:, kt, h * 129:(h + 1) * 129])))
        return [(0, N, kts)]

    groups = [(0, 4), (4, 4), (8, 4), (12, 3), (15, 2)]

    def s_l0(g):
        t0, nt = groups[g]
        hook = None
        if g >= 2:
            hook = lambda: l1_prologue(groups[g - 1][1], 0, (g - 1) % 2)
        layer0_group(lambda t_: xs[(t0 + t_) * 128:(t0 + t_ + 1) * 128, :], nt, 0, g % 2, True, t0 * 128,
                     sample_ktiles, None, pre_ln2=hook)

    def s_l1(g):
        t0, nt = groups[g]
        xnext = XR[:, (g + 1) % 2] if g + 1 < len(groups) else None
        hoisted = (g >= 1 and g + 1 < len(groups))
        layer1_group(nt, 0, g % 2, [(0, nt * 128)], xnext, True,
                     lambda t_: ys[(t0 + t_) * 128:(t0 + t_ + 1) * 128, :], skip_prologue=hoisted)

    if stop >= 3:
        s_l0(0)
    cast_all(["w_in1", "w_out1", "w_ff1_1", "w_ff2_1"])
    if stop >= 4:
        for g in range(1, len(groups)):
            s_l0(g)
            if g == 1:
                modulation(1, list(range(12)), "ab")
            s_l1(g - 1)
        s_l1(len(groups) - 1)

    def prompt_ktiles(N):
        res = []
        for bi in range(2):
            kts = []
            for j in range(2):
                kt = bi * 2 + j
                kts.append(((lambda h, kt=kt: KTP[:, h, kt * 128:(kt + 1) * 128]),
                            (lambda h, kt=kt: VV[:, kt, h * 129:(h + 1) * 129])))
            res.append((bi * 256, 256, kts))
        return res

    for pg in range(2 if stop >= 5 else 0):
        r0 = pg * 512
        layer0_group(lambda t_: xp[r0 + t_ * 128: r0 + (t_ + 1) * 128, :], 4, 1, 0, False, 0, prompt_ktiles,
                     lambda t_: (nk[r0 + t_ * 128: r0 + (t_ + 1) * 128, :], nv[r0 + t_ * 128: r0 + (t_ + 1) * 128, :]))
        layer1_group(4, 1, 0, [(0, 256), (256, 512)], None, False,
                     lambda t_: yp[r0 + t_ * 128: r0 + (t_ + 1) * 128, :])

    ops = P.ops
    for o in ops:
        for d in o.deps:
            ops[d].has_dep = True
    ENG = ["pe", "act", "dve", "pool", "sp"]
    semkeys = sorted({o.semkey for o in ops if o.dma})
    sems = {}
    for e_ in ENG:
        sems[e_] = es.enter_context(nc.semaphore("s_" + e_))
    for k in semkeys:
        sems["d_" + k] = es.enter_context(nc.semaphore("d_" + k))
    ecount = {e_: 0 for e_ in ENG}
    dcount = {k: 0 for k in semkeys}
    for o in ops:
        if o.dma:
            dcount[o.semkey] += 16
            o.cnt = dcount[o.semkey]
        elif o.has_dep:
            ecount[o.eng] += 1
            o.cnt = ecount[o.eng]
    dtotal = dict(dcount)
    out_keys = [k for k in semkeys if k.startswith("stg")]

    block = es.enter_context(nc.Block())

    def emit_engine(ename, e):
        waited = {}
        for o in ops:
            if o.eng != ename:
                continue
            need = {}
            for d in o.deps:
                p = ops[d]
                if p.dma:
                    sk = "d_" + p.semkey
                    val = dtotal[p.semkey] if p.whole else p.cnt
                else:
                    if p.eng == "pe" and ename == "pe":
                        continue
                    sk = p.eng
                    val = p.cnt
                if val > need.get(sk, 0):
                    need[sk] = val
            for sk, val in need.items():
                if waited.get(sk, 0) >= val:
                    continue
                e.wait_ge(sems[sk], val)
                waited[sk] = val
            ins = o.fn(e)
            if o.dma:
                ins.then_inc(sems["d_" + o.semkey], 16)
            elif o.has_dep:
                ins.then_inc(sems[ename], 1)
        if ename == "pool":
            for k in out_keys:
                e.wait_ge(sems["d_" + k], dtotal[k])

    @block.tensor
    def _(e):
        emit_engine("pe", e)

    @block.scalar
    def _(e):
        emit_engine("act", e)

    @block.vector
    def _(e):
        emit_engine("dve", e)

    @block.gpsimd
    def _(e):
        emit_engine("pool", e)

    @block.sync
    def _(e):
        emit_engine("sp", e)

    es.close()
    return nc


_NC_CACHE = {}


def _rope_tables(order):
    pos = (np.asarray(order)[:, None] * 128 + np.arange(128)[None, :]).reshape(-1)
    row = (pos // 64).astype(np.float32)
    col = (pos % 64).astype(np.float32)
    inv = (1.0 / (np.float32(10000.0) ** (np.arange(16, dtype=np.float32) / np.float32(16)))).astype(np.float32)
    ang = [row[:, None] * inv[None, :], col[:, None] * inv[None, :]]
    cosT = np.zeros((128, 4096), np.float32)
    sinT = np.zeros((128, 4096), np.float32)
    for p in range(128):
        pm = p % 64
        s = pm // 32
        j = (pm % 32) // 16
        f = pm % 16
        cosT[p] = np.cos(ang[s][:, f])
        sinT[p] = np.sin(ang[s][:, f]) * (-1.0 if j == 0 else 1.0)
    return cosT, sinT


def kernel(**inp):
    f = lambda a: np.ascontiguousarray(np.asarray(a, dtype=np.float32))
    x_prompt, x_sample = f(inp["x_prompt"]), f(inp["x_sample"])
    cache_k0, cache_v0 = f(inp["cache_k0"]), f(inp["cache_v0"])
    c, c_ctx = f(inp["c"]), f(inp["c_ctx"])
    if "nc" not in _NC_CACHE:
        _NC_CACHE["nc"] = build_nc()
    nc = _NC_CACHE["nc"]
    ident = np.eye(128, dtype=np.float32)
    perm = np.zeros((128, 128), np.float32)
    for m in range(128):
        perm[m ^ 16, m] = 1.0
    shared = {"c_ident": ident, "c_perm": perm}
    for name, _, _ in W_SPECS:
        shared[name] = f(inp[name])
    for name in ("w_mod0", "w_mod1", "conv_w1", "subln_g0", "sgu_w0", "sgu_b0", "lambda_q1_0", "lambda_k1_0",
                 "lambda_q2_0", "lambda_k2_0"):
        shared[name] = f(inp[name])
    for name, _ in VEC_IN:
        shared[name] = f(inp[name])
    in_maps = []
    orders = []
    for core in range(8):
        b, half = core // 2, core % 2
        win = list(range(0, 17)) if half == 0 else list(range(15, 32))
        others = [t for t in range(32) if t not in win]
        order = win + others
        orders.append(order)
        xt = x_sample[b].reshape(32, 128, 1024)[order].reshape(4096, 1024)
        cosT, sinT = _rope_tables(order)
        m = dict(shared)
        m["xs"] = np.ascontiguousarray(xt)
        m["xp"] = np.ascontiguousarray(x_prompt[4 * core:4 * core + 4].reshape(1024, 1024))
        m["ck"] = np.ascontiguousarray(cache_k0[b].reshape(256, 512))
        m["cv"] = np.ascontiguousarray(cache_v0[b].reshape(256, 512))
        m["cvec"] = np.ascontiguousarray(np.stack([c[b], c_ctx], 0))
        m["cosT"] = cosT
        m["sinT"] = sinT
        in_maps.append(m)
    res = run_bass_kernel_spmd(nc, in_maps, core_ids=list(range(8)))
    y_prompt = np.zeros((32, 256, 1024), np.float32)
    y_sample = np.zeros((4, 4096, 1024), np.float32)
    new_k = np.zeros((32, 256, 4, 2, 64), np.float32)
    new_v = np.zeros((32, 256, 4, 128), np.float32)
    for core in range(8):
        r = res.results[core]
        b, half = core // 2, core % 2
        ysc = np.asarray(r["ys"])
        if half == 0:
            y_sample[b, 0:2048] = ysc[0:2048]
        else:
            y_sample[b, 2048:4096] = ysc[128:2176]
        y_prompt[4 * core:4 * core + 4] = np.asarray(r["yp"]).reshape(4, 256, 1024)
        new_k[4 * core:4 * core + 4] = np.asarray(r["nk"]).reshape(4, 256, 4, 2, 64)
        new_v[4 * core:4 * core + 4] = np.asarray(r["nv"]).reshape(4, 256, 4, 128)
    return (y_prompt, y_sample, new_k, new_v)
```

```python
import math
import os
from contextlib import ExitStack

import numpy as np
import concourse.bass as bass
import concourse.mybir as mybir
from concourse.bass_utils import run_bass_kernel_spmd

F32 = mybir.dt.float32
BF16 = mybir.dt.bfloat16
AF = mybir.ActivationFunctionType
ALU = mybir.AluOpType
AX = mybir.AxisListType

D = 1024
ALPHA = 4.0 ** 0.25
LAMBDA_INIT = 0.8 - 0.6 * math.exp(-0.3 * 0)
LN_EPS = 1e-5
EPS_R = LN_EPS / (ALPHA * ALPHA)
NWIN = 17
DSZ = {F32: 4, BF16: 2}


class Op:
    __slots__ = ("eng", "fn", "deps", "dma", "semkey", "whole", "has_dep", "cnt")

    def __init__(self, eng, fn, dma, semkey, whole):
        self.eng, self.fn, self.dma, self.semkey, self.whole = eng, fn, dma, semkey, whole
        self.deps = set()
        self.has_dep = False
        self.cnt = 0


def _region(ap):
    name = ap.tensor.name
    es = DSZ.get(ap.dtype, 4)
    dims = list(ap.ap)
    off = ap.offset
    if str(ap.space) == "DRAM":
        span = sum((c - 1) * abs(s) for s, c in dims)
        return (name, 0, 1, off * es, (off + span + 1) * es)
    ps, pc = dims[0]
    ps = max(ps, 1)
    p0 = off // ps
    f0 = off % ps
    span = sum((c - 1) * abs(s) for s, c in dims[1:])
    if str(ap.space) == "PSUM":
        return (name, 0, 128, (f0 * es) // 2048 * 2048, ((f0 + span + 1) * es + 2047) // 2048 * 2048)
    return (name, p0, p0 + pc, f0 * es, (f0 + span + 1) * es)


class Prog:
    def __init__(self):
        self.ops = []
        self.recs = {}

    def _rkey(self, idx):
        o = self.ops[idx]
        return (o.eng, o.semkey)

    def _touch(self, idx, ap, write):
        name, p0, p1, lo, hi = _region(ap)
        lst = self.recs.setdefault(name, [])
        deps = self.ops[idx].deps
        keep = []
        for r in lst:
            rp0, rp1, rlo, rhi, w, rd = r
            if rp1 <= p0 or p1 <= rp0 or rhi <= lo or hi <= rlo:
                keep.append(r)
                continue
            if w is not None:
                deps.add(w)
            if write:
                deps.update(rd.values())
                if p0 <= rp0 and rp1 <= p1 and lo <= rlo and rhi <= hi:
                    continue
            keep.append(r)
        if write:
            keep.append([p0, p1, lo, hi, idx, {}])
        else:
            done = False
            for r in keep:
                if r[0] <= p0 and p1 <= r[1] and r[2] <= lo and hi <= r[3]:
                    r[5][self._rkey(idx)] = idx
                    done = True
                    break
            if not done:
                keep.append([p0, p1, lo, hi, None, {self._rkey(idx): idx}])
        self.recs[name] = keep

    def op(self, eng, fn, reads=(), writes=(), dma=False, semkey=None, whole=False):
        idx = len(self.ops)
        self.ops.append(Op(eng, fn, dma, semkey, whole))
        for a in reads:
            if a is not None and not isinstance(a, (int, float)):
                self._touch(idx, a, str(a.space) == "PSUM")
        for a in writes:
            self._touch(idx, a, True)
        o = self.ops[idx]
        o.deps.discard(idx)
        return idx


W_SPECS = [
    ("w_in0", 1024, 2560), ("w_out0", 1024, 1024), ("w_ff1_0", 1024, 4096), ("w_ff2_0", 4096, 1024),
    ("w_in1", 1024, 3072), ("w_out1", 1024, 1024), ("w_ff1_1", 1024, 4096), ("w_ff2_1", 4096, 1024),
]
VEC_IN = [("b_mod0", 6144), ("b_mod1", 6144), ("ln_mix_g0", 1024), ("ln_mix_b0", 1024), ("ln_ff_g0", 1024),
          ("ln_ff_b0", 1024), ("ln_mix_g1", 1024), ("ln_mix_b1", 1024), ("ln_ff_g1", 1024), ("ln_ff_b1", 1024)]


def build_nc(stop=99):
    nc = bass.Bass("TRN2", target_bir_lowering=False)
    P = Prog()
    es = ExitStack()

    def din(name, shape, dt=F32):
        return nc.dram_tensor(name, list(shape), dt, kind="ExternalInput").ap()

    def dout(name, shape, dt=F32):
        return nc.dram_tensor(name, list(shape), dt, kind="ExternalOutput").ap()

    xs = din("xs", [4096, 1024])
    xp = din("xp", [1024, 1024])
    ck = din("ck", [256, 512])
    cv = din("cv", [256, 512])
    cvec = din("cvec", [2, 1024])
    cosT = din("cosT", [128, 4096])
    sinT = din("sinT", [128, 4096])
    c_ident = din("c_ident", [128, 128])
    c_perm = din("c_perm", [128, 128])
    wd = {}
    for name, K, ncol in W_SPECS:
        wd[name] = din(name, [K, ncol])
    wd["w_mod0"] = din("w_mod0", [1024, 6144])
    wd["w_mod1"] = din("w_mod1", [1024, 6144])
    vd = {n: din(n, [ln]) for n, ln in VEC_IN}
    conv_w1 = din("conv_w1", [3, 1024])
    lam_in = {n: din(n, [64]) for n in ("lambda_q1_0", "lambda_k1_0", "lambda_q2_0", "lambda_k2_0")}
    subln_g0 = din("subln_g0", [128])
    sgu_w0 = din("sgu_w0", [4, 128, 128])
    sgu_b0 = din("sgu_b0", [4, 128])

    ys = dout("ys", [NWIN * 128, 1024])
    yp = dout("yp", [1024, 1024])
    nk = dout("nk", [1024, 512])
    nv = dout("nv", [1024, 512])

    blk_of = {}
    nblk = 0
    for name, K, ncol in W_SPECS:
        if K == 1024:
            n = ncol // 512
        else:
            n = 8
        blk_of[name] = (nblk, n)
        nblk += n
    wscr = nc.dram_tensor("wscr", [nblk, 128, 4096], BF16, kind="Internal").ap()

    def sb(name, shape, dt):
        return es.enter_context(nc.sbuf_tensor(name, list(shape), dt))

    KT = sb("KT", [128, 4, 4352], BF16)
    VV = sb("VV", [128, 34, 516], BF16)
    XR = sb("XR", [128, 2, 8, 512], F32)
    XIN = sb("XIN", [128, 2, 1024], F32)
    STG = sb("STG", [128, 2, 1024], F32)
    HT = sb("HT", [128, 8, 512], BF16)
    HX = sb("HX", [128, 8, 2], BF16)
    TAB = sb("TAB", [128, 2, 512], F32)
    T1 = sb("T1", [128, 2, 512], F32)
    T2 = sb("T2", [128, 2, 512], F32)
    T3 = sb("T3", [128, 2, 512], F32)
    QRAW = T2
    RB = T1[:, 0, :].bitcast(BF16).rearrange("p (a n) -> p a n", a=2)
    ARENA = sb("ARENA", [128, 10240], BF16)
    WS = sb("WS", [128, 4, 4096], BF16)
    IDF = sb("IDF", [128, 128], F32)
    IDB = sb("IDB", [128, 128], BF16)
    PERM = sb("PERM", [128, 128], F32)
    WST = sb("WST", [128, 4, 128], BF16)
    GSUB = sb("GSUB", [128, 128], F32)
    VB1 = T2[:, 0, 0:128]
    VB2 = T2[:, 1, 0:128]
    COLS1 = sb("COLS1", [128, 96], F32)
    COLS2 = sb("COLS2", [128, 104], F32)
    SIL = sb("SIL", [128, 8, 2], BF16)
    MODT = sb("MODT", [128, 2, 48, 2], F32)
    NVD = 8
    DCOL = sb("DCOL", [128, 2, 2, NVD, 8], F32)
    BSGU = sb("BSGU", [1, 512], F32)
    ONER = sb("ONER", [1, 128], F32)
    LAMT = T3[:, 0, 0:256].rearrange("p (a b) -> p a b", a=4)
    LAMS = sb("LAMS", [128, 8], F32)
    NEGC = sb("NEGC", [128, 1], F32)
    JUNK = sb("JUNK", [128, 128], F32)
    AN = sb("AN", [128, 4, 128], BF16)
    SM = sb("SM", [128, 8, 8], F32)
    ST = sb("ST", [128, 2, 4, 6], F32)
    MV = sb("MV", [128, 2, 4, 2], F32)
    RS = sb("RS", [128, 2, 8], F32)
    ZSAVE = sb("ZSAVE", [128, 8], F32)
    SMZ = sb("SMZ", [128, 8, 2], F32)
    SGW = T1[:, 0, :].rearrange("p (a b) -> p a b", a=4)

    BLK1 = sb("BLK1", [128, 128], BF16)
    ONESB = sb("ONESB", [128, 128], BF16)
    SQB = sb("SQB", [128, 2, 512], BF16)
    KCOLS = sb("KCOLS", [128, 48], F32)
    QCOLS = sb("QCOLS", [128, 8], F32)
    MAXC = sb("MAXC", [128, 2], F32)
    S1 = sb("S1", [1, 16], F32)

    PS01 = es.enter_context(nc.psum_tensor("PS01", [128, 1024], F32))
    PS23 = es.enter_context(nc.psum_tensor("PS23", [128, 1024], F32))
    BANK = [PS01[:, 0:512], PS01[:, 512:1024], PS23[:, 0:512], PS23[:, 512:1024]]
    BANK += [es.enter_context(nc.psum_tensor(f"B{i}", [128, 512], F32))[:] for i in range(4, 8)]

    QT = ARENA[:, 0:2048].rearrange("p (c n) -> p c n", c=4)
    VC = ARENA[:, 2048:4096].rearrange("p (c n) -> p c n", c=4)
    CAT1 = ARENA[:, 0:4096].rearrange("p (c n) -> p c n", c=8)
    ET = ARENA[:, 4096:6144].rearrange("p (c n) -> p c n", c=4)
    UT = ARENA[:, 6144:10240].bitcast(F32).rearrange("p (c n) -> p c n", c=4)
    HID = ARENA[:, 0:8192].rearrange("p (c n) -> p c n", c=16)
    KTP = KT[:, :, 0:512]

    bank_ctr = [0]

    reserved = set()

    def nb():
        while (bank_ctr[0] % 8) in reserved:
            bank_ctr[0] += 1
        b = BANK[bank_ctr[0] % 8]
        bank_ctr[0] += 1
        return b

    def nb_reserve():
        while (bank_ctr[0] % 8) in reserved:
            bank_ctr[0] += 1
        i = bank_ctr[0] % 8
        bank_ctr[0] += 1
        reserved.add(i)
        return i

    def mm(out, lhsT, rhs, start=True, stop=True, **kw):
        P.op("pe", lambda e: e.matmul(out, lhsT, rhs, start=start, stop=stop, **kw), [lhsT, rhs], [out])

    def tr(out, in_, ident):
        P.op("pe", lambda e: e.transpose(out, in_, ident), [in_, ident], [out])

    def act(out, in_, func, bias=None, scale=None, accum=None):
        kw = {}
        if bias is not None:
            kw["bias"] = bias
        if scale is not None:
            kw["scale"] = scale
        if accum is not None:
            kw["accum_out"] = accum
        wr = [out] + ([accum] if accum is not None else [])
        P.op("act", lambda e: e.activation(out, in_, func, **kw), [in_, bias, scale], wr)

    def ts(eng, out, in0, s1, s2, op0, op1=None):
        if op1 is None:
            P.op(eng, lambda e: e.tensor_scalar(out, in0, s1, None, op0), [in0, s1], [out])
        else:
            P.op(eng, lambda e: e.tensor_scalar(out, in0, s1, s2, op0, op1), [in0, s1, s2], [out])

    def tt(eng, out, in0, in1, op):
        P.op(eng, lambda e: e.tensor_tensor(out, in0, in1, op), [in0, in1], [out])

    def stt(out, in0, scalar, in1, op0, op1):
        P.op("dve", lambda e: e.scalar_tensor_tensor(out, in0, scalar, in1, op0, op1), [in0, scalar, in1], [out])

    def cp(eng, out, in_):
        if eng == "act":
            P.op("act", lambda e: e.activation(out, in_, AF.Copy), [in_], [out])
        else:
            P.op(eng, lambda e: e.tensor_copy(out, in_), [in_], [out])

    def dma(q, out, in_, semkey, whole=False, **kw):
        P.op(q, lambda e: e.dma_start(out=out, in_=in_, **kw), [in_], [out], dma=True, semkey=semkey, whole=whole)

    def memset(eng, ap, val):
        P.op(eng, lambda e: e.memset(ap, val), [], [ap])

    def w_src(name, b):
        K = dict((n, k) for n, k, _ in W_SPECS)[name]
        w = wd[name]
        if K == 1024:
            return w[:, b * 512:(b + 1) * 512].rearrange("(kc p) n -> p kc n", p=128), 8, 512
        hf, mb = b // 4, b % 4
        return (w[hf * 2048:(hf + 1) * 2048, mb * 256:(mb + 1) * 256].rearrange("(kc p) n -> p kc n", p=128),
                16, 256)

    def scr_view(name, b):
        base, n = blk_of[name]
        _, kc, ncol = w_src(name, b)
        return wscr[base + b].rearrange("p (kc n) -> p kc n", kc=kc)

    def cast_all(names):
        for name in names:
            base, n = blk_of[name]
            for b in range(n):
                src, kc, ncol = w_src(name, b)
                dma("pool", scr_view(name, b), src, semkey="cast_" + name, whole=True)

    ws_ctr = [0]
    ws_pin = set()

    def ws_next():
        while (ws_ctr[0] % 4) in ws_pin:
            ws_ctr[0] += 1
        s_ = ws_ctr[0] % 4
        ws_ctr[0] += 1
        return s_

    def wload(name, b):
        s = ws_next()
        _, kc, ncol = w_src(name, b)
        dst = WS[:, s, :].rearrange("p (kc n) -> p kc n", kc=kc)
        dma("sp", dst, scr_view(name, b), semkey=f"ws{s}")
        return dst

    def wload_mod(layer, b):
        s = ws_next()
        dst = WS[:, s, :].rearrange("p (kc n) -> p kc n", kc=8)
        src = wd[f"w_mod{layer}"][:, b * 512:(b + 1) * 512].rearrange("(kc p) n -> p kc n", p=128)
        dma("pool", dst, src, semkey=f"wm{s}")
        return dst

    def vrows(ap1d, n):
        return ap1d.rearrange("(c p) -> c p", p=128)

    dma("sp", IDF[:], c_ident, "setup", whole=True)
    dma("sp", PERM[:], c_perm, "setup", whole=True)
    dma("sp", VB1[0:48, :], vrows(vd["b_mod0"], 48), "setup", whole=True)
    dma("sp", VB1[48:96, :], vrows(vd["b_mod1"], 48), "setup", whole=True)
    r = 0
    vb2_off = {}
    for l in range(2):
        for nm in ("ln_mix_g", "ln_mix_b", "ln_ff_g", "ln_ff_b"):
            dma("sp", VB2[r:r + 8, :], vrows(vd[f"{nm}{l}"], 8), "setup", whole=True)
            vb2_off[f"{nm}{l}"] = r
            r += 8
    dma("sp", VB2[r:r + 24, :], conv_w1.rearrange("t (c p) -> (t c) p", p=128), "setup", whole=True)
    vb2_off["conv"] = r
    r += 24
    dma("sp", VB2[r:r + 16, :], cvec.rearrange("b (c p) -> (b c) p", p=128), "setup", whole=True)
    vb2_off["cvec"] = r
    r += 16
    assert r == 104
    for i, n in enumerate(("lambda_q1_0", "lambda_k1_0", "lambda_q2_0", "lambda_k2_0")):
        dma("sp", LAMT[:, i, :], lam_in[n].partition_broadcast(128), "setup", whole=True)
    dma("sp", GSUB[:], subln_g0.partition_broadcast(128), "setup", whole=True)
    dma("sp", SGW, sgu_w0.rearrange("g p q -> p g q"), "setup", whole=True)
    dma("sp", BSGU[:], sgu_b0.rearrange("g p -> (g p)").partition_broadcast(1), "setup", whole=True)


    memset("dve", ONER[:], 1.0)
    memset("dve", NEGC[:], 0.0)
    memset("dve", ZSAVE[:], 0.0)
    memset("dve", VV[:].rearrange("p k (h c) -> p k h c", h=4)[:, :, :, 128:129], 1.0)
    cp("dve", IDB[:], IDF[:])
    memset("dve", BLK1[:], 0.0)
    memset("dve", ONESB[:], 1.0 / 1024.0)
    memset("dve", BLK1[0:64, 0:64], 1.0)
    memset("dve", BLK1[64:128, 64:128], 1.0)

    b0 = nb()
    tr(b0[:, 0:96], VB1[0:96, :], IDF[0:96, 0:96])
    cp("dve", COLS1[:], b0[:, 0:96])
    b1 = nb()
    tr(b1[:, 0:104], VB2[0:104, :], IDF[0:104, 0:104])
    cp("dve", COLS2[:], b1[:, 0:104])

    def col2(name, c):
        o = vb2_off[name] + c
        return COLS2[:, o:o + 1]

    co = vb2_off["cvec"]
    for cd in range(2):
        act(SIL[:, :, cd], COLS2[:, co + cd * 8: co + cd * 8 + 8], AF.Silu)

    bw = nb()
    for g in range(4):
        tr(bw[:, g * 128:(g + 1) * 128], SGW[:, g, :], IDF[:])
    cp("dve", WST[:].rearrange("p g q -> p (g q)"), bw[:, 0:512])

    tt("dve", LAMT[:, 0, :], LAMT[:, 0, :], LAMT[:, 1, :], ALU.mult)
    tt("dve", LAMT[:, 2, :], LAMT[:, 2, :], LAMT[:, 3, :], ALU.mult)
    P.op("dve", lambda e: e.reduce_sum(LAMS[:, 0:1], LAMT[:, 0, :], AX.X), [LAMT[:, 0, :]], [LAMS[:, 0:1]])
    P.op("dve", lambda e: e.reduce_sum(LAMS[:, 1:2], LAMT[:, 2, :], AX.X), [LAMT[:, 2, :]], [LAMS[:, 1:2]])
    act(LAMS[:, 2:4], LAMS[:, 0:2], AF.Exp)
    tt("dve", LAMS[:, 4:5], LAMS[:, 3:4], LAMS[:, 2:3], ALU.subtract)
    ts("dve", LAMS[:, 5:6], LAMS[:, 4:5], -LAMBDA_INIT, None, ALU.add)
    NEGLAM = LAMS[:, 5:6]
    ts("dve", GSUB[:], GSUB[:], 1.0 - LAMBDA_INIT, None, ALU.mult)

    def mod_blocks(layer, blocks, bm):
        for b in blocks:
            wb = wload_mod(layer, b)
            for j4 in range(4):
                j = b * 4 + j4
                for kc in range(8):
                    mm(bm[:, 2 * j:2 * j + 2], wb[:, kc, j4 * 128:(j4 + 1) * 128], SIL[:, kc, :],
                       start=(kc == 0), stop=(kc == 7))

    def modulation(layer, blocks, derive, bm=None):
        if bm is None:
            bm = nb()
            mod_blocks(layer, blocks, bm)
        j0, j1 = blocks[0] * 4, blocks[-1] * 4 + 4
        bmv = bm[:, 0:96].rearrange("p (j c) -> p j c", c=2)
        for cd in range(2):
            tt("dve", MODT[:, layer, j0:j1, cd], bmv[:, j0:j1, cd], COLS1[:, layer * 48 + j0:layer * 48 + j1],
               ALU.add)
        for cd in range(2):
            M = lambda w: MODT[:, layer, w * 8:(w + 1) * 8, cd]
            Dv = lambda k: DCOL[:, layer, cd, k, :]
            gm = COLS2[:, vb2_off[f"ln_mix_g{layer}"]: vb2_off[f"ln_mix_g{layer}"] + 8]
            bmx = COLS2[:, vb2_off[f"ln_mix_b{layer}"]: vb2_off[f"ln_mix_b{layer}"] + 8]
            if "a" in derive:
                ts("dve", Dv(0), M(1), 1.0, None, ALU.add)
                cp("dve", Dv(1), M(0))
            if "b" in derive:
                ts("dve", Dv(2), M(2), 1.0 / ALPHA, None, ALU.mult)
                ts("dve", Dv(6), M(4), 1.0, None, ALU.add)
                tt("dve", Dv(3), gm, Dv(6), ALU.mult)
                tt("dve", Dv(4), bmx, Dv(6), ALU.mult)
                tt("dve", Dv(4), Dv(4), M(3), ALU.add)
                ts("dve", Dv(5), M(5), 1.0 / ALPHA, None, ALU.mult)

    def dc(layer, cd, k, c):
        return DCOL[:, layer, cd, k, c:c + 1]

    modulation(0, [0, 1, 2, 3], "a")
    cast_all(["w_in0"])

    xin_ctr = [0]

    def load_tile(src_rows, cd, layer, dstx, dsth, tt_):
        k = xin_ctr[0] % 2
        xin_ctr[0] += 1
        dma("sp", XIN[:, k, :], src_rows, semkey=f"xin{k}")
        for half in range(2):
            bk = nb()
            for cc in range(4):
                c = half * 4 + cc
                tr(bk[:, cc * 128:(cc + 1) * 128], XIN[:, k, c * 128:(c + 1) * 128], IDF[:])
            if dstx is not None and os.environ.get("DBG_NOX") != "1":
                if os.environ.get("DBG_NOX") == "2":
                    for cc in range(4):
                        cp("dve", dstx[:, half * 4 + cc, tt_ * 128:(tt_ + 1) * 128], bk[:, cc * 128:(cc + 1) * 128])
                else:
                    cp("dve", dstx[:, half * 4:half * 4 + 4, tt_ * 128:(tt_ + 1) * 128],
                       bk[:, 0:512].rearrange("p (c n) -> p c n", c=4))
            for cc in range(4):
                c = half * 4 + cc
                act(dsth[:, c, tt_ * 128:(tt_ + 1) * 128], bk[:, cc * 128:(cc + 1) * 128], AF.Identity,
                    bias=dc(layer, cd, 1, c), scale=dc(layer, cd, 0, c))

    AXB = ARENA[:, 4096:8192].bitcast(F32).rearrange("p (a n) -> p a n", a=2)
    XINS = [XIN[:, 0, :], XIN[:, 1, :], AXB[:, 0, :], AXB[:, 1, :]]

    def load_group_dma(srcs):
        for t_, src in enumerate(srcs):
            dma("sp", XINS[t_], src, semkey=f"xin{t_}")

    def load_group_chunk(c, ntile, cd, layer, dstx, dsth):
        N = ntile * 128
        bk = nb()
        for t_ in range(ntile):
            tr(bk[:, t_ * 128:(t_ + 1) * 128], XINS[t_][:, c * 128:(c + 1) * 128], IDF[:])
        if dstx is not None:
            cp("dve", dstx[:, c, 0:N], bk[:, 0:N])
        act(dsth[:, c, 0:N], bk[:, 0:N], AF.Identity, bias=dc(layer, cd, 1, c), scale=dc(layer, cd, 0, c))

    def proj_fm(blk, m4, inT, N, bank):
        for kc in range(8):
            mm(bank[:, 0:N], blk[:, kc, m4 * 128:(m4 + 1) * 128], inT[:, kc, 0:N], start=(kc == 0), stop=(kc == 7))

    def proj_tm(blk, inT, tt_, bank):
        for kc in range(8):
            mm(bank[:, 0:512], inT[:, kc, tt_ * 128:(tt_ + 1) * 128], blk[:, kc, 0:512], start=(kc == 0),
               stop=(kc == 7))

    sq_ctr = [0]

    def norm_update(src, N, dstcol):
        k = sq_ctr[0] % 2
        sq_ctr[0] += 1
        act(SQB[:, k, 0:N], src, AF.Square)
        bkn = nb()
        mm(bkn[:, 0:N], BLK1[:], SQB[:, k, 0:N])
        P.op("dve", lambda e: e.reduce_max(dstcol, bkn[:, 0:N], AX.X), [bkn[:, 0:N]], [dstcol])

    def cols_max(cols, dst):
        P.op("dve", lambda e: e.reduce_max(dst, cols, AX.X), [cols], [dst])

    def bound_finalize():
        for j in range(2):
            bkt = nb()
            tr(bkt[0:1, 0:128], MAXC[:, j:j + 1], IDF[:])
            P.op("dve", (lambda e, bkt=bkt, j=j: e.reduce_max(S1[0:1, j:j + 1], bkt[0:1, 0:128], AX.X)),
                 [bkt[0:1, 0:128]], [S1[0:1, j:j + 1]])
        tt("dve", S1[0:1, 2:3], S1[0:1, 0:1], S1[0:1, 1:2], ALU.mult)
        act(S1[0:1, 3:4], S1[0:1, 2:3], AF.Ln)
        act(S1[0:1, 4:5], S1[0:1, 3:4], AF.Exp, scale=0.5)
        ts("dve", S1[0:1, 5:6], S1[0:1, 4:5], -1.01 / 8.0, None, ALU.mult)
        ts("dve", S1[0:1, 6:7], S1[0:1, 4:5], -1.01 / 8.0, None, ALU.mult)

    def bound_broadcast():
        bkb = nb()
        mm(bkb[:, 0:2], ONER[0:1, :], S1[0:1, 5:7])
        cp("dve", NEGC[:, 0:1], bkb[:, 0:1])

    qr_ctr = [0]

    def rope(bank, N, dst, normcol=None):
        k = qr_ctr[0] % 2
        qr_ctr[0] += 1
        cp("act", QRAW[:, k, 0:N], bank[:, 0:N])
        if normcol is not None:
            norm_update(QRAW[:, k, 0:N], N, normcol)
        b2 = nb()
        mm(b2[:, 0:N], PERM[:], QRAW[:, k, 0:N])
        tt("dve", T1[:, k, 0:N], QRAW[:, k, 0:N], TAB[:, 0, 0:N], ALU.mult)
        tt("dve", T3[:, k, 0:N], b2[:, 0:N], TAB[:, 1, 0:N], ALU.mult)
        tt("dve", dst, T1[:, k, 0:N], T3[:, k, 0:N], ALU.add)

    def sgu_norm_tile(bank, tt_):
        k = tt_ % 2
        for gi in range(4):
            P.op("dve", (lambda e, gi=gi: e.bn_stats(ST[:, k, gi, :], bank[:, gi * 128:(gi + 1) * 128])),
                 [bank[:, gi * 128:(gi + 1) * 128]], [ST[:, k, gi, :]])
        for gi in range(4):
            P.op("dve", (lambda e, gi=gi: e.bn_aggr(MV[:, k, gi, :], ST[:, k, gi, :])), [ST[:, k, gi, :]],
                 [MV[:, k, gi, :]])
        act(RS[:, k, 0:4], MV[:, k, :, 1], AF.Ln, bias=LN_EPS)
        act(RS[:, k, 4:8], RS[:, k, 0:4], AF.Exp, scale=-0.5)
        for gi in range(4):
            ts("dve", VC[:, tt_, gi * 128:(gi + 1) * 128], bank[:, gi * 128:(gi + 1) * 128],
               MV[:, k, gi, 0:1], RS[:, k, 4 + gi:5 + gi], ALU.subtract, ALU.mult)

    SBK = [[BANK[0], BANK[1]], [BANK[2], BANK[3]]]
    OBK = [BANK[4], BANK[5], BANK[6]]
    MISCB = BANK[7].bitcast(BF16)
    SPAIR = [PS01, PS23]

    deferred = []
    T2f = T2[:].rearrange("p a n -> p (a n)")
    T3f = T3[:].rearrange("p a n -> p (a n)")
    AEP = T3[:, 1, :].rearrange("p (s n) -> p s n", s=4)

    def osacc(j):
        return T2f[:, j * 129:(j + 1) * 129] if j < 6 else T3f[:, (j - 6) * 129:(j - 5) * 129]

    def run_deferred(n=None):
        k = 0
        while deferred and (n is None or k < n):
            deferred.pop(0)()
            k += 1

    def attention(qT, q0, Nq, ktiles, cat):
        nsub = Nq // 128
        nacc = 2 * nsub
        nk_ = len(ktiles)
        hooks = {4, 9, 14, 19, 24}
        for h in range(4):
            accs = {}
            first_in_bank = {}
            for sub in range(nsub):
                for i in range(2):
                    j = sub * 2 + i
                    bkk = OBK[j // 3]
                    accs[(sub, i)] = bkk[:, (j % 3) * 129:(j % 3) * 129 + 129]
                    first_in_bank[(sub, i)] = (j % 3 == 0)
            for kt in range(nk_ + 1):
                if kt < nk_:
                    Kh = ktiles[kt][0](h)
                    for i in range(2):
                        sbank = SBK[kt % 2][i]
                        mm(sbank[:, 0:Nq], Kh[i * 64:(i + 1) * 64, :], qT[i * 64:(i + 1) * 64, h, q0:q0 + Nq])
                    act(ET[:, (kt % 2) * 2:(kt % 2) * 2 + 2, 0:Nq],
                        SPAIR[kt % 2][:, :].rearrange("p (i n) -> p i n", i=2)[:, :, 0:Nq],
                        AF.Exp, bias=NEGC[:, 0:1], scale=0.125)
                if kt >= 1:
                    k1 = kt - 1
                    Vh = ktiles[k1][1](h)
                    for sub in range(nsub):
                        for i in range(2):
                            mm(accs[(sub, i)], ET[:, (k1 % 2) * 2 + i, sub * 128:(sub + 1) * 128], Vh,
                               start=(k1 == 0 and first_in_bank[(sub, i)]), stop=(k1 == nk_ - 1),
                               skip_group_check=True)
                if kt in hooks:
                    run_deferred(1)
            run_deferred()
            for b_ in range((nacc + 2) // 3):
                n_in = min(3, nacc - 3 * b_)
                dst = T2f[:, b_ * 387:b_ * 387 + n_in * 129] if b_ < 2 else T3f[:, 0:n_in * 129]
                cp("dve", dst, OBK[b_][:, 0:n_in * 129])
            p_ = h % 2
            R = SM[:, p_ * 2, :]
            SS = SM[:, p_ * 2 + 1, 0:4]
            LNV = SM[:, 4 + p_, 0:4]
            RSTD = SM[:, 4 + p_, 4:8]

            def stage1(nsub=nsub, nacc=nacc, R=R, SS=SS):
                n6 = min(nacc, 6)
                lcol = T2f[:, 0:n6 * 129].rearrange("p (j c) -> p j c", c=129)[:, :, 128:129]
                rout = R[:, 0:n6].rearrange("p (j o) -> p j o", o=1)
                P.op("dve", lambda e: e.reciprocal(rout, lcol), [lcol], [rout])
                if nacc > 6:
                    lcol2 = T3f[:, 0:258].rearrange("p (j c) -> p j c", c=129)[:, :, 128:129]
                    rout2 = R[:, 6:8].rearrange("p (j o) -> p j o", o=1)
                    P.op("dve", lambda e: e.reciprocal(rout2, lcol2), [lcol2], [rout2])
                Rv = R[:, 0:nacc].rearrange("p (s i) -> p s i", i=2)
                ts("dve", Rv[:, :, 1], Rv[:, :, 1], NEGLAM, None, ALU.mult)
                for sub in range(nsub):
                    ts("dve", AEP[:, sub, :], osacc(2 * sub)[:, 0:128], R[:, 2 * sub:2 * sub + 1], None, ALU.mult)
                    stt(AEP[:, sub, :], osacc(2 * sub + 1)[:, 0:128], R[:, 2 * sub + 1:2 * sub + 2], AEP[:, sub, :],
                        ALU.mult, ALU.add)
                    P.op("dve", (lambda e, sub=sub: e.scalar_tensor_tensor(JUNK[:], AEP[:, sub, :], 1.0, AEP[:, sub, :],
                                                                           ALU.mult, ALU.mult, accum_out=SS[:, sub:sub + 1])),
                         [AEP[:, sub, :]], [JUNK[:], SS[:, sub:sub + 1]])

            def stage2(nsub=nsub, SS=SS, LNV=LNV, RSTD=RSTD):
                act(LNV[:, 0:nsub], SS[:, 0:nsub], AF.Ln, bias=LN_EPS, scale=1.0 / 128.0)
                act(RSTD[:, 0:nsub], LNV[:, 0:nsub], AF.Exp, scale=-0.5)

            def stage3(nsub=nsub, RSTD=RSTD):
                for sub in range(nsub):
                    stt(AN[:, sub, :], AEP[:, sub, :], RSTD[:, sub:sub + 1], GSUB[:], ALU.mult, ALU.mult)

            def stage4(nsub=nsub):
                for sub in range(nsub):
                    tr(MISCB[:, sub * 128:(sub + 1) * 128], AN[:, sub, :], IDB[:])

            def stage5(h=h, q0=q0, Nq=Nq, cat=cat):
                cp("dve", cat[:, h, q0:q0 + Nq], MISCB[:, 0:Nq])

            deferred.extend([stage1, stage2, stage3, stage4, stage5])

    def sgu_mix(N, cat):
        ntile = N // 128
        for gi in range(4):
            bk = nb()
            for t_ in range(ntile):
                mm(bk[:, t_ * 128:(t_ + 1) * 128], VC[:, t_, gi * 128:(gi + 1) * 128], WST[:, gi, :],
                   start=True, stop=False)
                mm(bk[:, t_ * 128:(t_ + 1) * 128], ONER[0:1, :], BSGU[0:1, gi * 128:(gi + 1) * 128],
                   start=False, stop=True)
            tt("dve", cat[:, 4 + gi, 0:N], bk[:, 0:N], UT[:, gi, 0:N], ALU.mult)

    pending_release = []

    class StatAcc:
        def __init__(self, XS, N):
            self.XS, self.N = XS, N
            while pending_release:
                pending_release.pop().release()
            self.im, self.ie = nb_reserve(), nb_reserve()
            self.Bm, self.Be = BANK[self.im], BANK[self.ie]
            self.pending = []
            self.n = 0

        def add(self, c):
            self.pending.append(c)
            if len(self.pending) > 1:
                self._emit(self.pending.pop(0))

        def _emit(self, c):
            N, XS = self.N, self.XS
            k = self.n % 2
            act(SQB[:, k, 0:N], XS[:, c, 0:N], AF.Square)
            cp("act", RB[:, k, 0:N], XS[:, c, 0:N])
            mm(self.Bm[:, 0:N], ONESB[:], RB[:, k, 0:N], start=(self.n == 0), stop=(self.n == 7))
            mm(self.Be[:, 0:N], ONESB[:], SQB[:, k, 0:N], start=(self.n == 0), stop=(self.n == 7))
            self.n += 1

        def finish(self):
            while self.pending:
                self._emit(self.pending.pop(0))
            assert self.n == 8

        def release(self):
            reserved.discard(self.im)
            reserved.discard(self.ie)

    def out_proj(name, cat, XS, N, layer, cd):
        st = StatAcc(XS, N)
        for ob in range(2):
            blk = wload(name, ob)
            for m4 in range(4):
                m = ob * 4 + m4
                bk = nb()
                proj_fm(blk, m4, cat, N, bk)
                stt(XS[:, m, 0:N], bk[:, 0:N], dc(layer, cd, 2, m), XS[:, m, 0:N], ALU.mult, ALU.add)
                st.add(m)
        return st

    def layernorm(XS, N, layer, cd, which, want_h, st):
        gname = f"ln_{which}_g{layer}"
        bname = f"ln_{which}_b{layer}"
        st.finish()
        Bm, Be = st.Bm, st.Be
        cp("act", T2[:, 0, 0:N], Bm[:, 0:N])
        tt("dve", T2[:, 1, 0:N], T2[:, 0, 0:N], Bm[:, 0:N], ALU.mult)
        tt("dve", T2[:, 1, 0:N], Be[:, 0:N], T2[:, 1, 0:N], ALU.subtract)
        act(T2[:, 1, 0:N], T2[:, 1, 0:N], AF.Ln, bias=EPS_R)
        act(Be[:, 0:N], T2[:, 1, 0:N], AF.Exp, scale=-0.5)
        for c in range(8):
            tt("dve", XS[:, c, 0:N], XS[:, c, 0:N], Bm[:, 0:N], ALU.subtract)
            tt("dve", XS[:, c, 0:N], XS[:, c, 0:N], Be[:, 0:N], ALU.mult)
            if want_h:
                act(HT[:, c, 0:N], XS[:, c, 0:N], AF.Identity, bias=dc(layer, cd, 4, c), scale=dc(layer, cd, 3, c))
            act(XS[:, c, 0:N], XS[:, c, 0:N], AF.Identity, bias=col2(bname, c), scale=col2(gname, c))
        pending_release.append(st)

    t_ctr = [0]

    def ffn(XS, N, layer, cd):
        n1, n2 = f"w_ff1_{layer}", f"w_ff2_{layer}"
        st = None
        for hf in range(2):
            if hf == 1:
                st = StatAcc(XS, N)
            for j4 in range(4):
                blk = wload(n1, hf * 4 + j4)
                first = (hf == 0 and j4 == 0)
                if first:
                    bks = [nb() for _ in range(4)]
                    for kc in range(8):
                        for m4 in range(4):
                            mm(bks[m4][:, 0:N], blk[:, kc, m4 * 128:(m4 + 1) * 128], HT[:, kc, 0:N],
                               start=(kc == 0), stop=(kc == 7))
                for m4 in range(4):
                    if first:
                        bk = bks[m4]
                    else:
                        bk = nb()
                        proj_fm(blk, m4, HT, N, bk)
                    k = t_ctr[0] % 2
                    t_ctr[0] += 1
                    cp("act", T1[:, k, 0:N], bk[:, 0:N])
                    stt(HID[:, j4 * 4 + m4, 0:N], bk[:, 0:N], 0.0, T1[:, k, 0:N], ALU.max, ALU.mult)
            for mb in range(4):
                blk = wload(n2, hf * 4 + mb)
                for m2 in range(2):
                    m = mb * 2 + m2
                    bk = nb()
                    for kc in range(16):
                        mm(bk[:, 0:N], blk[:, kc, m2 * 128:(m2 + 1) * 128], HID[:, kc, 0:N], start=(kc == 0),
                           stop=(kc == 15))
                    stt(XS[:, m, 0:N], bk[:, 0:N], dc(layer, cd, 5, m), XS[:, m, 0:N], ALU.mult, ALU.add)
                    if hf == 1:
                        st.add(m)
        return st

    stg_ctr = [0]

    def store_tiles(XS, ntile, dst_rows_fn):
        for t_ in range(ntile):
            k = stg_ctr[0] % 2
            stg_ctr[0] += 1
            for half in range(2):
                bk = nb()
                for cc in range(4):
                    tr(bk[:, cc * 128:(cc + 1) * 128], XS[:, half * 4 + cc, t_ * 128:(t_ + 1) * 128], IDF[:])
                cp("act" if half == 0 else "dve", STG[:, k, half * 512:(half + 1) * 512], bk[:, 0:512])
            dma("pool", dst_rows_fn(t_), STG[:, k, :], semkey=f"stg{k}")

    def layer0_group(src_fn, ntile, cd, slot, sample, tab0, ktiles_fn, kvout_fn, pre_ln2=None):
        N = ntile * 128
        XS = XR[:, slot]
        load_group_dma([src_fn(t_) for t_ in range(ntile)])
        for c in range(8):
            load_group_chunk(c, ntile, cd, 0, XS, HT)
        if sample:
            dma("sp", TAB[:, 0, 0:N], cosT[:, tab0:tab0 + N], semkey="tabc")
            dma("sp", TAB[:, 1, 0:N], sinT[:, tab0:tab0 + N], semkey="tabs")
        if stop < 3.02:
            return
        blk = wload("w_in0", 0)
        for m4 in range(4):
            bk = nb()
            proj_fm(blk, m4, HT, N, bk)
            if sample:
                rope(bk, N, QT[:, m4, 0:N], QCOLS[:, m4:m4 + 1])
            else:
                cp("act", QT[:, m4, 0:N], bk[:, 0:N])
                norm_update(QT[:, m4, 0:N], N, QCOLS[:, m4:m4 + 1])
        cols_max(QCOLS[:, 0:4], MAXC[:, 0:1])
        if not sample:
            blk = wload("w_in0", 1)
            for m4 in range(4):
                bk = nb()
                proj_fm(blk, m4, HT, N, bk)
                cp("act", KTP[:, m4, 0:N], bk[:, 0:N])
                norm_update(KTP[:, m4, 0:N], N, KCOLS[:, m4:m4 + 1])
            cols_max(KCOLS[:, 0:4], MAXC[:, 1:2])
            blkv = wload("w_in0", 2)
            for t_ in range(ntile):
                k = stg_ctr[0] % 2
                stg_ctr[0] += 1
                bk = nb()
                proj_tm(blk, HT, t_, bk)
                cp("dve", STG[:, k, 0:512], bk[:, 0:512])
                bk2 = nb()
                proj_tm(blkv, HT, t_, bk2)
                cp("act", STG[:, k, 512:1024], bk2[:, 0:512])
                cp("dve", VV[:, t_, :].rearrange("p (h c) -> p h c", h=4)[:, :, 0:128],
                   bk2[:, 0:512].rearrange("p (h c) -> p h c", h=4))
                kd, vd_ = kvout_fn(t_)
                dma("pool", kd, STG[:, k, 0:512], semkey=f"stg{k}")
                dma("pool", vd_, STG[:, k, 512:1024], semkey=f"stg{k}")
        if stop < 3.03:
            return
        blk = wload("w_in0", 3)
        for m4 in range(4):
            bk = nb()
            proj_fm(blk, m4, HT, N, bk)
            cp("act" if m4 % 2 == 0 else "dve", UT[:, m4, 0:N], bk[:, 0:N])
        bound_finalize()
        if stop < 3.04:
            return
        blk = wload("w_in0", 4)
        for t_ in range(ntile):
            bk = nb()
            proj_tm(blk, HT, t_, bk)
            if stop >= 3.05:
                sgu_norm_tile(bk, t_)
        if stop < 3.2:
            return
        bound_broadcast()
        for (q0, Nq, kts) in ktiles_fn(N):
            attention(QT, q0, Nq, kts, HT)
        run_deferred()
        if stop < 3.3:
            return
        sgu_mix(N, HT)
        if stop < 3.4:
            return
        st = out_proj("w_out0", HT, XS, N, 0, cd)
        layernorm(XS, N, 0, cd, "mix", True, st)
        st = ffn(XS, N, 0, cd)
        if pre_ln2 is not None:
            pre_ln2()
        layernorm(XS, N, 0, cd, "ff", False, st)

    def l1_prologue(ntile, cd, slot):
        N = ntile * 128
        XS = XR[:, slot]
        for c in range(8):
            act(HT[:, c, 0:N], XS[:, c, 0:N], AF.Identity, bias=dc(1, cd, 1, c), scale=dc(1, cd, 0, c))

    def layer1_group(ntile, cd, slot, segs, xnext, use_prev, dst_rows_fn, skip_prologue=False):
        N = ntile * 128
        XS = XR[:, slot]
        if not skip_prologue:
            l1_prologue(ntile, cd, slot)
        if xnext is not None:
            for c in range(8):
                act(HX[:, c, 0:1], xnext[:, c, 0:1], AF.Identity, bias=dc(1, cd, 1, c), scale=dc(1, cd, 0, c))
        for half in range(2):
            bgb = wload("w_in1", 0 + half)
            cgb = wload("w_in1", 2 + half)
            xtb = wload("w_in1", 4 + half)
            for m4 in range(4):
                m = half * 4 + m4
                k = m % 2
                bC, bX, bB = nb(), nb(), nb()
                if half == 0 and m4 == 0:
                    for kc in range(8):
                        for (wb_, bk_) in ((cgb, bC), (xtb, bX), (bgb, bB)):
                            mm(bk_[:, 0:N], wb_[:, kc, 0:128], HT[:, kc, 0:N], start=(kc == 0), stop=(kc == 7))
                else:
                    proj_fm(cgb, m4, HT, N, bC)
                    proj_fm(xtb, m4, HT, N, bX)
                    proj_fm(bgb, m4, HT, N, bB)
                cp("act", T1[:, k, 0:N], bC[:, 0:N])
                Z = T3[:, k, 0:N]
                C = T2[:, k, 0:N]
                tt("dve", Z, bX[:, 0:N], T1[:, k, 0:N], ALU.mult)
                ts("dve", C, Z, col2("conv", 8 + m), None, ALU.mult)
                for (a, b) in segs:
                    stt(C[:, a + 1:b], Z[:, a:b - 1], col2("conv", 0 + m), C[:, a + 1:b], ALU.mult, ALU.add)
                    stt(C[:, a:b - 1], Z[:, a + 1:b], col2("conv", 16 + m), C[:, a:b - 1], ALU.mult, ALU.add)
                if use_prev:
                    stt(C[:, 0:1], ZSAVE[:, m:m + 1], col2("conv", 0 + m), C[:, 0:1], ALU.mult, ALU.add)
                    cp("dve", ZSAVE[:, m:m + 1], Z[:, N - 1:N])
                if xnext is not None:
                    bH = nb()
                    for kc in range(8):
                        mm(bH[:, 0:1], cgb[:, kc, m4 * 128:(m4 + 1) * 128], HX[:, kc, 0:1], start=(kc == 0),
                           stop=(kc == 7))
                    for kc in range(8):
                        mm(bH[:, 2:3], xtb[:, kc, m4 * 128:(m4 + 1) * 128], HX[:, kc, 0:1], start=(kc == 0),
                           stop=(kc == 7), skip_group_check=True)
                    cp("act", SMZ[:, m, 0:1], bH[:, 0:1])
                    tt("dve", SMZ[:, m, 1:2], bH[:, 2:3], SMZ[:, m, 0:1], ALU.mult)
                    stt(C[:, N - 1:N], SMZ[:, m, 1:2], col2("conv", 16 + m), C[:, N - 1:N], ALU.mult, ALU.add)
                tt("dve", CAT1[:, m, 0:N], bB[:, 0:N], C, ALU.mult)
        st = out_proj("w_out1", CAT1, XS, N, 1, cd)
        layernorm(XS, N, 1, cd, "mix", True, st)
        st = ffn(XS, N, 1, cd)
        layernorm(XS, N, 1, cd, "ff", False, st)
        store_tiles(XS, ntile, dst_rows_fn)

    HTB = [HT, CAT1]

    def kv_srcs(g):
        return [xs[(g * 4 + t_) * 128:(g * 4 + t_ + 1) * 128, :] for t_ in range(4)]

    def kv_tab(g):
        dma("sp", TAB[:, 0, :], cosT[:, g * 512:(g + 1) * 512], semkey="tabc")
        dma("sp", TAB[:, 1, :], sinT[:, g * 512:(g + 1) * 512], semkey="tabs")

    NKV = 8 if stop >= 2 else 0
    if NKV:
        load_group_dma(kv_srcs(0))
        for c in range(8):
            load_group_chunk(c, 4, 0, 0, None, HTB[0])
        load_group_dma(kv_srcs(1))
        kv_tab(0)
        blk_k = wload("w_in0", 1)
        sk_ = (ws_ctr[0] - 1) % 4
        blk_v = wload("w_in0", 2)
        sv_ = (ws_ctr[0] - 1) % 4
        ws_pin.update([sk_, sv_])
        ibm2 = nb_reserve()
        bm2 = BANK[ibm2]
    for kvg in range(NKV):
        H = HTB[kvg % 2]
        for m4 in range(4):
            bk = nb()
            proj_fm(blk_k, m4, H, 512, bk)
            rope(bk, 512, KT[:, m4, kvg * 512:(kvg + 1) * 512], KCOLS[:, kvg * 4 + m4:kvg * 4 + m4 + 1])
            if kvg + 1 < NKV:
                for c in (2 * m4, 2 * m4 + 1):
                    load_group_chunk(c, 4, 0, 0, None, HTB[(kvg + 1) % 2])
        if kvg + 1 < NKV:
            kv_tab(kvg + 1)
        if kvg + 2 < NKV:
            load_group_dma(kv_srcs(kvg + 2))
        mod_blocks(0, [4 + kvg], bm2)
        for t_ in range(4):
            bk = nb()
            proj_tm(blk_v, H, t_, bk)
            cp("act" if t_ % 2 == 0 else "dve",
               VV[:, kvg * 4 + t_, :].rearrange("p (h c) -> p h c", h=4)[:, :, 0:128],
               bk[:, 0:512].rearrange("p (h c) -> p h c", h=4))
    if NKV:
        ws_pin.clear()
    for j in range(2 if stop >= 2 else 0):
        k = xin_ctr[0] % 2
        xin_ctr[0] += 1
        dma("sp", XIN[:, k, 0:512], ck[j * 128:(j + 1) * 128, :], semkey=f"xin{k}")
        bk = nb()
        for m4 in range(4):
            tr(bk[:, m4 * 128:(m4 + 1) * 128], XIN[:, k, m4 * 128:(m4 + 1) * 128], IDF[:])
        cp("dve", KT[:, :, (32 + j) * 128:(33 + j) * 128], bk[:, 0:512].rearrange("p (c n) -> p c n", c=4))
        for m4 in range(4):
            norm_update(KT[:, m4, (32 + j) * 128:(33 + j) * 128], 128, KCOLS[:, 32 + j * 4 + m4:33 + j * 4 + m4])
        dma("pool", VV[:, 32 + j, :].rearrange("p (h c) -> p h c", h=4)[:, :, 0:128],
            cv[j * 128:(j + 1) * 128, :].rearrange("p (h c) -> p h c", h=4), semkey=f"cvld{j}")

    if stop >= 2:
        cols_max(KCOLS[:, 0:40], MAXC[:, 1:2])
    if NKV:
        modulation(0, [4, 5, 6, 7, 8, 9, 10, 11], "b", bm=bm2)
        reserved.discard(ibm2)
    else:
        modulation(0, [4, 5, 6, 7, 8, 9, 10, 11], "b")
    cast_all(["w_out0", "w_ff1_0", "w_ff2_0"])

    def sample_ktiles(N):
        kts = []
        for kt in range(34):
            kts.append(((lambda h, kt=kt: KT[:, h, kt * 128:(kt + 1) * 128]),
                        (lambda h, kt=kt: VV[:, kt, h * 129:(h + 1) * 129])))
        return [(0, N, kts)]

    groups = [(0, 4), (4, 4), (8, 4), (12, 3), (15, 2)]

    def s_l0(g):
        t0, nt = groups[g]
        hook = None
        if g >= 2:
            hook = lambda: l1_prologue(groups[g - 1][1], 0, (g - 1) % 2)
        layer0_group(lambda t_: xs[(t0 + t_) * 128:(t0 + t_ + 1) * 128, :], nt, 0, g % 2, True, t0 * 128,
                     sample_ktiles, None, pre_ln2=hook)

    def s_l1(g):
        t0, nt = groups[g]
        xnext = XR[:, (g + 1) % 2] if g + 1 < len(groups) else None
        hoisted = (g >= 1 and g + 1 < len(groups))
        layer1_group(nt, 0, g % 2, [(0, nt * 128)], xnext, True,
                     lambda t_: ys[(t0 + t_) * 128:(t0 + t_ + 1) * 128, :], skip_prologue=hoisted)

    if stop >= 3:
        s_l0(0)
    cast_all(["w_in1", "w_out1", "w_ff1_1", "w_ff2_1"])
    if stop >= 4:
        for g in range(1, len(groups)):
            s_l0(g)
            if g == 1:
                modulation(1, list(range(12)), "ab")
            s_l1(g - 1)
        s_l1(len(groups) - 1)

    def prompt_ktiles(N):
        res = []
        for bi in range(2):
            kts = []
            for j in range(2):
                kt = bi * 2 + j
                kts.append(((lambda h, kt=kt: KTP[:, h, kt * 128:(kt + 1) * 128]),
                            (lambda h, kt=kt: VV[:, kt, h * 129:(h + 1) * 129])))
            res.append((bi * 256, 256, kts))
        return res

    for pg in range(2 if stop >= 5 else 0):
        r0 = pg * 512
        layer0_group(lambda t_: xp[r0 + t_ * 128: r0 + (t_ + 1) * 128, :], 4, 1, 0, False, 0, prompt_ktiles,
                     lambda t_: (nk[r0 + t_ * 128: r0 + (t_ + 1) * 128, :], nv[r0 + t_ * 128: r0 + (t_ + 1) * 128, :]))
        layer1_group(4, 1, 0, [(0, 256), (256, 512)], None, False,
                     lambda t_: yp[r0 + t_ * 128: r0 + (t_ + 1) * 128, :])

    ops = P.ops
    for o in ops:
        for d in o.deps:
            ops[d].has_dep = True
    ENG = ["pe", "act", "dve", "pool", "sp"]
    semkeys = sorted({o.semkey for o in ops if o.dma})
    sems = {}
    for e_ in ENG:
        sems[e_] = es.enter_context(nc.semaphore("s_" + e_))
    for k in semkeys:
        sems["d_" + k] = es.enter_context(nc.semaphore("d_" + k))
    ecount = {e_: 0 for e_ in ENG}
    dcount = {k: 0 for k in semkeys}
    for o in ops:
        if o.dma:
            dcount[o.semkey] += 16
            o.cnt = dcount[o.semkey]
        elif o.has_dep:
            ecount[o.eng] += 1
            o.cnt = ecount[o.eng]
    dtotal = dict(dcount)
    out_keys = [k for k in semkeys if k.startswith("stg")]

    block = es.enter_context(nc.Block())

    def emit_engine(ename, e):
        waited = {}
        for o in ops:
            if o.eng != ename:
                continue
            need = {}
            for d in o.deps:
                p = ops[d]
                if p.dma:
                    sk = "d_" + p.semkey
                    val = dtotal[p.semkey] if p.whole else p.cnt
                else:
                    if p.eng == "pe" and ename == "pe":
                        continue
                    sk = p.eng
                    val = p.cnt
                if val > need.get(sk, 0):
                    need[sk] = val
            for sk, val in need.items():
                if waited.get(sk, 0) >= val:
                    continue
                e.wait_ge(sems[sk], val)
                waited[sk] = val
            ins = o.fn(e)
            if o.dma:
                ins.then_inc(sems["d_" + o.semkey], 16)
            elif o.has_dep:
                ins.then_inc(sems[ename], 1)
        if ename == "pool":
            for k in out_keys:
                e.wait_ge(sems["d_" + k], dtotal[k])

    @block.tensor
    def _(e):
        emit_engine("pe", e)

    @block.scalar
    def _(e):
        emit_engine("act", e)

    @block.vector
    def _(e):
        emit_engine("dve", e)

    @block.gpsimd
    def _(e):
        emit_engine("pool", e)

    @block.sync
    def _(e):
        emit_engine("sp", e)

    es.close()
    return nc


_NC_CACHE = {}


def _rope_tables(order):
    pos = (np.asarray(order)[:, None] * 128 + np.arange(128)[None, :]).reshape(-1)
    row = (pos // 64).astype(np.float32)
    col = (pos % 64).astype(np.float32)
    inv = (1.0 / (np.float32(10000.0) ** (np.arange(16, dtype=np.float32) / np.float32(16)))).astype(np.float32)
    ang = [row[:, None] * inv[None, :], col[:, None] * inv[None, :]]
    cosT = np.zeros((128, 4096), np.float32)
    sinT = np.zeros((128, 4096), np.float32)
    for p in range(128):
        pm = p % 64
        s = pm // 32
        j = (pm % 32) // 16
        f = pm % 16
        cosT[p] = np.cos(ang[s][:, f])
        sinT[p] = np.sin(ang[s][:, f]) * (-1.0 if j == 0 else 1.0)
    return cosT, sinT


def kernel(**inp):
    f = lambda a: np.ascontiguousarray(np.asarray(a, dtype=np.float32))
    x_prompt, x_sample = f(inp["x_prompt"]), f(inp["x_sample"])
    cache_k0, cache_v0 = f(inp["cache_k0"]), f(inp["cache_v0"])
    c, c_ctx = f(inp["c"]), f(inp["c_ctx"])
    if "nc" not in _NC_CACHE:
        _NC_CACHE["nc"] = build_nc()
    nc = _NC_CACHE["nc"]
    ident = np.eye(128, dtype=np.float32)
    perm = np.zeros((128, 128), np.float32)
    for m in range(128):
        perm[m ^ 16, m] = 1.0
    shared = {"c_ident": ident, "c_perm": perm}
    for name, _, _ in W_SPECS:
        shared[name] = f(inp[name])
    for name in ("w_mod0", "w_mod1", "conv_w1", "subln_g0", "sgu_w0", "sgu_b0", "lambda_q1_0", "lambda_k1_0",
                 "lambda_q2_0", "lambda_k2_0"):
        shared[name] = f(inp[name])
    for name, _ in VEC_IN:
        shared[name] = f(inp[name])
    in_maps = []
    orders = []
    for core in range(8):
        b, half = core // 2, core % 2
        win = list(range(0, 17)) if half == 0 else list(range(15, 32))
        others = [t for t in range(32) if t not in win]
        order = win + others
        orders.append(order)
        xt = x_sample[b].reshape(32, 128, 1024)[order].reshape(4096, 1024)
        cosT, sinT = _rope_tables(order)
        m = dict(shared)
        m["xs"] = np.ascontiguousarray(xt)
        m["xp"] = np.ascontiguousarray(x_prompt[4 * core:4 * core + 4].reshape(1024, 1024))
        m["ck"] = np.ascontiguousarray(cache_k0[b].reshape(256, 512))
        m["cv"] = np.ascontiguousarray(cache_v0[b].reshape(256, 512))
        m["cvec"] = np.ascontiguousarray(np.stack([c[b], c_ctx], 0))
        m["cosT"] = cosT
        m["sinT"] = sinT
        in_maps.append(m)
    res = run_bass_kernel_spmd(nc, in_maps, core_ids=list(range(8)))
    y_prompt = np.zeros((32, 256, 1024), np.float32)
    y_sample = np.zeros((4, 4096, 1024), np.float32)
    new_k = np.zeros((32, 256, 4, 2, 64), np.float32)
    new_v = np.zeros((32, 256, 4, 128), np.float32)
    for core in range(8):
        r = res.results[core]
        b, half = core // 2, core % 2
        ysc = np.asarray(r["ys"])
        if half == 0:
            y_sample[b, 0:2048] = ysc[0:2048]
        else:
            y_sample[b, 2048:4096] = ysc[128:2176]
        y_prompt[4 * core:4 * core + 4] = np.asarray(r["yp"]).reshape(4, 256, 1024)
        new_k[4 * core:4 * core + 4] = np.asarray(r["nk"]).reshape(4, 256, 4, 2, 64)
        new_v[4 * core:4 * core + 4] = np.asarray(r["nv"]).reshape(4, 256, 4, 128)
    return (y_prompt, y_sample, new_k, new_v)
```

```python
import math
import os
from contextlib import ExitStack

import numpy as np
import concourse.bass as bass
import concourse.mybir as mybir
from concourse.bass_utils import run_bass_kernel_spmd

F32 = mybir.dt.float32
BF16 = mybir.dt.bfloat16
AF = mybir.ActivationFunctionType
ALU = mybir.AluOpType
AX = mybir.AxisListType

D = 1024
ALPHA = 4.0 ** 0.25
LAMBDA_INIT = 0.8 - 0.6 * math.exp(-0.3 * 0)
LN_EPS = 1e-5
EPS_R = LN_EPS / (ALPHA * ALPHA)
NWIN = 17
DSZ = {F32: 4, BF16: 2}


class Op:
    __slots__ = ("eng", "fn", "deps", "dma", "semkey", "whole", "has_dep", "cnt")

    def __init__(self, eng, fn, dma, semkey, whole):
        self.eng, self.fn, self.dma, self.semkey, self.whole = eng, fn, dma, semkey, whole
        self.deps = set()
        self.has_dep = False
        self.cnt = 0


def _region(ap):
    name = ap.tensor.name
    es = DSZ.get(ap.dtype, 4)
    dims = list(ap.ap)
    off = ap.offset
    if str(ap.space) == "DRAM":
        span = sum((c - 1) * abs(s) for s, c in dims)
        return (name, 0, 1, off * es, (off + span + 1) * es)
    ps, pc = dims[0]
    ps = max(ps, 1)
    p0 = off // ps
    f0 = off % ps
    span = sum((c - 1) * abs(s) for s, c in dims[1:])
    if str(ap.space) == "PSUM":
        return (name, 0, 128, (f0 * es) // 2048 * 2048, ((f0 + span + 1) * es + 2047) // 2048 * 2048)
    return (name, p0, p0 + pc, f0 * es, (f0 + span + 1) * es)


class Prog:
    def __init__(self):
        self.ops = []
        self.recs = {}

    def _rkey(self, idx):
        o = self.ops[idx]
        return (o.eng, o.semkey)

    def _touch(self, idx, ap, write):
        name, p0, p1, lo, hi = _region(ap)
        lst = self.recs.setdefault(name, [])
        deps = self.ops[idx].deps
        keep = []
        for r in lst:
            rp0, rp1, rlo, rhi, w, rd = r
            if rp1 <= p0 or p1 <= rp0 or rhi <= lo or hi <= rlo:
                keep.append(r)
                continue
            if w is not None:
                deps.add(w)
            if write:
                deps.update(rd.values())
                if p0 <= rp0 and rp1 <= p1 and lo <= rlo and rhi <= hi:
                    continue
            keep.append(r)
        if write:
            keep.append([p0, p1, lo, hi, idx, {}])
        else:
            done = False
            for r in keep:
                if r[0] <= p0 and p1 <= r[1] and r[2] <= lo and hi <= r[3]:
                    r[5][self._rkey(idx)] = idx
                    done = True
                    break
            if not done:
                keep.append([p0, p1, lo, hi, None, {self._rkey(idx): idx}])
        self.recs[name] = keep

    def op(self, eng, fn, reads=(), writes=(), dma=False, semkey=None, whole=False):
        idx = len(self.ops)
        self.ops.append(Op(eng, fn, dma, semkey, whole))
        for a in reads:
            if a is not None and not isinstance(a, (int, float)):
                self._touch(idx, a, str(a.space) == "PSUM")
        for a in writes:
            self._touch(idx, a, True)
        o = self.ops[idx]
        o.deps.discard(idx)
        return idx


W_SPECS = [
    ("w_in0", 1024, 2560), ("w_out0", 1024, 1024), ("w_ff1_0", 1024, 4096), ("w_ff2_0", 4096, 1024),
    ("w_in1", 1024, 3072), ("w_out1", 1024, 1024), ("w_ff1_1", 1024, 4096), ("w_ff2_1", 4096, 1024),
]
VEC_IN = [("b_mod0", 6144), ("b_mod1", 6144), ("ln_mix_g0", 1024), ("ln_mix_b0", 1024), ("ln_ff_g0", 1024),
          ("ln_ff_b0", 1024), ("ln_mix_g1", 1024), ("ln_mix_b1", 1024), ("ln_ff_g1", 1024), ("ln_ff_b1", 1024)]


def build_nc(stop=99):
    nc = bass.Bass("TRN2", target_bir_lowering=False)
    P = Prog()
    es = ExitStack()

    def din(name, shape, dt=F32):
        return nc.dram_tensor(name, list(shape), dt, kind="ExternalInput").ap()

    def dout(name, shape, dt=F32):
        return nc.dram_tensor(name, list(shape), dt, kind="ExternalOutput").ap()

    xs = din("xs", [4096, 1024])
    xp = din("xp", [1024, 1024])
    ck = din("ck", [256, 512])
    cv = din("cv", [256, 512])
    cvec = din("cvec", [2, 1024])
    cosT = din("cosT", [128, 4096])
    sinT = din("sinT", [128, 4096])
    c_ident = din("c_ident", [128, 128])
    c_perm = din("c_perm", [128, 128])
    wd = {}
    for name, K, ncol in W_SPECS:
        wd[name] = din(name, [K, ncol])
    wd["w_mod0"] = din("w_mod0", [1024, 6144])
    wd["w_mod1"] = din("w_mod1", [1024, 6144])
    vd = {n: din(n, [ln]) for n, ln in VEC_IN}
    conv_w1 = din("conv_w1", [3, 1024])
    lam_in = {n: din(n, [64]) for n in ("lambda_q1_0", "lambda_k1_0", "lambda_q2_0", "lambda_k2_0")}
    subln_g0 = din("subln_g0", [128])
    sgu_w0 = din("sgu_w0", [4, 128, 128])
    sgu_b0 = din("sgu_b0", [4, 128])

    ys = dout("ys", [NWIN * 128, 1024])
    yp = dout("yp", [1024, 1024])
    nk = dout("nk", [1024, 512])
    nv = dout("nv", [1024, 512])

    blk_of = {}
    nblk = 0
    for name, K, ncol in W_SPECS:
        if K == 1024:
            n = ncol // 512
        else:
            n = 8
        blk_of[name] = (nblk, n)
        nblk += n
    wscr = nc.dram_tensor("wscr", [nblk, 128, 4096], BF16, kind="Internal").ap()

    def sb(name, shape, dt):
        return es.enter_context(nc.sbuf_tensor(name, list(shape), dt))

    KT = sb("KT", [128, 4, 4352], BF16)
    VV = sb("VV", [128, 34, 516], BF16)
    XR = sb("XR", [128, 2, 8, 512], F32)
    XIN = sb("XIN", [128, 2, 1024], F32)
    STG = sb("STG", [128, 2, 1024], F32)
    HT = sb("HT", [128, 8, 512], BF16)
    HX = sb("HX", [128, 8, 2], BF16)
    TAB = sb("TAB", [128, 2, 512], F32)
    T1 = sb("T1", [128, 2, 512], F32)
    T2 = sb("T2", [128, 2, 512], F32)
    T3 = sb("T3", [128, 2, 512], F32)
    QRAW = T2
    RB = T1[:, 0, :].bitcast(BF16).rearrange("p (a n) -> p a n", a=2)
    ARENA = sb("ARENA", [128, 10240], BF16)
    WS = sb("WS", [128, 4, 4096], BF16)
    IDF = sb("IDF", [128, 128], F32)
    IDB = sb("IDB", [128, 128], BF16)
    PERM = sb("PERM", [128, 128], F32)
    WST = sb("WST", [128, 4, 128], BF16)
    GSUB = sb("GSUB", [128, 128], F32)
    VB1 = T2[:, 0, 0:128]
    VB2 = T2[:, 1, 0:128]
    COLS1 = sb("COLS1", [128, 96], F32)
    COLS2 = sb("COLS2", [128, 104], F32)
    SIL = sb("SIL", [128, 8, 2], BF16)
    MODT = sb("MODT", [128, 2, 48, 2], F32)
    NVD = 8
    DCOL = sb("DCOL", [128, 2, 2, NVD, 8], F32)
    BSGU = sb("BSGU", [1, 512], F32)
    BSG2 = sb("BSG2", [1, 2, 512], BF16)
    ONERB = sb("ONERB", [1, 128], BF16)
    ONER = sb("ONER", [1, 128], F32)
    LAMT = T3[:, 0, 0:256].rearrange("p (a b) -> p a b", a=4)
    LAMS = sb("LAMS", [128, 8], F32)
    NEGC = sb("NEGC", [128, 1], F32)
    JUNK = sb("JUNK", [128, 128], F32)
    AN = sb("AN", [128, 4, 128], BF16)
    SM = sb("SM", [128, 8, 8], F32)
    ST = sb("ST", [128, 2, 4, 6], F32)
    MV = sb("MV", [128, 2, 4, 2], F32)
    RS = sb("RS", [128, 2, 8], F32)
    ZSAVE = sb("ZSAVE", [128, 8], F32)
    HAL = sb("HAL", [128, 4, 8], F32)
    SGW = T1[:, 0, :].rearrange("p (a b) -> p a b", a=4)

    BLK1 = sb("BLK1", [128, 128], BF16)
    ONESB = sb("ONESB", [128, 128], BF16)
    SQB = sb("SQB", [128, 2, 512], BF16)
    KCOLS = sb("KCOLS", [128, 48], F32)
    QCOLS = sb("QCOLS", [128, 8], F32)
    MAXC = sb("MAXC", [128, 2], F32)
    S1 = sb("S1", [1, 16], F32)

    PS01 = es.enter_context(nc.psum_tensor("PS01", [128, 1024], F32))
    PS23 = es.enter_context(nc.psum_tensor("PS23", [128, 1024], F32))
    BANK = [PS01[:, 0:512], PS01[:, 512:1024], PS23[:, 0:512], PS23[:, 512:1024]]
    BANK += [es.enter_context(nc.psum_tensor(f"B{i}", [128, 512], F32))[:] for i in range(4, 8)]

    QT = ARENA[:, 0:2048].rearrange("p (c n) -> p c n", c=4)
    VC = ARENA[:, 2048:4096].rearrange("p (c n) -> p c n", c=4)
    CAT1 = ARENA[:, 0:4096].rearrange("p (c n) -> p c n", c=8)
    ET = ARENA[:, 4096:6144].rearrange("p (c n) -> p c n", c=4)
    UT = ARENA[:, 6144:10240].bitcast(F32).rearrange("p (c n) -> p c n", c=4)
    HID = ARENA[:, 0:8192].rearrange("p (c n) -> p c n", c=16)
    KTP = KT[:, :, 0:512]

    bank_ctr = [0]

    reserved = set()

    def nb():
        while (bank_ctr[0] % 8) in reserved:
            bank_ctr[0] += 1
        b = BANK[bank_ctr[0] % 8]
        bank_ctr[0] += 1
        return b

    def nb_reserve():
        while (bank_ctr[0] % 8) in reserved:
            bank_ctr[0] += 1
        i = bank_ctr[0] % 8
        bank_ctr[0] += 1
        reserved.add(i)
        return i

    def mm(out, lhsT, rhs, start=True, stop=True, **kw):
        P.op("pe", lambda e: e.matmul(out, lhsT, rhs, start=start, stop=stop, **kw), [lhsT, rhs], [out])

    def tr(out, in_, ident):
        P.op("pe", lambda e: e.transpose(out, in_, ident), [in_, ident], [out])

    def act(out, in_, func, bias=None, scale=None, accum=None):
        kw = {}
        if bias is not None:
            kw["bias"] = bias
        if scale is not None:
            kw["scale"] = scale
        if accum is not None:
            kw["accum_out"] = accum
        wr = [out] + ([accum] if accum is not None else [])
        P.op("act", lambda e: e.activation(out, in_, func, **kw), [in_, bias, scale], wr)

    def ts(eng, out, in0, s1, s2, op0, op1=None):
        if op1 is None:
            P.op(eng, lambda e: e.tensor_scalar(out, in0, s1, None, op0), [in0, s1], [out])
        else:
            P.op(eng, lambda e: e.tensor_scalar(out, in0, s1, s2, op0, op1), [in0, s1, s2], [out])

    def tt(eng, out, in0, in1, op):
        P.op(eng, lambda e: e.tensor_tensor(out, in0, in1, op), [in0, in1], [out])

    def stt(out, in0, scalar, in1, op0, op1):
        P.op("dve", lambda e: e.scalar_tensor_tensor(out, in0, scalar, in1, op0, op1), [in0, scalar, in1], [out])

    def cp(eng, out, in_):
        if eng == "act":
            P.op("act", lambda e: e.activation(out, in_, AF.Copy), [in_], [out])
        else:
            P.op(eng, lambda e: e.tensor_copy(out, in_), [in_], [out])

    def dma(q, out, in_, semkey, whole=False, **kw):
        P.op(q, lambda e: e.dma_start(out=out, in_=in_, **kw), [in_], [out], dma=True, semkey=semkey, whole=whole)

    def memset(eng, ap, val):
        P.op(eng, lambda e: e.memset(ap, val), [], [ap])

    def w_src(name, b):
        K = dict((n, k) for n, k, _ in W_SPECS)[name]
        w = wd[name]
        if K == 1024:
            return w[:, b * 512:(b + 1) * 512].rearrange("(kc p) n -> p kc n", p=128), 8, 512
        hf, mb = b // 4, b % 4
        return (w[hf * 2048:(hf + 1) * 2048, mb * 256:(mb + 1) * 256].rearrange("(kc p) n -> p kc n", p=128),
                16, 256)

    def scr_view(name, b):
        base, n = blk_of[name]
        _, kc, ncol = w_src(name, b)
        return wscr[base + b].rearrange("p (kc n) -> p kc n", kc=kc)

    def cast_all(names):
        for name in names:
            base, n = blk_of[name]
            for b in range(n):
                src, kc, ncol = w_src(name, b)
                dma("pool", scr_view(name, b), src, semkey="cast_" + name, whole=True)

    ws_ctr = [0]
    ws_pin = set()

    def ws_next():
        while (ws_ctr[0] % 4) in ws_pin:
            ws_ctr[0] += 1
        s_ = ws_ctr[0] % 4
        ws_ctr[0] += 1
        return s_

    def wload(name, b):
        s = ws_next()
        _, kc, ncol = w_src(name, b)
        dst = WS[:, s, :].rearrange("p (kc n) -> p kc n", kc=kc)
        dma("sp", dst, scr_view(name, b), semkey=f"ws{s}")
        return dst

    def wload_mod(layer, b):
        s = ws_next()
        dst = WS[:, s, :].rearrange("p (kc n) -> p kc n", kc=8)
        src = wd[f"w_mod{layer}"][:, b * 512:(b + 1) * 512].rearrange("(kc p) n -> p kc n", p=128)
        dma("pool", dst, src, semkey=f"wm{s}")
        return dst

    def vrows(ap1d, n):
        return ap1d.rearrange("(c p) -> c p", p=128)

    dma("sp", IDF[:], c_ident, "setup", whole=True)
    dma("sp", PERM[:], c_perm, "setup", whole=True)
    dma("sp", VB1[0:48, :], vrows(vd["b_mod0"], 48), "setup", whole=True)
    dma("sp", VB1[48:96, :], vrows(vd["b_mod1"], 48), "setup", whole=True)
    r = 0
    vb2_off = {}
    for l in range(2):
        for nm in ("ln_mix_g", "ln_mix_b", "ln_ff_g", "ln_ff_b"):
            dma("sp", VB2[r:r + 8, :], vrows(vd[f"{nm}{l}"], 8), "setup", whole=True)
            vb2_off[f"{nm}{l}"] = r
            r += 8
    dma("sp", VB2[r:r + 24, :], conv_w1.rearrange("t (c p) -> (t c) p", p=128), "setup", whole=True)
    vb2_off["conv"] = r
    r += 24
    dma("sp", VB2[r:r + 16, :], cvec.rearrange("b (c p) -> (b c) p", p=128), "setup", whole=True)
    vb2_off["cvec"] = r
    r += 16
    assert r == 104
    for i, n in enumerate(("lambda_q1_0", "lambda_k1_0", "lambda_q2_0", "lambda_k2_0")):
        dma("sp", LAMT[:, i, :], lam_in[n].partition_broadcast(128), "setup", whole=True)
    dma("sp", GSUB[:], subln_g0.partition_broadcast(128), "setup", whole=True)
    dma("sp", SGW, sgu_w0.rearrange("g p q -> p g q"), "setup", whole=True)
    dma("sp", BSGU[:], sgu_b0.rearrange("g p -> (g p)").partition_broadcast(1), "setup", whole=True)


    memset("dve", ONER[:], 1.0)
    memset("dve", ONERB[:], 1.0)
    cp("dve", BSG2[0:1, 0, :], BSGU[0:1, :])
    tt("dve", BSG2[0:1, 1, :], BSGU[0:1, :], BSG2[0:1, 0, :], ALU.subtract)
    memset("dve", NEGC[:], 0.0)
    memset("dve", ZSAVE[:], 0.0)
    memset("dve", VV[:].rearrange("p k (h c) -> p k h c", h=4)[:, :, :, 128:129], 1.0)
    cp("dve", IDB[:], IDF[:])
    memset("dve", BLK1[:], 0.0)
    memset("dve", ONESB[:], 1.0 / 1024.0)
    memset("dve", BLK1[0:64, 0:64], 1.0)
    memset("dve", BLK1[64:128, 64:128], 1.0)

    b0 = nb()
    tr(b0[:, 0:96], VB1[0:96, :], IDF[0:96, 0:96])
    cp("dve", COLS1[:], b0[:, 0:96])
    b1 = nb()
    tr(b1[:, 0:104], VB2[0:104, :], IDF[0:104, 0:104])
    cp("dve", COLS2[:], b1[:, 0:104])

    def col2(name, c):
        o = vb2_off[name] + c
        return COLS2[:, o:o + 1]

    co = vb2_off["cvec"]
    for cd in range(2):
        act(SIL[:, :, cd], COLS2[:, co + cd * 8: co + cd * 8 + 8], AF.Silu)

    bw = nb()
    for g in range(4):
        tr(bw[:, g * 128:(g + 1) * 128], SGW[:, g, :], IDF[:])
    cp("dve", WST[:].rearrange("p g q -> p (g q)"), bw[:, 0:512])

    tt("dve", LAMT[:, 0, :], LAMT[:, 0, :], LAMT[:, 1, :], ALU.mult)
    tt("dve", LAMT[:, 2, :], LAMT[:, 2, :], LAMT[:, 3, :], ALU.mult)
    P.op("dve", lambda e: e.reduce_sum(LAMS[:, 0:1], LAMT[:, 0, :], AX.X), [LAMT[:, 0, :]], [LAMS[:, 0:1]])
    P.op("dve", lambda e: e.reduce_sum(LAMS[:, 1:2], LAMT[:, 2, :], AX.X), [LAMT[:, 2, :]], [LAMS[:, 1:2]])
    act(LAMS[:, 2:4], LAMS[:, 0:2], AF.Exp)
    tt("dve", LAMS[:, 4:5], LAMS[:, 3:4], LAMS[:, 2:3], ALU.subtract)
    ts("dve", LAMS[:, 5:6], LAMS[:, 4:5], -LAMBDA_INIT, None, ALU.add)
    NEGLAM = LAMS[:, 5:6]
    ts("dve", GSUB[:], GSUB[:], 1.0 - LAMBDA_INIT, None, ALU.mult)

    def mod_blocks(layer, blocks, bm):
        for b in blocks:
            wb = wload_mod(layer, b)
            for j4 in range(4):
                j = b * 4 + j4
                for kc in range(8):
                    mm(bm[:, 2 * j:2 * j + 2], wb[:, kc, j4 * 128:(j4 + 1) * 128], SIL[:, kc, :],
                       start=(kc == 0), stop=(kc == 7))

    def modulation(layer, blocks, derive, bm=None):
        if bm is None:
            bm = nb()
            mod_blocks(layer, blocks, bm)
        j0, j1 = blocks[0] * 4, blocks[-1] * 4 + 4
        bmv = bm[:, 0:96].rearrange("p (j c) -> p j c", c=2)
        for cd in range(2):
            tt("dve", MODT[:, layer, j0:j1, cd], bmv[:, j0:j1, cd], COLS1[:, layer * 48 + j0:layer * 48 + j1],
               ALU.add)
        for cd in range(2):
            M = lambda w: MODT[:, layer, w * 8:(w + 1) * 8, cd]
            Dv = lambda k: DCOL[:, layer, cd, k, :]
            gm = COLS2[:, vb2_off[f"ln_mix_g{layer}"]: vb2_off[f"ln_mix_g{layer}"] + 8]
            bmx = COLS2[:, vb2_off[f"ln_mix_b{layer}"]: vb2_off[f"ln_mix_b{layer}"] + 8]
            if "a" in derive:
                ts("dve", Dv(0), M(1), 1.0, None, ALU.add)
                cp("dve", Dv(1), M(0))
            if "b" in derive:
                ts("dve", Dv(2), M(2), 1.0 / ALPHA, None, ALU.mult)
                ts("dve", Dv(6), M(4), 1.0, None, ALU.add)
                tt("dve", Dv(3), gm, Dv(6), ALU.mult)
                tt("dve", Dv(4), bmx, Dv(6), ALU.mult)
                tt("dve", Dv(4), Dv(4), M(3), ALU.add)
                ts("dve", Dv(5), M(5), 1.0 / ALPHA, None, ALU.mult)

    def dc(layer, cd, k, c):
        return DCOL[:, layer, cd, k, c:c + 1]

    modulation(0, [0, 1, 2, 3], "a")
    cast_all(["w_in0"])

    xin_ctr = [0]

    def load_tile(src_rows, cd, layer, dstx, dsth, tt_):
        k = xin_ctr[0] % 2
        xin_ctr[0] += 1
        dma("sp", XIN[:, k, :], src_rows, semkey=f"xin{k}")
        for half in range(2):
            bk = nb()
            for cc in range(4):
                c = half * 4 + cc
                tr(bk[:, cc * 128:(cc + 1) * 128], XIN[:, k, c * 128:(c + 1) * 128], IDF[:])
            if dstx is not None and os.environ.get("DBG_NOX") != "1":
                if os.environ.get("DBG_NOX") == "2":
                    for cc in range(4):
                        cp("dve", dstx[:, half * 4 + cc, tt_ * 128:(tt_ + 1) * 128], bk[:, cc * 128:(cc + 1) * 128])
                else:
                    cp("dve", dstx[:, half * 4:half * 4 + 4, tt_ * 128:(tt_ + 1) * 128],
                       bk[:, 0:512].rearrange("p (c n) -> p c n", c=4))
            for cc in range(4):
                c = half * 4 + cc
                act(dsth[:, c, tt_ * 128:(tt_ + 1) * 128], bk[:, cc * 128:(cc + 1) * 128], AF.Identity,
                    bias=dc(layer, cd, 1, c), scale=dc(layer, cd, 0, c))

    AXB = ARENA[:, 4096:8192].bitcast(F32).rearrange("p (a n) -> p a n", a=2)
    XINS = [XIN[:, 0, :], XIN[:, 1, :], AXB[:, 0, :], AXB[:, 1, :]]

    def load_group_dma(srcs):
        for t_, src in enumerate(srcs):
            dma("sp", XINS[t_], src, semkey=f"xin{t_}")

    def load_group_chunk(c, ntile, cd, layer, dstx, dsth):
        N = ntile * 128
        bk = nb()
        for t_ in range(ntile):
            tr(bk[:, t_ * 128:(t_ + 1) * 128], XINS[t_][:, c * 128:(c + 1) * 128], IDF[:])
        if dstx is not None:
            cp("dve", dstx[:, c, 0:N], bk[:, 0:N])
        act(dsth[:, c, 0:N], bk[:, 0:N], AF.Identity, bias=dc(layer, cd, 1, c), scale=dc(layer, cd, 0, c))

    def proj_fm(blk, m4, inT, N, bank):
        for kc in range(8):
            mm(bank[:, 0:N], blk[:, kc, m4 * 128:(m4 + 1) * 128], inT[:, kc, 0:N], start=(kc == 0), stop=(kc == 7))

    def proj_tm(blk, inT, tt_, bank):
        for kc in range(8):
            mm(bank[:, 0:512], inT[:, kc, tt_ * 128:(tt_ + 1) * 128], blk[:, kc, 0:512], start=(kc == 0),
               stop=(kc == 7))

    sq_ctr = [0]

    def norm_update(src, N, dstcol):
        k = sq_ctr[0] % 2
        sq_ctr[0] += 1
        act(SQB[:, k, 0:N], src, AF.Square)
        bkn = nb()
        mm(bkn[:, 0:N], BLK1[:], SQB[:, k, 0:N])
        P.op("dve", lambda e: e.reduce_max(dstcol, bkn[:, 0:N], AX.X), [bkn[:, 0:N]], [dstcol])

    def cols_max(cols, dst):
        P.op("dve", lambda e: e.reduce_max(dst, cols, AX.X), [cols], [dst])

    def bound_finalize():
        for j in range(2):
            bkt = nb()
            tr(bkt[0:1, 0:128], MAXC[:, j:j + 1], IDF[:])
            P.op("dve", (lambda e, bkt=bkt, j=j: e.reduce_max(S1[0:1, j:j + 1], bkt[0:1, 0:128], AX.X)),
                 [bkt[0:1, 0:128]], [S1[0:1, j:j + 1]])
        tt("dve", S1[0:1, 2:3], S1[0:1, 0:1], S1[0:1, 1:2], ALU.mult)
        act(S1[0:1, 3:4], S1[0:1, 2:3], AF.Ln)
        act(S1[0:1, 4:5], S1[0:1, 3:4], AF.Exp, scale=0.5)
        ts("dve", S1[0:1, 5:6], S1[0:1, 4:5], -1.01 / 8.0, None, ALU.mult)
        ts("dve", S1[0:1, 6:7], S1[0:1, 4:5], -1.01 / 8.0, None, ALU.mult)

    def bound_broadcast():
        bkb = nb()
        mm(bkb[:, 0:2], ONER[0:1, :], S1[0:1, 5:7])
        cp("dve", NEGC[:, 0:1], bkb[:, 0:1])

    qr_ctr = [0]

    def rope(bank, N, dst, normcol=None):
        k = qr_ctr[0] % 2
        qr_ctr[0] += 1
        cp("act", QRAW[:, k, 0:N], bank[:, 0:N])
        if normcol is not None:
            norm_update(QRAW[:, k, 0:N], N, normcol)
        b2 = nb()
        mm(b2[:, 0:N], PERM[:], QRAW[:, k, 0:N])
        tt("dve", T1[:, k, 0:N], QRAW[:, k, 0:N], TAB[:, 0, 0:N], ALU.mult)
        tt("dve", T3[:, k, 0:N], b2[:, 0:N], TAB[:, 1, 0:N], ALU.mult)
        tt("dve", dst, T1[:, k, 0:N], T3[:, k, 0:N], ALU.add)

    def sgu_norm_tile(bank, tt_):
        k = tt_ % 2
        for gi in range(4):
            P.op("dve", (lambda e, gi=gi: e.bn_stats(ST[:, k, gi, :], bank[:, gi * 128:(gi + 1) * 128])),
                 [bank[:, gi * 128:(gi + 1) * 128]], [ST[:, k, gi, :]])
        for gi in range(4):
            P.op("dve", (lambda e, gi=gi: e.bn_aggr(MV[:, k, gi, :], ST[:, k, gi, :])), [ST[:, k, gi, :]],
                 [MV[:, k, gi, :]])
        act(RS[:, k, 0:4], MV[:, k, :, 1], AF.Ln, bias=LN_EPS)
        act(RS[:, k, 4:8], RS[:, k, 0:4], AF.Exp, scale=-0.5)
        for gi in range(4):
            ts("dve", VC[:, tt_, gi * 128:(gi + 1) * 128], bank[:, gi * 128:(gi + 1) * 128],
               MV[:, k, gi, 0:1], RS[:, k, 4 + gi:5 + gi], ALU.subtract, ALU.mult)

    SBK = [[BANK[0], BANK[1]], [BANK[2], BANK[3]]]
    OBK = [BANK[4], BANK[5], BANK[6]]
    MISCB = BANK[7].bitcast(BF16)
    SPAIR = [PS01, PS23]

    deferred = []
    T2f = T2[:].rearrange("p a n -> p (a n)")
    T3f = T3[:].rearrange("p a n -> p (a n)")
    AEP = T3[:, 1, :].rearrange("p (s n) -> p s n", s=4)

    def osacc(j):
        return T2f[:, j * 129:(j + 1) * 129] if j < 6 else T3f[:, (j - 6) * 129:(j - 5) * 129]

    def run_deferred(n=None):
        k = 0
        while deferred and (n is None or k < n):
            deferred.pop(0)()
            k += 1

    def attention(qT, q0, Nq, ktiles, cat):
        nsub = Nq // 128
        nacc = 2 * nsub
        nk_ = len(ktiles)
        hooks = {4, 9, 14, 19, 24}
        for h in range(4):
            accs = {}
            first_in_bank = {}
            for sub in range(nsub):
                for i in range(2):
                    j = sub * 2 + i
                    bkk = OBK[j // 3]
                    accs[(sub, i)] = bkk[:, (j % 3) * 129:(j % 3) * 129 + 129]
                    first_in_bank[(sub, i)] = (j % 3 == 0)
            for kt in range(nk_ + 1):
                if kt < nk_:
                    Kh = ktiles[kt][0](h)
                    for i in range(2):
                        sbank = SBK[kt % 2][i]
                        mm(sbank[:, 0:Nq], Kh[i * 64:(i + 1) * 64, :], qT[i * 64:(i + 1) * 64, h, q0:q0 + Nq])
                    act(ET[:, (kt % 2) * 2:(kt % 2) * 2 + 2, 0:Nq],
                        SPAIR[kt % 2][:, :].rearrange("p (i n) -> p i n", i=2)[:, :, 0:Nq],
                        AF.Exp, bias=NEGC[:, 0:1], scale=0.125)
                if kt >= 1:
                    k1 = kt - 1
                    Vh = ktiles[k1][1](h)
                    for sub in range(nsub):
                        for i in range(2):
                            mm(accs[(sub, i)], ET[:, (k1 % 2) * 2 + i, sub * 128:(sub + 1) * 128], Vh,
                               start=(k1 == 0 and first_in_bank[(sub, i)]), stop=(k1 == nk_ - 1),
                               skip_group_check=True)
                if kt in hooks:
                    run_deferred(1)
            run_deferred()
            for b_ in range((nacc + 2) // 3):
                n_in = min(3, nacc - 3 * b_)
                dst = T2f[:, b_ * 387:b_ * 387 + n_in * 129] if b_ < 2 else T3f[:, 0:n_in * 129]
                cp("dve", dst, OBK[b_][:, 0:n_in * 129])
            p_ = h % 2
            R = SM[:, p_ * 2, :]
            SS = SM[:, p_ * 2 + 1, 0:4]
            LNV = SM[:, 4 + p_, 0:4]
            RSTD = SM[:, 4 + p_, 4:8]

            def stage1(nsub=nsub, nacc=nacc, R=R, SS=SS):
                n6 = min(nacc, 6)
                lcol = T2f[:, 0:n6 * 129].rearrange("p (j c) -> p j c", c=129)[:, :, 128:129]
                rout = R[:, 0:n6].rearrange("p (j o) -> p j o", o=1)
                P.op("dve", lambda e: e.reciprocal(rout, lcol), [lcol], [rout])
                if nacc > 6:
                    lcol2 = T3f[:, 0:258].rearrange("p (j c) -> p j c", c=129)[:, :, 128:129]
                    rout2 = R[:, 6:8].rearrange("p (j o) -> p j o", o=1)
                    P.op("dve", lambda e: e.reciprocal(rout2, lcol2), [lcol2], [rout2])
                Rv = R[:, 0:nacc].rearrange("p (s i) -> p s i", i=2)
                ts("dve", Rv[:, :, 1], Rv[:, :, 1], NEGLAM, None, ALU.mult)
                for sub in range(nsub):
                    ts("dve", AEP[:, sub, :], osacc(2 * sub)[:, 0:128], R[:, 2 * sub:2 * sub + 1], None, ALU.mult)
                    stt(AEP[:, sub, :], osacc(2 * sub + 1)[:, 0:128], R[:, 2 * sub + 1:2 * sub + 2], AEP[:, sub, :],
                        ALU.mult, ALU.add)
                    P.op("dve", (lambda e, sub=sub: e.scalar_tensor_tensor(JUNK[:], AEP[:, sub, :], 1.0, AEP[:, sub, :],
                                                                           ALU.mult, ALU.mult, accum_out=SS[:, sub:sub + 1])),
                         [AEP[:, sub, :]], [JUNK[:], SS[:, sub:sub + 1]])

            def stage2(nsub=nsub, SS=SS, LNV=LNV, RSTD=RSTD):
                act(LNV[:, 0:nsub], SS[:, 0:nsub], AF.Ln, bias=LN_EPS, scale=1.0 / 128.0)
                act(RSTD[:, 0:nsub], LNV[:, 0:nsub], AF.Exp, scale=-0.5)

            def stage3(nsub=nsub, RSTD=RSTD):
                for sub in range(nsub):
                    stt(AN[:, sub, :], AEP[:, sub, :], RSTD[:, sub:sub + 1], GSUB[:], ALU.mult, ALU.mult)

            def stage4(nsub=nsub):
                for sub in range(nsub):
                    tr(MISCB[:, sub * 128:(sub + 1) * 128], AN[:, sub, :], IDB[:])

            def stage5(h=h, q0=q0, Nq=Nq, cat=cat):
                cp("dve", cat[:, h, q0:q0 + Nq], MISCB[:, 0:Nq])

            deferred.extend([stage1, stage2, stage3, stage4, stage5])

    def sgu_mix(N, cat):
        ntile = N // 128
        for gi in range(4):
            bk = nb()
            for t_ in range(ntile):
                mm(bk[:, t_ * 128:(t_ + 1) * 128], VC[:, t_, gi * 128:(gi + 1) * 128], WST[:, gi, :],
                   start=True, stop=False)
                mm(bk[:, t_ * 128:(t_ + 1) * 128], ONERB[0:1, :], BSG2[0:1, 0, gi * 128:(gi + 1) * 128],
                   start=False, stop=False)
                mm(bk[:, t_ * 128:(t_ + 1) * 128], ONERB[0:1, :], BSG2[0:1, 1, gi * 128:(gi + 1) * 128],
                   start=False, stop=True)
            tt("dve", cat[:, 4 + gi, 0:N], bk[:, 0:N], UT[:, gi, 0:N], ALU.mult)

    pending_release = []

    class StatAcc:
        def __init__(self, XS, N):
            self.XS, self.N = XS, N
            while pending_release:
                pending_release.pop().release()
            self.im, self.ie = nb_reserve(), nb_reserve()
            self.Bm, self.Be = BANK[self.im], BANK[self.ie]
            self.pending = []
            self.n = 0

        def add(self, c):
            self.pending.append(c)
            if len(self.pending) > 1:
                self._emit(self.pending.pop(0))

        def _emit(self, c):
            N, XS = self.N, self.XS
            k = self.n % 2
            act(SQB[:, k, 0:N], XS[:, c, 0:N], AF.Square)
            cp("act", RB[:, k, 0:N], XS[:, c, 0:N])
            mm(self.Bm[:, 0:N], ONESB[:], RB[:, k, 0:N], start=(self.n == 0), stop=(self.n == 7))
            mm(self.Be[:, 0:N], ONESB[:], SQB[:, k, 0:N], start=(self.n == 0), stop=(self.n == 7))
            self.n += 1

        def finish(self):
            while self.pending:
                self._emit(self.pending.pop(0))
            assert self.n == 8

        def release(self):
            reserved.discard(self.im)
            reserved.discard(self.ie)

    def out_proj(name, cat, XS, N, layer, cd):
        st = StatAcc(XS, N)
        for ob in range(2):
            blk = wload(name, ob)
            for m4 in range(4):
                m = ob * 4 + m4
                bk = nb()
                proj_fm(blk, m4, cat, N, bk)
                stt(XS[:, m, 0:N], bk[:, 0:N], dc(layer, cd, 2, m), XS[:, m, 0:N], ALU.mult, ALU.add)
                st.add(m)
        return st

    def layernorm(XS, N, layer, cd, which, want_h, st):
        gname = f"ln_{which}_g{layer}"
        bname = f"ln_{which}_b{layer}"
        st.finish()
        Bm, Be = st.Bm, st.Be
        cp("act", T2[:, 0, 0:N], Bm[:, 0:N])
        tt("dve", T2[:, 1, 0:N], T2[:, 0, 0:N], Bm[:, 0:N], ALU.mult)
        tt("dve", T2[:, 1, 0:N], Be[:, 0:N], T2[:, 1, 0:N], ALU.subtract)
        act(T2[:, 1, 0:N], T2[:, 1, 0:N], AF.Ln, bias=EPS_R)
        act(Be[:, 0:N], T2[:, 1, 0:N], AF.Exp, scale=-0.5)
        for c in range(8):
            tt("dve", XS[:, c, 0:N], XS[:, c, 0:N], Bm[:, 0:N], ALU.subtract)
            tt("dve", XS[:, c, 0:N], XS[:, c, 0:N], Be[:, 0:N], ALU.mult)
            if want_h:
                act(HT[:, c, 0:N], XS[:, c, 0:N], AF.Identity, bias=dc(layer, cd, 4, c), scale=dc(layer, cd, 3, c))
            act(XS[:, c, 0:N], XS[:, c, 0:N], AF.Identity, bias=col2(bname, c), scale=col2(gname, c))
        pending_release.append(st)

    t_ctr = [0]

    def ffn(XS, N, layer, cd):
        n1, n2 = f"w_ff1_{layer}", f"w_ff2_{layer}"
        st = None
        for hf in range(2):
            if hf == 1:
                st = StatAcc(XS, N)
            for j4 in range(4):
                blk = wload(n1, hf * 4 + j4)
                first = (hf == 0 and j4 == 0)
                if first:
                    bks = [nb() for _ in range(4)]
                    for kc in range(8):
                        for m4 in range(4):
                            mm(bks[m4][:, 0:N], blk[:, kc, m4 * 128:(m4 + 1) * 128], HT[:, kc, 0:N],
                               start=(kc == 0), stop=(kc == 7))
                for m4 in range(4):
                    if first:
                        bk = bks[m4]
                    else:
                        bk = nb()
                        proj_fm(blk, m4, HT, N, bk)
                    k = t_ctr[0] % 2
                    t_ctr[0] += 1
                    cp("act", T1[:, k, 0:N], bk[:, 0:N])
                    stt(HID[:, j4 * 4 + m4, 0:N], bk[:, 0:N], 0.0, T1[:, k, 0:N], ALU.max, ALU.mult)
            for mb in range(4):
                blk = wload(n2, hf * 4 + mb)
                for m2 in range(2):
                    m = mb * 2 + m2
                    bk = nb()
                    for kc in range(16):
                        mm(bk[:, 0:N], blk[:, kc, m2 * 128:(m2 + 1) * 128], HID[:, kc, 0:N], start=(kc == 0),
                           stop=(kc == 15))
                    stt(XS[:, m, 0:N], bk[:, 0:N], dc(layer, cd, 5, m), XS[:, m, 0:N], ALU.mult, ALU.add)
                    if hf == 1:
                        st.add(m)
        return st

    stg_ctr = [0]

    def store_tiles(XS, ntile, dst_rows_fn):
        for t_ in range(ntile):
            k = stg_ctr[0] % 2
            stg_ctr[0] += 1
            for half in range(2):
                bk = nb()
                for cc in range(4):
                    tr(bk[:, cc * 128:(cc + 1) * 128], XS[:, half * 4 + cc, t_ * 128:(t_ + 1) * 128], IDF[:])
                cp("act" if half == 0 else "dve", STG[:, k, half * 512:(half + 1) * 512], bk[:, 0:512])
            dma("pool", dst_rows_fn(t_), STG[:, k, :], semkey=f"stg{k}")

    def layer0_group(src_fn, ntile, cd, slot, sample, tab0, ktiles_fn, kvout_fn, pre_ln2=None):
        N = ntile * 128
        XS = XR[:, slot]
        load_group_dma([src_fn(t_) for t_ in range(ntile)])
        for c in range(8):
            load_group_chunk(c, ntile, cd, 0, XS, HT)
        if sample:
            dma("sp", TAB[:, 0, 0:N], cosT[:, tab0:tab0 + N], semkey="tabc")
            dma("sp", TAB[:, 1, 0:N], sinT[:, tab0:tab0 + N], semkey="tabs")
        if stop < 3.02:
            return
        blk = wload("w_in0", 0)
        for m4 in range(4):
            bk = nb()
            proj_fm(blk, m4, HT, N, bk)
            if sample:
                rope(bk, N, QT[:, m4, 0:N], QCOLS[:, m4:m4 + 1])
            else:
                cp("act", QT[:, m4, 0:N], bk[:, 0:N])
                norm_update(QT[:, m4, 0:N], N, QCOLS[:, m4:m4 + 1])
        cols_max(QCOLS[:, 0:4], MAXC[:, 0:1])
        if not sample:
            blk = wload("w_in0", 1)
            for m4 in range(4):
                bk = nb()
                proj_fm(blk, m4, HT, N, bk)
                cp("act", KTP[:, m4, 0:N], bk[:, 0:N])
                norm_update(KTP[:, m4, 0:N], N, KCOLS[:, m4:m4 + 1])
            cols_max(KCOLS[:, 0:4], MAXC[:, 1:2])
            blkv = wload("w_in0", 2)
            for t_ in range(ntile):
                k = stg_ctr[0] % 2
                stg_ctr[0] += 1
                bk = nb()
                proj_tm(blk, HT, t_, bk)
                cp("dve", STG[:, k, 0:512], bk[:, 0:512])
                bk2 = nb()
                proj_tm(blkv, HT, t_, bk2)
                cp("act", STG[:, k, 512:1024], bk2[:, 0:512])
                cp("dve", VV[:, t_, :].rearrange("p (h c) -> p h c", h=4)[:, :, 0:128],
                   bk2[:, 0:512].rearrange("p (h c) -> p h c", h=4))
                kd, vd_ = kvout_fn(t_)
                dma("pool", kd, STG[:, k, 0:512], semkey=f"stg{k}")
                dma("pool", vd_, STG[:, k, 512:1024], semkey=f"stg{k}")
        bound_finalize()
        if stop < 3.03:
            return
        blk = wload("w_in0", 3)
        for m4 in range(4):
            bk = nb()
            proj_fm(blk, m4, HT, N, bk)
            cp("act" if m4 % 2 == 0 else "dve", UT[:, m4, 0:N], bk[:, 0:N])
        if stop < 3.04:
            return
        blk = wload("w_in0", 4)
        for t_ in range(ntile):
            bk = nb()
            proj_tm(blk, HT, t_, bk)
            if stop >= 3.05:
                sgu_norm_tile(bk, t_)
        if stop < 3.2:
            return
        bound_broadcast()
        for (q0, Nq, kts) in ktiles_fn(N):
            attention(QT, q0, Nq, kts, HT)
        run_deferred()
        if stop < 3.3:
            return
        sgu_mix(N, HT)
        if stop < 3.4:
            return
        st = out_proj("w_out0", HT, XS, N, 0, cd)
        layernorm(XS, N, 0, cd, "mix", True, st)
        st = ffn(XS, N, 0, cd)
        if pre_ln2 is not None:
            pre_ln2()
        layernorm(XS, N, 0, cd, "ff", False, st)

    def l1_prologue(ntile, cd, slot):
        N = ntile * 128
        XS = XR[:, slot]
        for c in range(8):
            act(HT[:, c, 0:N], XS[:, c, 0:N], AF.Identity, bias=dc(1, cd, 1, c), scale=dc(1, cd, 0, c))

    def layer1_group(ntile, cd, slot, segs, xnext, use_prev, dst_rows_fn, skip_prologue=False):
        N = ntile * 128
        XS = XR[:, slot]
        if not skip_prologue:
            l1_prologue(ntile, cd, slot)
        if xnext is not None:
            for c in range(8):
                act(HX[:, c, 0:1], xnext[:, c, 0:1], AF.Identity, bias=dc(1, cd, 1, c), scale=dc(1, cd, 0, c))
        for half in range(2):
            bgb = wload("w_in1", 0 + half)
            cgb = wload("w_in1", 2 + half)
            xtb = wload("w_in1", 4 + half)
            for m4 in range(4):
                m = half * 4 + m4
                k = m % 2
                bC, bX, bB = nb(), nb(), nb()
                if half == 0 and m4 == 0:
                    for kc in range(8):
                        for (wb_, bk_) in ((cgb, bC), (xtb, bX), (bgb, bB)):
                            mm(bk_[:, 0:N], wb_[:, kc, 0:128], HT[:, kc, 0:N], start=(kc == 0), stop=(kc == 7))
                else:
                    proj_fm(cgb, m4, HT, N, bC)
                    proj_fm(xtb, m4, HT, N, bX)
                    proj_fm(bgb, m4, HT, N, bB)
                cp("act", T1[:, k, 0:N], bC[:, 0:N])
                Z = T3[:, k, 0:N]
                C = T2[:, k, 0:N]
                tt("dve", Z, bX[:, 0:N], T1[:, k, 0:N], ALU.mult)
                ts("dve", C, Z, col2("conv", 8 + m), None, ALU.mult)
                for (a, b) in segs:
                    stt(C[:, a + 1:b], Z[:, a:b - 1], col2("conv", 0 + m), C[:, a + 1:b], ALU.mult, ALU.add)
                    stt(C[:, a:b - 1], Z[:, a + 1:b], col2("conv", 16 + m), C[:, a:b - 1], ALU.mult, ALU.add)
                if use_prev:
                    stt(C[:, 0:1], ZSAVE[:, m:m + 1], col2("conv", 0 + m), C[:, 0:1], ALU.mult, ALU.add)
                    cp("dve", ZSAVE[:, m:m + 1], Z[:, N - 1:N])
                if xnext is not None:
                    cp("dve", HAL[:, 0, m:m + 1], C[:, N - 1:N])
                    cp("dve", HAL[:, 1, m:m + 1], bB[:, N - 1:N])
                tt("dve", CAT1[:, m, 0:N], bB[:, 0:N], C, ALU.mult)
            if xnext is not None:
                m0 = half * 4
                bH = nb()
                for m4 in range(4):
                    for kc in range(8):
                        mm(bH[:, 4 * m4:4 * m4 + 1], cgb[:, kc, m4 * 128:(m4 + 1) * 128], HX[:, kc, 0:1],
                           start=(kc == 0), stop=(kc == 7), skip_group_check=True)
                    for kc in range(8):
                        mm(bH[:, 4 * m4 + 2:4 * m4 + 3], xtb[:, kc, m4 * 128:(m4 + 1) * 128], HX[:, kc, 0:1],
                           start=(kc == 0), stop=(kc == 7), skip_group_check=True)
                bHv = bH[:, 0:16].rearrange("p (m c) -> p m c", c=4)
                cp("act", HAL[:, 2, m0:m0 + 4], bHv[:, :, 0])
                tt("dve", HAL[:, 3, m0:m0 + 4], bHv[:, :, 2], HAL[:, 2, m0:m0 + 4], ALU.mult)
                o_ = vb2_off["conv"] + 16 + m0
                tt("dve", HAL[:, 3, m0:m0 + 4], HAL[:, 3, m0:m0 + 4], COLS2[:, o_:o_ + 4], ALU.mult)
                tt("dve", HAL[:, 3, m0:m0 + 4], HAL[:, 3, m0:m0 + 4], HAL[:, 0, m0:m0 + 4], ALU.add)
                tt("dve", CAT1[:, m0:m0 + 4, N - 1], HAL[:, 3, m0:m0 + 4], HAL[:, 1, m0:m0 + 4], ALU.mult)
        st = out_proj("w_out1", CAT1, XS, N, 1, cd)
        layernorm(XS, N, 1, cd, "mix", True, st)
        st = ffn(XS, N, 1, cd)
        layernorm(XS, N, 1, cd, "ff", False, st)
        store_tiles(XS, ntile, dst_rows_fn)

    HTB = [HT, CAT1]

    def kv_srcs(g):
        return [xs[(g * 4 + t_) * 128:(g * 4 + t_ + 1) * 128, :] for t_ in range(4)]

    def kv_tab(g):
        dma("sp", TAB[:, 0, :], cosT[:, g * 512:(g + 1) * 512], semkey="tabc")
        dma("sp", TAB[:, 1, :], sinT[:, g * 512:(g + 1) * 512], semkey="tabs")

    NKV = 8 if stop >= 2 else 0
    if NKV:
        load_group_dma(kv_srcs(0))
        for c in range(8):
            load_group_chunk(c, 4, 0, 0, None, HTB[0])
        load_group_dma(kv_srcs(1))
        kv_tab(0)
        blk_k = wload("w_in0", 1)
        sk_ = (ws_ctr[0] - 1) % 4
        blk_v = wload("w_in0", 2)
        sv_ = (ws_ctr[0] - 1) % 4
        ws_pin.update([sk_, sv_])
        ibm2 = nb_reserve()
        bm2 = BANK[ibm2]
    for kvg in range(NKV):
        H = HTB[kvg % 2]
        for m4 in range(4):
            bk = nb()
            proj_fm(blk_k, m4, H, 512, bk)
            rope(bk, 512, KT[:, m4, kvg * 512:(kvg + 1) * 512], KCOLS[:, kvg * 4 + m4:kvg * 4 + m4 + 1])
            if kvg + 1 < NKV:
                for c in (2 * m4, 2 * m4 + 1):
                    load_group_chunk(c, 4, 0, 0, None, HTB[(kvg + 1) % 2])
        if kvg + 1 < NKV:
            kv_tab(kvg + 1)
        if kvg + 2 < NKV:
            load_group_dma(kv_srcs(kvg + 2))
        mod_blocks(0, [4 + kvg], bm2)
        for t_ in range(4):
            bk = nb()
            proj_tm(blk_v, H, t_, bk)
            cp("act" if t_ % 2 == 0 else "dve",
               VV[:, kvg * 4 + t_, :].rearrange("p (h c) -> p h c", h=4)[:, :, 0:128],
               bk[:, 0:512].rearrange("p (h c) -> p h c", h=4))
    if NKV:
        ws_pin.clear()
    for j in range(2 if stop >= 2 else 0):
        k = xin_ctr[0] % 2
        xin_ctr[0] += 1
        dma("sp", XIN[:, k, 0:512], ck[j * 128:(j + 1) * 128, :], semkey=f"xin{k}")
        bk = nb()
        for m4 in range(4):
            tr(bk[:, m4 * 128:(m4 + 1) * 128], XIN[:, k, m4 * 128:(m4 + 1) * 128], IDF[:])
        cp("dve", KT[:, :, (32 + j) * 128:(33 + j) * 128], bk[:, 0:512].rearrange("p (c n) -> p c n", c=4))
        for m4 in range(4):
            norm_update(KT[:, m4, (32 + j) * 128:(33 + j) * 128], 128, KCOLS[:, 32 + j * 4 + m4:33 + j * 4 + m4])
        dma("pool", VV[:, 32 + j, :].rearrange("p (h c) -> p h c", h=4)[:, :, 0:128],
            cv[j * 128:(j + 1) * 128, :].rearrange("p (h c) -> p h c", h=4), semkey=f"cvld{j}")

    if stop >= 2:
        cols_max(KCOLS[:, 0:40], MAXC[:, 1:2])
    if NKV:
        modulation(0, [4, 5, 6, 7, 8, 9, 10, 11], "b", bm=bm2)
        reserved.discard(ibm2)
    else:
        modulation(0, [4, 5, 6, 7, 8, 9, 10, 11], "b")
    cast_all(["w_out0", "w_ff1_0", "w_ff2_0"])

    def sample_ktiles(N):
        kts = []
        for kt in range(34):
            kts.append(((lambda h, kt=kt: KT[:, h, kt * 128:(kt + 1) * 128]),
                        (lambda h, kt=kt: VV[:, kt, h * 129:(h + 1) * 129])))
        return [(0, N, kts)]

    groups = [(0, 4), (4, 4), (8, 4), (12, 3), (15, 2)]

    def s_l0(g):
        t0, nt = groups[g]
        hook = None
        if g >= 2:
            hook = lambda: l1_prologue(groups[g - 1][1], 0, (g - 1) % 2)
        layer0_group(lambda t_: xs[(t0 + t_) * 128:(t0 + t_ + 1) * 128, :], nt, 0, g % 2, True, t0 * 128,
                     sample_ktiles, None, pre_ln2=hook)

    def s_l1(g):
        t0, nt = groups[g]
        xnext = XR[:, (g + 1) % 2] if g + 1 < len(groups) else None
        hoisted = (g >= 1 and g + 1 < len(groups))
        layer1_group(nt, 0, g % 2, [(0, nt * 128)], xnext, True,
                     lambda t_: ys[(t0 + t_) * 128:(t0 + t_ + 1) * 128, :], skip_prologue=hoisted)

    if stop >= 3:
        s_l0(0)
    cast_all(["w_in1", "w_out1", "w_ff1_1", "w_ff2_1"])
    if stop >= 4:
        for g in range(1, len(groups)):
            s_l0(g)
            if g == 1:
                modulation(1, list(range(12)), "ab")
            s_l1(g - 1)
        s_l1(len(groups) - 1)

    def prompt_ktiles(N):
        res = []
        for bi in range(2):
            kts = []
            for j in range(2):
                kt = bi * 2 + j
                kts.append(((lambda h, kt=kt: KTP[:, h, kt * 128:(kt + 1) * 128]),
                            (lambda h, kt=kt: VV[:, kt, h * 129:(h + 1) * 129])))
            res.append((bi * 256, 256, kts))
        return res

    for pg in range(2 if stop >= 5 else 0):
        r0 = pg * 512
        layer0_group(lambda t_: xp[r0 + t_ * 128: r0 + (t_ + 1) * 128, :], 4, 1, 0, False, 0, prompt_ktiles,
                     lambda t_: (nk[r0 + t_ * 128: r0 + (t_ + 1) * 128, :], nv[r0 + t_ * 128: r0 + (t_ + 1) * 128, :]))
        layer1_group(4, 1, 0, [(0, 256), (256, 512)], None, False,
                     lambda t_: yp[r0 + t_ * 128: r0 + (t_ + 1) * 128, :])

    ops = P.ops
    for o in ops:
        for d in o.deps:
            ops[d].has_dep = True
    ENG = ["pe", "act", "dve", "pool", "sp"]
    semkeys = sorted({o.semkey for o in ops if o.dma})
    sems = {}
    for e_ in ENG:
        sems[e_] = es.enter_context(nc.semaphore("s_" + e_))
    for k in semkeys:
        sems["d_" + k] = es.enter_context(nc.semaphore("d_" + k))
    ecount = {e_: 0 for e_ in ENG}
    dcount = {k: 0 for k in semkeys}
    for o in ops:
        if o.dma:
            dcount[o.semkey] += 16
            o.cnt = dcount[o.semkey]
        elif o.has_dep:
            ecount[o.eng] += 1
            o.cnt = ecount[o.eng]
    dtotal = dict(dcount)
    out_keys = [k for k in semkeys if k.startswith("stg")]

    block = es.enter_context(nc.Block())

    def emit_engine(ename, e):
        waited = {}
        for o in ops:
            if o.eng != ename:
                continue
            need = {}
            for d in o.deps:
                p = ops[d]
                if p.dma:
                    sk = "d_" + p.semkey
                    val = dtotal[p.semkey] if p.whole else p.cnt
                else:
                    if p.eng == "pe" and ename == "pe":
                        continue
                    sk = p.eng
                    val = p.cnt
                if val > need.get(sk, 0):
                    need[sk] = val
            for sk, val in need.items():
                if waited.get(sk, 0) >= val:
                    continue
                e.wait_ge(sems[sk], val)
                waited[sk] = val
            ins = o.fn(e)
            if o.dma:
                ins.then_inc(sems["d_" + o.semkey], 16)
            elif o.has_dep:
                ins.then_inc(sems[ename], 1)
        if ename == "pool":
            for k in out_keys:
                e.wait_ge(sems["d_" + k], dtotal[k])

    @block.tensor
    def _(e):
        emit_engine("pe", e)

    @block.scalar
    def _(e):
        emit_engine("act", e)

    @block.vector
    def _(e):
        emit_engine("dve", e)

    @block.gpsimd
    def _(e):
        emit_engine("pool", e)

    @block.sync
    def _(e):
        emit_engine("sp", e)

    es.close()
    return nc


_NC_CACHE = {}


def _rope_tables(order):
    pos = (np.asarray(order)[:, None] * 128 + np.arange(128)[None, :]).reshape(-1)
    row = (pos // 64).astype(np.float32)
    col = (pos % 64).astype(np.float32)
    inv = (1.0 / (np.float32(10000.0) ** (np.arange(16, dtype=np.float32) / np.float32(16)))).astype(np.float32)
    ang = [row[:, None] * inv[None, :], col[:, None] * inv[None, :]]
    cosT = np.zeros((128, 4096), np.float32)
    sinT = np.zeros((128, 4096), np.float32)
    for p in range(128):
        pm = p % 64
        s = pm // 32
        j = (pm % 32) // 16
        f = pm % 16
        cosT[p] = np.cos(ang[s][:, f])
        sinT[p] = np.sin(ang[s][:, f]) * (-1.0 if j == 0 else 1.0)
    return cosT, sinT


def kernel(**inp):
    f = lambda a: np.ascontiguousarray(np.asarray(a, dtype=np.float32))
    x_prompt, x_sample = f(inp["x_prompt"]), f(inp["x_sample"])
    cache_k0, cache_v0 = f(inp["cache_k0"]), f(inp["cache_v0"])
    c, c_ctx = f(inp["c"]), f(inp["c_ctx"])
    if "nc" not in _NC_CACHE:
        _NC_CACHE["nc"] = build_nc()
    nc = _NC_CACHE["nc"]
    ident = np.eye(128, dtype=np.float32)
    perm = np.zeros((128, 128), np.float32)
    for m in range(128):
        perm[m ^ 16, m] = 1.0
    shared = {"c_ident": ident, "c_perm": perm}
    for name, _, _ in W_SPECS:
        shared[name] = f(inp[name])
    for name in ("w_mod0", "w_mod1", "conv_w1", "subln_g0", "sgu_w0", "sgu_b0", "lambda_q1_0", "lambda_k1_0",
                 "lambda_q2_0", "lambda_k2_0"):
        shared[name] = f(inp[name])
    for name, _ in VEC_IN:
        shared[name] = f(inp[name])
    in_maps = []
    orders = []
    for core in range(8):
        b, half = core // 2, core % 2
        win = list(range(0, 17)) if half == 0 else list(range(15, 32))
        others = [t for t in range(32) if t not in win]
        order = win + others
        orders.append(order)
        xt = x_sample[b].reshape(32, 128, 1024)[order].reshape(4096, 1024)
        cosT, sinT = _rope_tables(order)
        m = dict(shared)
        m["xs"] = np.ascontiguousarray(xt)
        m["xp"] = np.ascontiguousarray(x_prompt[4 * core:4 * core + 4].reshape(1024, 1024))
        m["ck"] = np.ascontiguousarray(cache_k0[b].reshape(256, 512))
        m["cv"] = np.ascontiguousarray(cache_v0[b].reshape(256, 512))
        m["cvec"] = np.ascontiguousarray(np.stack([c[b], c_ctx], 0))
        m["cosT"] = cosT
        m["sinT"] = sinT
        in_maps.append(m)
    res = run_bass_kernel_spmd(nc, in_maps, core_ids=list(range(8)))
    y_prompt = np.zeros((32, 256, 1024), np.float32)
    y_sample = np.zeros((4, 4096, 1024), np.float32)
    new_k = np.zeros((32, 256, 4, 2, 64), np.float32)
    new_v = np.zeros((32, 256, 4, 128), np.float32)
    for core in range(8):
        r = res.results[core]
        b, half = core // 2, core % 2
        ysc = np.asarray(r["ys"])
        if half == 0:
            y_sample[b, 0:2048] = ysc[0:2048]
        else:
            y_sample[b, 2048:4096] = ysc[128:2176]
        y_prompt[4 * core:4 * core + 4] = np.asarray(r["yp"]).reshape(4, 256, 1024)
        new_k[4 * core:4 * core + 4] = np.asarray(r["nk"]).reshape(4, 256, 4, 2, 64)
        new_v[4 * core:4 * core + 4] = np.asarray(r["nv"]).reshape(4, 256, 4, 128)
    return (y_prompt, y_sample, new_k, new_v)
```

```python
import math
import os
from contextlib import ExitStack

import numpy as np
import concourse.bass as bass
import concourse.mybir as mybir
from concourse.bass_utils import run_bass_kernel_spmd

F32 = mybir.dt.float32
BF16 = mybir.dt.bfloat16
AF = mybir.ActivationFunctionType
ALU = mybir.AluOpType
AX = mybir.AxisListType

D = 1024
ALPHA = 4.0 ** 0.25
LAMBDA_INIT = 0.8 - 0.6 * math.exp(-0.3 * 0)
LN_EPS = 1e-5
EPS_R = LN_EPS / (ALPHA * ALPHA)
NWIN = 17
DSZ = {F32: 4, BF16: 2}


class Op:
    __slots__ = ("eng", "fn", "deps", "dma", "semkey", "whole", "has_dep", "cnt")

    def __init__(self, eng, fn, dma, semkey, whole):
        self.eng, self.fn, self.dma, self.semkey, self.whole = eng, fn, dma, semkey, whole
        self.deps = set()
        self.has_dep = False
        self.cnt = 0


def _region(ap):
    name = ap.tensor.name
    es = DSZ.get(ap.dtype, 4)
    dims = list(ap.ap)
    off = ap.offset
    if str(ap.space) == "DRAM":
        span = sum((c - 1) * abs(s) for s, c in dims)
        return (name, 0, 1, off * es, (off + span + 1) * es)
    ps, pc = dims[0]
    ps = max(ps, 1)
    p0 = off // ps
    f0 = off % ps
    span = sum((c - 1) * abs(s) for s, c in dims[1:])
    if str(ap.space) == "PSUM":
        return (name, 0, 128, (f0 * es) // 2048 * 2048, ((f0 + span + 1) * es + 2047) // 2048 * 2048)
    return (name, p0, p0 + pc, f0 * es, (f0 + span + 1) * es)


class Prog:
    def __init__(self):
        self.ops = []
        self.recs = {}

    def _rkey(self, idx):
        o = self.ops[idx]
        return (o.eng, o.semkey)

    def _touch(self, idx, ap, write):
        name, p0, p1, lo, hi = _region(ap)
        lst = self.recs.setdefault(name, [])
        deps = self.ops[idx].deps
        keep = []
        for r in lst:
            rp0, rp1, rlo, rhi, w, rd = r
            if rp1 <= p0 or p1 <= rp0 or rhi <= lo or hi <= rlo:
                keep.append(r)
                continue
            if w is not None:
                deps.add(w)
            if write:
                deps.update(rd.values())
                if p0 <= rp0 and rp1 <= p1 and lo <= rlo and rhi <= hi:
                    continue
            keep.append(r)
        if write:
            keep.append([p0, p1, lo, hi, idx, {}])
        else:
            done = False
            for r in keep:
                if r[0] <= p0 and p1 <= r[1] and r[2] <= lo and hi <= r[3]:
                    r[5][self._rkey(idx)] = idx
                    done = True
                    break
            if not done:
                keep.append([p0, p1, lo, hi, None, {self._rkey(idx): idx}])
        self.recs[name] = keep

    def op(self, eng, fn, reads=(), writes=(), dma=False, semkey=None, whole=False):
        idx = len(self.ops)
        self.ops.append(Op(eng, fn, dma, semkey, whole))
        for a in reads:
            if a is not None and not isinstance(a, (int, float)):
                self._touch(idx, a, str(a.space) == "PSUM")
        for a in writes:
            self._touch(idx, a, True)
        o = self.ops[idx]
        o.deps.discard(idx)
        return idx


W_SPECS = [
    ("w_in0", 1024, 2560), ("w_out0", 1024, 1024), ("w_ff1_0", 1024, 4096), ("w_ff2_0", 4096, 1024),
    ("w_in1", 1024, 3072), ("w_out1", 1024, 1024), ("w_ff1_1", 1024, 4096), ("w_ff2_1", 4096, 1024),
]
VEC_IN = [("b_mod0", 6144), ("b_mod1", 6144), ("ln_mix_g0", 1024), ("ln_mix_b0", 1024), ("ln_ff_g0", 1024),
          ("ln_ff_b0", 1024), ("ln_mix_g1", 1024), ("ln_mix_b1", 1024), ("ln_ff_g1", 1024), ("ln_ff_b1", 1024)]


def build_nc(stop=99):
    nc = bass.Bass("TRN2", target_bir_lowering=False)
    P = Prog()
    es = ExitStack()

    def din(name, shape, dt=F32):
        return nc.dram_tensor(name, list(shape), dt, kind="ExternalInput").ap()

    def dout(name, shape, dt=F32):
        return nc.dram_tensor(name, list(shape), dt, kind="ExternalOutput").ap()

    xs = din("xs", [4096, 1024])
    xp = din("xp", [1024, 1024])
    ck = din("ck", [256, 512])
    cv = din("cv", [256, 512])
    cvec = din("cvec", [2, 1024])
    cosT = din("cosT", [128, 4096])
    sinT = din("sinT", [128, 4096])
    c_ident = din("c_ident", [128, 128])
    c_perm = din("c_perm", [128, 128])
    wd = {}
    for name, K, ncol in W_SPECS:
        wd[name] = din(name, [K, ncol])
    wd["w_mod0"] = din("w_mod0", [1024, 6144])
    wd["w_mod1"] = din("w_mod1", [1024, 6144])
    vd = {n: din(n, [ln]) for n, ln in VEC_IN}
    conv_w1 = din("conv_w1", [3, 1024])
    lam_in = {n: din(n, [64]) for n in ("lambda_q1_0", "lambda_k1_0", "lambda_q2_0", "lambda_k2_0")}
    subln_g0 = din("subln_g0", [128])
    sgu_w0 = din("sgu_w0", [4, 128, 128])
    sgu_b0 = din("sgu_b0", [4, 128])

    ys = dout("ys", [NWIN * 128, 1024])
    yp = dout("yp", [1024, 1024])
    nk = dout("nk", [1024, 512])
    nv = dout("nv", [1024, 512])

    blk_of = {}
    nblk = 0
    for name, K, ncol in W_SPECS:
        if K == 1024:
            n = ncol // 512
        else:
            n = 8
        blk_of[name] = (nblk, n)
        nblk += n
    wscr = nc.dram_tensor("wscr", [nblk, 128, 4096], BF16, kind="Internal").ap()

    def sb(name, shape, dt):
        return es.enter_context(nc.sbuf_tensor(name, list(shape), dt))

    KT = sb("KT", [128, 4, 4352], BF16)
    VV = sb("VV", [128, 34, 516], BF16)
    XR = sb("XR", [128, 2, 8, 512], F32)
    XIN = sb("XIN", [128, 2, 1024], F32)
    STG = sb("STG", [128, 2, 1024], F32)
    HT = sb("HT", [128, 8, 512], BF16)
    HX = sb("HX", [128, 8, 2], BF16)
    TAB = sb("TAB", [128, 2, 512], F32)
    T1 = sb("T1", [128, 2, 512], F32)
    T2 = sb("T2", [128, 2, 512], F32)
    T3 = sb("T3", [128, 2, 512], F32)
    QRAW = T2
    RB = T1[:, 0, :].bitcast(BF16).rearrange("p (a n) -> p a n", a=2)
    ARENA = sb("ARENA", [128, 10240], BF16)
    WS = sb("WS", [128, 4, 4096], BF16)
    IDF = sb("IDF", [128, 128], F32)
    IDB = sb("IDB", [128, 128], BF16)
    PERM = sb("PERM", [128, 128], F32)
    WST = sb("WST", [128, 4, 128], BF16)
    GSUB = sb("GSUB", [128, 128], F32)
    VB1 = T2[:, 0, 0:128]
    VB2 = T2[:, 1, 0:128]
    COLS1 = sb("COLS1", [128, 96], F32)
    COLS2 = sb("COLS2", [128, 104], F32)
    SIL = sb("SIL", [128, 8, 2], BF16)
    MODT = sb("MODT", [128, 2, 48, 2], F32)
    NVD = 8
    DCOL = sb("DCOL", [128, 2, 2, NVD, 8], F32)
    BSGU = sb("BSGU", [1, 512], F32)
    BSG2 = sb("BSG2", [1, 2, 512], BF16)
    ONERB = sb("ONERB", [1, 128], BF16)
    ONER = sb("ONER", [1, 128], F32)
    LAMT = T3[:, 0, 0:256].rearrange("p (a b) -> p a b", a=4)
    LAMS = sb("LAMS", [128, 8], F32)
    NEGC = sb("NEGC", [128, 1], F32)
    JUNK = sb("JUNK", [128, 128], F32)
    AN = sb("AN", [128, 4, 128], BF16)
    SM = sb("SM", [128, 8, 8], F32)
    ST = sb("ST", [128, 4, 4, 6], F32)
    MV = sb("MV", [128, 4, 4, 2], F32)
    RS = sb("RS", [128, 2, 16], F32)
    ZSAVE = sb("ZSAVE", [128, 8], F32)
    HAL = sb("HAL", [128, 4, 8], F32)
    SGW = T1[:, 0, :].rearrange("p (a b) -> p a b", a=4)

    BLK1 = sb("BLK1", [128, 128], BF16)
    ONESB = sb("ONESB", [128, 128], BF16)
    SQB = sb("SQB", [128, 2, 512], BF16)
    KCOLS = sb("KCOLS", [128, 48], F32)
    QCOLS = sb("QCOLS", [128, 8], F32)
    MAXC = sb("MAXC", [128, 2], F32)
    S1 = sb("S1", [1, 16], F32)

    PS01 = es.enter_context(nc.psum_tensor("PS01", [128, 1024], F32))
    PS23 = es.enter_context(nc.psum_tensor("PS23", [128, 1024], F32))
    BANK = [PS01[:, 0:512], PS01[:, 512:1024], PS23[:, 0:512], PS23[:, 512:1024]]
    BANK += [es.enter_context(nc.psum_tensor(f"B{i}", [128, 512], F32))[:] for i in range(4, 8)]

    QT = ARENA[:, 0:2048].rearrange("p (c n) -> p c n", c=4)
    VC = ARENA[:, 2048:4096].rearrange("p (c n) -> p c n", c=4)
    CAT1 = ARENA[:, 0:4096].rearrange("p (c n) -> p c n", c=8)
    ET = ARENA[:, 4096:6144].rearrange("p (c n) -> p c n", c=4)
    UT = ARENA[:, 6144:10240].bitcast(F32).rearrange("p (c n) -> p c n", c=4)
    HID = ARENA[:, 0:8192].rearrange("p (c n) -> p c n", c=16)
    KTP = KT[:, :, 0:512]

    bank_ctr = [0]

    reserved = set()

    def nb(allowed=None):
        while (bank_ctr[0] % 8) in reserved or (allowed is not None and (bank_ctr[0] % 8) not in allowed):
            bank_ctr[0] += 1
        b = BANK[bank_ctr[0] % 8]
        bank_ctr[0] += 1
        return b

    def nb_reserve():
        while (bank_ctr[0] % 8) in reserved:
            bank_ctr[0] += 1
        i = bank_ctr[0] % 8
        bank_ctr[0] += 1
        reserved.add(i)
        return i

    def mm(out, lhsT, rhs, start=True, stop=True, **kw):
        P.op("pe", lambda e: e.matmul(out, lhsT, rhs, start=start, stop=stop, **kw), [lhsT, rhs], [out])

    def tr(out, in_, ident):
        P.op("pe", lambda e: e.transpose(out, in_, ident), [in_, ident], [out])

    def act(out, in_, func, bias=None, scale=None, accum=None):
        kw = {}
        if bias is not None:
            kw["bias"] = bias
        if scale is not None:
            kw["scale"] = scale
        if accum is not None:
            kw["accum_out"] = accum
        wr = [out] + ([accum] if accum is not None else [])
        P.op("act", lambda e: e.activation(out, in_, func, **kw), [in_, bias, scale], wr)

    def ts(eng, out, in0, s1, s2, op0, op1=None):
        if op1 is None:
            P.op(eng, lambda e: e.tensor_scalar(out, in0, s1, None, op0), [in0, s1], [out])
        else:
            P.op(eng, lambda e: e.tensor_scalar(out, in0, s1, s2, op0, op1), [in0, s1, s2], [out])

    def tt(eng, out, in0, in1, op):
        P.op(eng, lambda e: e.tensor_tensor(out, in0, in1, op), [in0, in1], [out])

    def stt(out, in0, scalar, in1, op0, op1):
        P.op("dve", lambda e: e.scalar_tensor_tensor(out, in0, scalar, in1, op0, op1), [in0, scalar, in1], [out])

    def cp(eng, out, in_):
        if eng == "act":
            P.op("act", lambda e: e.activation(out, in_, AF.Copy), [in_], [out])
        else:
            P.op(eng, lambda e: e.tensor_copy(out, in_), [in_], [out])

    def dma(q, out, in_, semkey, whole=False, **kw):
        P.op(q, lambda e: e.dma_start(out=out, in_=in_, **kw), [in_], [out], dma=True, semkey=semkey, whole=whole)

    def memset(eng, ap, val):
        P.op(eng, lambda e: e.memset(ap, val), [], [ap])

    def w_src(name, b):
        K = dict((n, k) for n, k, _ in W_SPECS)[name]
        w = wd[name]
        if K == 1024:
            return w[:, b * 512:(b + 1) * 512].rearrange("(kc p) n -> p kc n", p=128), 8, 512
        hf, mb = b // 4, b % 4
        return (w[hf * 2048:(hf + 1) * 2048, mb * 256:(mb + 1) * 256].rearrange("(kc p) n -> p kc n", p=128),
                16, 256)

    def scr_view(name, b):
        base, n = blk_of[name]
        _, kc, ncol = w_src(name, b)
        return wscr[base + b].rearrange("p (kc n) -> p kc n", kc=kc)

    def cast_all(names):
        for name in names:
            base, n = blk_of[name]
            for b in range(n):
                src, kc, ncol = w_src(name, b)
                dma("pool", scr_view(name, b), src, semkey="cast_" + name, whole=True)

    ws_ctr = [0]
    ws_pin = set()

    def ws_next():
        while (ws_ctr[0] % 4) in ws_pin:
            ws_ctr[0] += 1
        s_ = ws_ctr[0] % 4
        ws_ctr[0] += 1
        return s_

    def wload(name, b):
        s = ws_next()
        _, kc, ncol = w_src(name, b)
        dst = WS[:, s, :].rearrange("p (kc n) -> p kc n", kc=kc)
        dma("sp", dst, scr_view(name, b), semkey=f"ws{s}")
        return dst

    def wload_mod(layer, b):
        s = ws_next()
        dst = WS[:, s, :].rearrange("p (kc n) -> p kc n", kc=8)
        src = wd[f"w_mod{layer}"][:, b * 512:(b + 1) * 512].rearrange("(kc p) n -> p kc n", p=128)
        dma("pool", dst, src, semkey=f"wm{s}")
        return dst

    def vrows(ap1d, n):
        return ap1d.rearrange("(c p) -> c p", p=128)

    dma("sp", IDF[:], c_ident, "setup", whole=True)
    dma("sp", PERM[:], c_perm, "setup", whole=True)
    dma("sp", VB1[0:48, :], vrows(vd["b_mod0"], 48), "setup", whole=True)
    dma("sp", VB1[48:96, :], vrows(vd["b_mod1"], 48), "setup", whole=True)
    r = 0
    vb2_off = {}
    for l in range(2):
        for nm in ("ln_mix_g", "ln_mix_b", "ln_ff_g", "ln_ff_b"):
            dma("sp", VB2[r:r + 8, :], vrows(vd[f"{nm}{l}"], 8), "setup", whole=True)
            vb2_off[f"{nm}{l}"] = r
            r += 8
    dma("sp", VB2[r:r + 24, :], conv_w1.rearrange("t (c p) -> (t c) p", p=128), "setup", whole=True)
    vb2_off["conv"] = r
    r += 24
    dma("sp", VB2[r:r + 16, :], cvec.rearrange("b (c p) -> (b c) p", p=128), "setup", whole=True)
    vb2_off["cvec"] = r
    r += 16
    assert r == 104
    for i, n in enumerate(("lambda_q1_0", "lambda_k1_0", "lambda_q2_0", "lambda_k2_0")):
        dma("sp", LAMT[:, i, :], lam_in[n].partition_broadcast(128), "setup", whole=True)
    dma("sp", GSUB[:], subln_g0.partition_broadcast(128), "setup", whole=True)
    dma("sp", SGW, sgu_w0.rearrange("g p q -> p g q"), "setup", whole=True)
    dma("sp", BSGU[:], sgu_b0.rearrange("g p -> (g p)").partition_broadcast(1), "setup", whole=True)


    memset("dve", ONER[:], 1.0)
    memset("dve", ONERB[:], 1.0)
    cp("dve", BSG2[0:1, 0, :], BSGU[0:1, :])
    tt("dve", BSG2[0:1, 1, :], BSGU[0:1, :], BSG2[0:1, 0, :], ALU.subtract)
    memset("dve", NEGC[:], 0.0)
    memset("dve", ZSAVE[:], 0.0)
    memset("dve", VV[:].rearrange("p k (h c) -> p k h c", h=4)[:, :, :, 128:129], 1.0)
    cp("dve", IDB[:], IDF[:])
    memset("dve", BLK1[:], 0.0)
    memset("dve", ONESB[:], 1.0 / 1024.0)
    memset("dve", BLK1[0:64, 0:64], 1.0)
    memset("dve", BLK1[64:128, 64:128], 1.0)

    b0 = nb()
    tr(b0[:, 0:96], VB1[0:96, :], IDF[0:96, 0:96])
    cp("dve", COLS1[:], b0[:, 0:96])
    b1 = nb()
    tr(b1[:, 0:104], VB2[0:104, :], IDF[0:104, 0:104])
    cp("dve", COLS2[:], b1[:, 0:104])

    def col2(name, c):
        o = vb2_off[name] + c
        return COLS2[:, o:o + 1]

    co = vb2_off["cvec"]
    for cd in range(2):
        act(SIL[:, :, cd], COLS2[:, co + cd * 8: co + cd * 8 + 8], AF.Silu)

    bw = nb()
    for g in range(4):
        tr(bw[:, g * 128:(g + 1) * 128], SGW[:, g, :], IDF[:])
    cp("dve", WST[:].rearrange("p g q -> p (g q)"), bw[:, 0:512])

    tt("dve", LAMT[:, 0, :], LAMT[:, 0, :], LAMT[:, 1, :], ALU.mult)
    tt("dve", LAMT[:, 2, :], LAMT[:, 2, :], LAMT[:, 3, :], ALU.mult)
    P.op("dve", lambda e: e.reduce_sum(LAMS[:, 0:1], LAMT[:, 0, :], AX.X), [LAMT[:, 0, :]], [LAMS[:, 0:1]])
    P.op("dve", lambda e: e.reduce_sum(LAMS[:, 1:2], LAMT[:, 2, :], AX.X), [LAMT[:, 2, :]], [LAMS[:, 1:2]])
    act(LAMS[:, 2:4], LAMS[:, 0:2], AF.Exp)
    tt("dve", LAMS[:, 4:5], LAMS[:, 3:4], LAMS[:, 2:3], ALU.subtract)
    ts("dve", LAMS[:, 5:6], LAMS[:, 4:5], -LAMBDA_INIT, None, ALU.add)
    NEGLAM = LAMS[:, 5:6]
    ts("dve", GSUB[:], GSUB[:], 1.0 - LAMBDA_INIT, None, ALU.mult)

    def mod_blocks(layer, blocks, bm):
        for b in blocks:
            wb = wload_mod(layer, b)
            for j4 in range(4):
                j = b * 4 + j4
                for kc in range(8):
                    mm(bm[:, 2 * j:2 * j + 2], wb[:, kc, j4 * 128:(j4 + 1) * 128], SIL[:, kc, :],
                       start=(kc == 0), stop=(kc == 7))

    def modulation(layer, blocks, derive, bm=None):
        if bm is None:
            bm = nb()
            mod_blocks(layer, blocks, bm)
        j0, j1 = blocks[0] * 4, blocks[-1] * 4 + 4
        bmv = bm[:, 0:96].rearrange("p (j c) -> p j c", c=2)
        for cd in range(2):
            tt("dve", MODT[:, layer, j0:j1, cd], bmv[:, j0:j1, cd], COLS1[:, layer * 48 + j0:layer * 48 + j1],
               ALU.add)
        for cd in range(2):
            M = lambda w: MODT[:, layer, w * 8:(w + 1) * 8, cd]
            Dv = lambda k: DCOL[:, layer, cd, k, :]
            gm = COLS2[:, vb2_off[f"ln_mix_g{layer}"]: vb2_off[f"ln_mix_g{layer}"] + 8]
            bmx = COLS2[:, vb2_off[f"ln_mix_b{layer}"]: vb2_off[f"ln_mix_b{layer}"] + 8]
            if "a" in derive:
                ts("dve", Dv(0), M(1), 1.0, None, ALU.add)
                cp("dve", Dv(1), M(0))
            if "b" in derive:
                ts("dve", Dv(2), M(2), 1.0 / ALPHA, None, ALU.mult)
                ts("dve", Dv(6), M(4), 1.0, None, ALU.add)
                tt("dve", Dv(3), gm, Dv(6), ALU.mult)
                tt("dve", Dv(4), bmx, Dv(6), ALU.mult)
                tt("dve", Dv(4), Dv(4), M(3), ALU.add)
                ts("dve", Dv(5), M(5), 1.0 / ALPHA, None, ALU.mult)

    def dc(layer, cd, k, c):
        return DCOL[:, layer, cd, k, c:c + 1]

    modulation(0, [0, 1, 2, 3], "a")
    cast_all(["w_in0"])

    xin_ctr = [0]

    def load_tile(src_rows, cd, layer, dstx, dsth, tt_):
        k = xin_ctr[0] % 2
        xin_ctr[0] += 1
        dma("sp", XIN[:, k, :], src_rows, semkey=f"xin{k}")
        for half in range(2):
            bk = nb()
            for cc in range(4):
                c = half * 4 + cc
                tr(bk[:, cc * 128:(cc + 1) * 128], XIN[:, k, c * 128:(c + 1) * 128], IDF[:])
            if dstx is not None and os.environ.get("DBG_NOX") != "1":
                if os.environ.get("DBG_NOX") == "2":
                    for cc in range(4):
                        cp("dve", dstx[:, half * 4 + cc, tt_ * 128:(tt_ + 1) * 128], bk[:, cc * 128:(cc + 1) * 128])
                else:
                    cp("dve", dstx[:, half * 4:half * 4 + 4, tt_ * 128:(tt_ + 1) * 128],
                       bk[:, 0:512].rearrange("p (c n) -> p c n", c=4))
            for cc in range(4):
                c = half * 4 + cc
                act(dsth[:, c, tt_ * 128:(tt_ + 1) * 128], bk[:, cc * 128:(cc + 1) * 128], AF.Identity,
                    bias=dc(layer, cd, 1, c), scale=dc(layer, cd, 0, c))

    AXB = ARENA[:, 4096:8192].bitcast(F32).rearrange("p (a n) -> p a n", a=2)
    XINS = [XIN[:, 0, :], XIN[:, 1, :], AXB[:, 0, :], AXB[:, 1, :]]

    def load_group_dma(srcs):
        for t_, src in enumerate(srcs):
            dma("sp", XINS[t_], src, semkey=f"xin{t_}")

    def load_group_chunk(c, ntile, cd, layer, dstx, dsth):
        N = ntile * 128
        bk = nb()
        for t_ in range(ntile):
            tr(bk[:, t_ * 128:(t_ + 1) * 128], XINS[t_][:, c * 128:(c + 1) * 128], IDF[:])
        if dstx is not None:
            cp("dve", dstx[:, c, 0:N], bk[:, 0:N])
        act(dsth[:, c, 0:N], bk[:, 0:N], AF.Identity, bias=dc(layer, cd, 1, c), scale=dc(layer, cd, 0, c))

    def proj_fm(blk, m4, inT, N, bank):
        for kc in range(8):
            mm(bank[:, 0:N], blk[:, kc, m4 * 128:(m4 + 1) * 128], inT[:, kc, 0:N], start=(kc == 0), stop=(kc == 7))

    def proj_tm(blk, inT, tt_, bank):
        for kc in range(8):
            mm(bank[:, 0:512], inT[:, kc, tt_ * 128:(tt_ + 1) * 128], blk[:, kc, 0:512], start=(kc == 0),
               stop=(kc == 7))

    sq_ctr = [0]

    def norm_update(src, N, dstcol):
        k = sq_ctr[0] % 2
        sq_ctr[0] += 1
        act(SQB[:, k, 0:N], src, AF.Square)
        bkn = nb()
        mm(bkn[:, 0:N], BLK1[:], SQB[:, k, 0:N])
        P.op("dve", lambda e: e.reduce_max(dstcol, bkn[:, 0:N], AX.X), [bkn[:, 0:N]], [dstcol])

    def cols_max(cols, dst):
        P.op("dve", lambda e: e.reduce_max(dst, cols, AX.X), [cols], [dst])

    def bound_finalize():
        for j in range(2):
            bkt = nb()
            tr(bkt[0:1, 0:128], MAXC[:, j:j + 1], IDF[:])
            P.op("dve", (lambda e, bkt=bkt, j=j: e.reduce_max(S1[0:1, j:j + 1], bkt[0:1, 0:128], AX.X)),
                 [bkt[0:1, 0:128]], [S1[0:1, j:j + 1]])
        tt("dve", S1[0:1, 2:3], S1[0:1, 0:1], S1[0:1, 1:2], ALU.mult)
        act(S1[0:1, 3:4], S1[0:1, 2:3], AF.Ln)
        act(S1[0:1, 4:5], S1[0:1, 3:4], AF.Exp, scale=0.5)
        ts("dve", S1[0:1, 5:6], S1[0:1, 4:5], -1.01 / 8.0, None, ALU.mult)
        ts("dve", S1[0:1, 6:7], S1[0:1, 4:5], -1.01 / 8.0, None, ALU.mult)

    def bound_broadcast():
        bkb = nb()
        mm(bkb[:, 0:2], ONER[0:1, :], S1[0:1, 5:7])
        cp("dve", NEGC[:, 0:1], bkb[:, 0:1])

    qr_ctr = [0]

    def rope(bank, N, dst, normcol=None):
        k = qr_ctr[0] % 2
        qr_ctr[0] += 1
        cp("act", QRAW[:, k, 0:N], bank[:, 0:N])
        if normcol is not None:
            norm_update(QRAW[:, k, 0:N], N, normcol)
        b2 = nb()
        mm(b2[:, 0:N], PERM[:], QRAW[:, k, 0:N])
        tt("dve", T1[:, k, 0:N], QRAW[:, k, 0:N], TAB[:, 0, 0:N], ALU.mult)
        tt("dve", T3[:, k, 0:N], b2[:, 0:N], TAB[:, 1, 0:N], ALU.mult)
        tt("dve", dst, T1[:, k, 0:N], T3[:, k, 0:N], ALU.add)

    def sgu_norm_stats(bank, tt_):
        k = tt_
        for gi in range(4):
            P.op("dve", (lambda e, gi=gi: e.bn_stats(ST[:, k, gi, :], bank[:, gi * 128:(gi + 1) * 128])),
                 [bank[:, gi * 128:(gi + 1) * 128]], [ST[:, k, gi, :]])
        for gi in range(4):
            P.op("dve", (lambda e, gi=gi: e.bn_aggr(MV[:, k, gi, :], ST[:, k, gi, :])), [ST[:, k, gi, :]],
                 [MV[:, k, gi, :]])

    def sgu_norm_apply(banks, ntile):
        n = ntile * 4
        var = MV[:, 0:ntile, :, 1].rearrange("p t g -> p (t g)")
        act(RS[:, 0, 0:n], var, AF.Ln, bias=LN_EPS)
        act(RS[:, 1, 0:n], RS[:, 0, 0:n], AF.Exp, scale=-0.5)
        for tt_ in range(ntile):
            for gi in range(4):
                ts("dve", VC[:, tt_, gi * 128:(gi + 1) * 128], banks[tt_][:, gi * 128:(gi + 1) * 128],
                   MV[:, tt_, gi, 0:1], RS[:, 1, tt_ * 4 + gi:tt_ * 4 + gi + 1], ALU.subtract, ALU.mult)

    SBK = [[BANK[0], BANK[1]], [BANK[2], BANK[3]]]
    OBK = [BANK[4], BANK[5], BANK[6]]
    MISCB = BANK[7].bitcast(BF16)
    SPAIR = [PS01, PS23]

    deferred = []
    T2f = T2[:].rearrange("p a n -> p (a n)")
    T3f = T3[:].rearrange("p a n -> p (a n)")
    AEP = T3[:, 1, :].rearrange("p (s n) -> p s n", s=4)

    def osacc(j):
        return T2f[:, j * 129:(j + 1) * 129] if j < 6 else T3f[:, (j - 6) * 129:(j - 5) * 129]

    def run_deferred(n=None):
        k = 0
        while deferred and (n is None or k < n):
            deferred.pop(0)()
            k += 1

    def attention(qT, q0, Nq, ktiles, cat):
        nsub = Nq // 128
        nacc = 2 * nsub
        nk_ = len(ktiles)
        hooks = {4, 9, 14, 19, 24}
        for h in range(4):
            accs = {}
            first_in_bank = {}
            for sub in range(nsub):
                for i in range(2):
                    j = sub * 2 + i
                    bkk = OBK[j // 3]
                    accs[(sub, i)] = bkk[:, (j % 3) * 129:(j % 3) * 129 + 129]
                    first_in_bank[(sub, i)] = (j % 3 == 0)
            for kt in range(nk_ + 1):
                if kt < nk_:
                    Kh = ktiles[kt][0](h)
                    for i in range(2):
                        sbank = SBK[kt % 2][i]
                        mm(sbank[:, 0:Nq], Kh[i * 64:(i + 1) * 64, :], qT[i * 64:(i + 1) * 64, h, q0:q0 + Nq])
                    act(ET[:, (kt % 2) * 2:(kt % 2) * 2 + 2, 0:Nq],
                        SPAIR[kt % 2][:, :].rearrange("p (i n) -> p i n", i=2)[:, :, 0:Nq],
                        AF.Exp, bias=NEGC[:, 0:1], scale=0.125)
                if kt >= 1:
                    k1 = kt - 1
                    Vh = ktiles[k1][1](h)
                    for sub in range(nsub):
                        for i in range(2):
                            mm(accs[(sub, i)], ET[:, (k1 % 2) * 2 + i, sub * 128:(sub + 1) * 128], Vh,
                               start=(k1 == 0 and first_in_bank[(sub, i)]), stop=(k1 == nk_ - 1),
                               skip_group_check=True)
                if kt in hooks:
                    run_deferred(1)
            run_deferred()
            for b_ in range((nacc + 2) // 3):
                n_in = min(3, nacc - 3 * b_)
                dst = T2f[:, b_ * 387:b_ * 387 + n_in * 129] if b_ < 2 else T3f[:, 0:n_in * 129]
                cp("dve", dst, OBK[b_][:, 0:n_in * 129])
            p_ = h % 2
            R = SM[:, p_ * 2, :]
            SS = SM[:, p_ * 2 + 1, 0:4]
            LNV = SM[:, 4 + p_, 0:4]
            RSTD = SM[:, 4 + p_, 4:8]

            def stage1(nsub=nsub, nacc=nacc, R=R, SS=SS):
                n6 = min(nacc, 6)
                lcol = T2f[:, 0:n6 * 129].rearrange("p (j c) -> p j c", c=129)[:, :, 128:129]
                rout = R[:, 0:n6].rearrange("p (j o) -> p j o", o=1)
                P.op("dve", lambda e: e.reciprocal(rout, lcol), [lcol], [rout])
                if nacc > 6:
                    lcol2 = T3f[:, 0:258].rearrange("p (j c) -> p j c", c=129)[:, :, 128:129]
                    rout2 = R[:, 6:8].rearrange("p (j o) -> p j o", o=1)
                    P.op("dve", lambda e: e.reciprocal(rout2, lcol2), [lcol2], [rout2])
                Rv = R[:, 0:nacc].rearrange("p (s i) -> p s i", i=2)
                ts("dve", Rv[:, :, 1], Rv[:, :, 1], NEGLAM, None, ALU.mult)
                for sub in range(nsub):
                    ts("dve", AEP[:, sub, :], osacc(2 * sub)[:, 0:128], R[:, 2 * sub:2 * sub + 1], None, ALU.mult)
                    stt(AEP[:, sub, :], osacc(2 * sub + 1)[:, 0:128], R[:, 2 * sub + 1:2 * sub + 2], AEP[:, sub, :],
                        ALU.mult, ALU.add)
                    P.op("dve", (lambda e, sub=sub: e.scalar_tensor_tensor(JUNK[:], AEP[:, sub, :], 1.0, AEP[:, sub, :],
                                                                           ALU.mult, ALU.mult, accum_out=SS[:, sub:sub + 1])),
                         [AEP[:, sub, :]], [JUNK[:], SS[:, sub:sub + 1]])

            def stage2(nsub=nsub, SS=SS, LNV=LNV, RSTD=RSTD):
                act(LNV[:, 0:nsub], SS[:, 0:nsub], AF.Ln, bias=LN_EPS, scale=1.0 / 128.0)
                act(RSTD[:, 0:nsub], LNV[:, 0:nsub], AF.Exp, scale=-0.5)

            def stage3(nsub=nsub, RSTD=RSTD):
                for sub in range(nsub):
                    stt(AN[:, sub, :], AEP[:, sub, :], RSTD[:, sub:sub + 1], GSUB[:], ALU.mult, ALU.mult)

            def stage4(nsub=nsub):
                for sub in range(nsub):
                    tr(MISCB[:, sub * 128:(sub + 1) * 128], AN[:, sub, :], IDB[:])

            def stage5(h=h, q0=q0, Nq=Nq, cat=cat):
                cp("dve", cat[:, h, q0:q0 + Nq], MISCB[:, 0:Nq])

            deferred.extend([stage1, stage2, stage3, stage4, stage5])

    def sgu_mix(N, cat):
        ntile = N // 128
        for gi in range(4):
            bk = nb()
            for t_ in range(ntile):
                mm(bk[:, t_ * 128:(t_ + 1) * 128], VC[:, t_, gi * 128:(gi + 1) * 128], WST[:, gi, :],
                   start=True, stop=False)
                mm(bk[:, t_ * 128:(t_ + 1) * 128], ONERB[0:1, :], BSG2[0:1, 0, gi * 128:(gi + 1) * 128],
                   start=False, stop=False)
                mm(bk[:, t_ * 128:(t_ + 1) * 128], ONERB[0:1, :], BSG2[0:1, 1, gi * 128:(gi + 1) * 128],
                   start=False, stop=True)
            tt("dve", cat[:, 4 + gi, 0:N], bk[:, 0:N], UT[:, gi, 0:N], ALU.mult)

    pending_release = []

    class StatAcc:
        def __init__(self, XS, N):
            self.XS, self.N = XS, N
            while pending_release:
                pending_release.pop().release()
            self.im, self.ie = nb_reserve(), nb_reserve()
            self.Bm, self.Be = BANK[self.im], BANK[self.ie]
            self.pending = []
            self.n = 0

        def add(self, c):
            self.pending.append(c)
            if len(self.pending) > 1:
                self._emit(self.pending.pop(0))

        def _emit(self, c):
            N, XS = self.N, self.XS
            k = self.n % 2
            act(SQB[:, k, 0:N], XS[:, c, 0:N], AF.Square)
            cp("act", RB[:, k, 0:N], XS[:, c, 0:N])
            mm(self.Bm[:, 0:N], ONESB[:], RB[:, k, 0:N], start=(self.n == 0), stop=(self.n == 7))
            mm(self.Be[:, 0:N], ONESB[:], SQB[:, k, 0:N], start=(self.n == 0), stop=(self.n == 7))
            self.n += 1

        def finish(self):
            while self.pending:
                self._emit(self.pending.pop(0))
            assert self.n == 8

        def release(self):
            reserved.discard(self.im)
            reserved.discard(self.ie)

    def out_proj(name, cat, XS, N, layer, cd):
        st = StatAcc(XS, N)
        for ob in range(2):
            blk = wload(name, ob)
            for m4 in range(4):
                m = ob * 4 + m4
                bk = nb()
                proj_fm(blk, m4, cat, N, bk)
                stt(XS[:, m, 0:N], bk[:, 0:N], dc(layer, cd, 2, m), XS[:, m, 0:N], ALU.mult, ALU.add)
                st.add(m)
        return st

    def layernorm(XS, N, layer, cd, which, want_h, st):
        gname = f"ln_{which}_g{layer}"
        bname = f"ln_{which}_b{layer}"
        st.finish()
        Bm, Be = st.Bm, st.Be
        cp("act", T2[:, 0, 0:N], Bm[:, 0:N])
        tt("dve", T2[:, 1, 0:N], T2[:, 0, 0:N], Bm[:, 0:N], ALU.mult)
        tt("dve", T2[:, 1, 0:N], Be[:, 0:N], T2[:, 1, 0:N], ALU.subtract)
        act(T2[:, 1, 0:N], T2[:, 1, 0:N], AF.Ln, bias=EPS_R)
        act(Be[:, 0:N], T2[:, 1, 0:N], AF.Exp, scale=-0.5)
        for c in range(8):
            tt("dve", XS[:, c, 0:N], XS[:, c, 0:N], Bm[:, 0:N], ALU.subtract)
            tt("dve", XS[:, c, 0:N], XS[:, c, 0:N], Be[:, 0:N], ALU.mult)
            if want_h:
                act(HT[:, c, 0:N], XS[:, c, 0:N], AF.Identity, bias=dc(layer, cd, 4, c), scale=dc(layer, cd, 3, c))
            act(XS[:, c, 0:N], XS[:, c, 0:N], AF.Identity, bias=col2(bname, c), scale=col2(gname, c))
        pending_release.append(st)

    t_ctr = [0]

    def ffn(XS, N, layer, cd):
        n1, n2 = f"w_ff1_{layer}", f"w_ff2_{layer}"
        st = None
        for hf in range(2):
            if hf == 1:
                st = StatAcc(XS, N)
            for j4 in range(4):
                blk = wload(n1, hf * 4 + j4)
                first = (hf == 0 and j4 == 0)
                if first:
                    bks = [nb() for _ in range(4)]
                    for kc in range(8):
                        for m4 in range(4):
                            mm(bks[m4][:, 0:N], blk[:, kc, m4 * 128:(m4 + 1) * 128], HT[:, kc, 0:N],
                               start=(kc == 0), stop=(kc == 7))
                for m4 in range(4):
                    if first:
                        bk = bks[m4]
                    else:
                        bk = nb()
                        proj_fm(blk, m4, HT, N, bk)
                    k = t_ctr[0] % 2
                    t_ctr[0] += 1
                    cp("act", T1[:, k, 0:N], bk[:, 0:N])
                    stt(HID[:, j4 * 4 + m4, 0:N], bk[:, 0:N], 0.0, T1[:, k, 0:N], ALU.max, ALU.mult)
            for mb in range(4):
                blk = wload(n2, hf * 4 + mb)
                for m2 in range(2):
                    m = mb * 2 + m2
                    bk = nb()
                    for kc in range(16):
                        mm(bk[:, 0:N], blk[:, kc, m2 * 128:(m2 + 1) * 128], HID[:, kc, 0:N], start=(kc == 0),
                           stop=(kc == 15))
                    stt(XS[:, m, 0:N], bk[:, 0:N], dc(layer, cd, 5, m), XS[:, m, 0:N], ALU.mult, ALU.add)
                    if hf == 1:
                        st.add(m)
        return st

    stg_ctr = [0]

    def store_tiles(XS, ntile, dst_rows_fn):
        for t_ in range(ntile):
            k = stg_ctr[0] % 2
            stg_ctr[0] += 1
            for half in range(2):
                bk = nb()
                for cc in range(4):
                    tr(bk[:, cc * 128:(cc + 1) * 128], XS[:, half * 4 + cc, t_ * 128:(t_ + 1) * 128], IDF[:])
                cp("act" if half == 0 else "dve", STG[:, k, half * 512:(half + 1) * 512], bk[:, 0:512])
            dma("pool", dst_rows_fn(t_), STG[:, k, :], semkey=f"stg{k}")

    def layer0_group(src_fn, ntile, cd, slot, sample, tab0, ktiles_fn, kvout_fn, pre_ln2=None):
        N = ntile * 128
        XS = XR[:, slot]
        load_group_dma([src_fn(t_) for t_ in range(ntile)])
        for c in range(8):
            load_group_chunk(c, ntile, cd, 0, XS, HT)
        if sample:
            dma("sp", TAB[:, 0, 0:N], cosT[:, tab0:tab0 + N], semkey="tabc")
            dma("sp", TAB[:, 1, 0:N], sinT[:, tab0:tab0 + N], semkey="tabs")
        blk = wload("w_in0", 4)
        gbanks = []
        for t_ in range(ntile):
            bk = nb()
            gbanks.append(bk)
            proj_tm(blk, HT, t_, bk)
            sgu_norm_stats(bk, t_)
        sgu_norm_apply(gbanks, ntile)
        if stop < 3.02:
            return
        blk = wload("w_in0", 0)
        for m4 in range(4):
            bk = nb()
            proj_fm(blk, m4, HT, N, bk)
            if sample:
                rope(bk, N, QT[:, m4, 0:N], QCOLS[:, m4:m4 + 1])
            else:
                cp("act", QT[:, m4, 0:N], bk[:, 0:N])
                norm_update(QT[:, m4, 0:N], N, QCOLS[:, m4:m4 + 1])
        cols_max(QCOLS[:, 0:4], MAXC[:, 0:1])
        if not sample:
            blk = wload("w_in0", 1)
            for m4 in range(4):
                bk = nb()
                proj_fm(blk, m4, HT, N, bk)
                cp("act", KTP[:, m4, 0:N], bk[:, 0:N])
                norm_update(KTP[:, m4, 0:N], N, KCOLS[:, m4:m4 + 1])
            cols_max(KCOLS[:, 0:4], MAXC[:, 1:2])
            blkv = wload("w_in0", 2)
            for t_ in range(ntile):
                k = stg_ctr[0] % 2
                stg_ctr[0] += 1
                bk = nb()
                proj_tm(blk, HT, t_, bk)
                cp("dve", STG[:, k, 0:512], bk[:, 0:512])
                bk2 = nb()
                proj_tm(blkv, HT, t_, bk2)
                cp("act", STG[:, k, 512:1024], bk2[:, 0:512])
                cp("dve", VV[:, t_, :].rearrange("p (h c) -> p h c", h=4)[:, :, 0:128],
                   bk2[:, 0:512].rearrange("p (h c) -> p h c", h=4))
                kd, vd_ = kvout_fn(t_)
                dma("pool", kd, STG[:, k, 0:512], semkey=f"stg{k}")
                dma("pool", vd_, STG[:, k, 512:1024], semkey=f"stg{k}")
        bound_finalize()
        if stop < 3.03:
            return
        blk = wload("w_in0", 3)
        for m4 in range(4):
            bk = nb()
            proj_fm(blk, m4, HT, N, bk)
            cp("act" if m4 % 2 == 0 else "dve", UT[:, m4, 0:N], bk[:, 0:N])
        if stop < 3.2:
            return
        bound_broadcast()
        for (q0, Nq, kts) in ktiles_fn(N):
            attention(QT, q0, Nq, kts, HT)
        run_deferred()
        if stop < 3.3:
            return
        sgu_mix(N, HT)
        if stop < 3.4:
            return
        st = out_proj("w_out0", HT, XS, N, 0, cd)
        layernorm(XS, N, 0, cd, "mix", True, st)
        st = ffn(XS, N, 0, cd)
        if pre_ln2 is not None:
            pre_ln2()
        layernorm(XS, N, 0, cd, "ff", False, st)

    def l1_prologue(ntile, cd, slot):
        N = ntile * 128
        XS = XR[:, slot]
        for c in range(8):
            act(HT[:, c, 0:N], XS[:, c, 0:N], AF.Identity, bias=dc(1, cd, 1, c), scale=dc(1, cd, 0, c))

    def layer1_group(ntile, cd, slot, segs, xnext, use_prev, dst_rows_fn, skip_prologue=False):
        N = ntile * 128
        XS = XR[:, slot]
        if not skip_prologue:
            l1_prologue(ntile, cd, slot)
        if xnext is not None:
            for c in range(8):
                act(HX[:, c, 0:1], xnext[:, c, 0:1], AF.Identity, bias=dc(1, cd, 1, c), scale=dc(1, cd, 0, c))
        for half in range(2):
            bgb = wload("w_in1", 0 + half)
            cgb = wload("w_in1", 2 + half)
            xtb = wload("w_in1", 4 + half)
            for m4 in range(4):
                m = half * 4 + m4
                k = m % 2
                bC, bX, bB = nb(), nb(), nb()
                if half == 0 and m4 == 0:
                    for kc in range(8):
                        for (wb_, bk_) in ((cgb, bC), (xtb, bX), (bgb, bB)):
                            mm(bk_[:, 0:N], wb_[:, kc, 0:128], HT[:, kc, 0:N], start=(kc == 0), stop=(kc == 7))
                else:
                    proj_fm(cgb, m4, HT, N, bC)
                    proj_fm(xtb, m4, HT, N, bX)
                    proj_fm(bgb, m4, HT, N, bB)
                cp("act", T1[:, k, 0:N], bC[:, 0:N])
                Z = T3[:, k, 0:N]
                C = T2[:, k, 0:N]
                tt("dve", Z, bX[:, 0:N], T1[:, k, 0:N], ALU.mult)
                ts("dve", C, Z, col2("conv", 8 + m), None, ALU.mult)
                for (a, b) in segs:
                    stt(C[:, a + 1:b], Z[:, a:b - 1], col2("conv", 0 + m), C[:, a + 1:b], ALU.mult, ALU.add)
                    stt(C[:, a:b - 1], Z[:, a + 1:b], col2("conv", 16 + m), C[:, a:b - 1], ALU.mult, ALU.add)
                if use_prev:
                    stt(C[:, 0:1], ZSAVE[:, m:m + 1], col2("conv", 0 + m), C[:, 0:1], ALU.mult, ALU.add)
                    cp("dve", ZSAVE[:, m:m + 1], Z[:, N - 1:N])
                if xnext is not None:
                    cp("dve", HAL[:, 0, m:m + 1], C[:, N - 1:N])
                    cp("dve", HAL[:, 1, m:m + 1], bB[:, N - 1:N])
                tt("dve", CAT1[:, m, 0:N], bB[:, 0:N], C, ALU.mult)
            if xnext is not None:
                m0 = half * 4
                bH = nb()
                for m4 in range(4):
                    for kc in range(8):
                        mm(bH[:, 4 * m4:4 * m4 + 1], cgb[:, kc, m4 * 128:(m4 + 1) * 128], HX[:, kc, 0:1],
                           start=(kc == 0), stop=(kc == 7), skip_group_check=True)
                    for kc in range(8):
                        mm(bH[:, 4 * m4 + 2:4 * m4 + 3], xtb[:, kc, m4 * 128:(m4 + 1) * 128], HX[:, kc, 0:1],
                           start=(kc == 0), stop=(kc == 7), skip_group_check=True)
                bHv = bH[:, 0:16].rearrange("p (m c) -> p m c", c=4)
                cp("act", HAL[:, 2, m0:m0 + 4], bHv[:, :, 0])
                tt("dve", HAL[:, 3, m0:m0 + 4], bHv[:, :, 2], HAL[:, 2, m0:m0 + 4], ALU.mult)
                o_ = vb2_off["conv"] + 16 + m0
                tt("dve", HAL[:, 3, m0:m0 + 4], HAL[:, 3, m0:m0 + 4], COLS2[:, o_:o_ + 4], ALU.mult)
                tt("dve", HAL[:, 3, m0:m0 + 4], HAL[:, 3, m0:m0 + 4], HAL[:, 0, m0:m0 + 4], ALU.add)
                tt("dve", CAT1[:, m0:m0 + 4, N - 1], HAL[:, 3, m0:m0 + 4], HAL[:, 1, m0:m0 + 4], ALU.mult)
        st = out_proj("w_out1", CAT1, XS, N, 1, cd)
        layernorm(XS, N, 1, cd, "mix", True, st)
        st = ffn(XS, N, 1, cd)
        layernorm(XS, N, 1, cd, "ff", False, st)
        store_tiles(XS, ntile, dst_rows_fn)

    HTB = [HT, CAT1]

    def kv_srcs(g):
        return [xs[(g * 4 + t_) * 128:(g * 4 + t_ + 1) * 128, :] for t_ in range(4)]

    def kv_tab(g):
        dma("sp", TAB[:, 0, :], cosT[:, g * 512:(g + 1) * 512], semkey="tabc")
        dma("sp", TAB[:, 1, :], sinT[:, g * 512:(g + 1) * 512], semkey="tabs")

    NKV = 8 if stop >= 2 else 0
    if NKV:
        load_group_dma(kv_srcs(0))
        for c in range(8):
            load_group_chunk(c, 4, 0, 0, None, HTB[0])
        load_group_dma(kv_srcs(1))
        kv_tab(0)
        blk_k = wload("w_in0", 1)
        sk_ = (ws_ctr[0] - 1) % 4
        blk_v = wload("w_in0", 2)
        sv_ = (ws_ctr[0] - 1) % 4
        ws_pin.update([sk_, sv_])
        ibm2 = nb_reserve()
        bm2 = BANK[ibm2]
    for kvg in range(NKV):
        H = HTB[kvg % 2]
        for m4 in range(4):
            bk = nb()
            proj_fm(blk_k, m4, H, 512, bk)
            rope(bk, 512, KT[:, m4, kvg * 512:(kvg + 1) * 512], KCOLS[:, kvg * 4 + m4:kvg * 4 + m4 + 1])
            if kvg + 1 < NKV:
                for c in (2 * m4, 2 * m4 + 1):
                    load_group_chunk(c, 4, 0, 0, None, HTB[(kvg + 1) % 2])
        if kvg + 1 < NKV:
            kv_tab(kvg + 1)
        if kvg + 2 < NKV:
            load_group_dma(kv_srcs(kvg + 2))
        mod_blocks(0, [4 + kvg], bm2)
        for t_ in range(4):
            bk = nb()
            proj_tm(blk_v, H, t_, bk)
            cp("act" if t_ % 2 == 0 else "dve",
               VV[:, kvg * 4 + t_, :].rearrange("p (h c) -> p h c", h=4)[:, :, 0:128],
               bk[:, 0:512].rearrange("p (h c) -> p h c", h=4))
    if NKV:
        ws_pin.clear()
    for j in range(2 if stop >= 2 else 0):
        k = xin_ctr[0] % 2
        xin_ctr[0] += 1
        dma("sp", XIN[:, k, 0:512], ck[j * 128:(j + 1) * 128, :], semkey=f"xin{k}")
        bk = nb()
        for m4 in range(4):
            tr(bk[:, m4 * 128:(m4 + 1) * 128], XIN[:, k, m4 * 128:(m4 + 1) * 128], IDF[:])
        cp("dve", KT[:, :, (32 + j) * 128:(33 + j) * 128], bk[:, 0:512].rearrange("p (c n) -> p c n", c=4))
        for m4 in range(4):
            norm_update(KT[:, m4, (32 + j) * 128:(33 + j) * 128], 128, KCOLS[:, 32 + j * 4 + m4:33 + j * 4 + m4])
        dma("pool", VV[:, 32 + j, :].rearrange("p (h c) -> p h c", h=4)[:, :, 0:128],
            cv[j * 128:(j + 1) * 128, :].rearrange("p (h c) -> p h c", h=4), semkey=f"cvld{j}")

    if stop >= 2:
        cols_max(KCOLS[:, 0:40], MAXC[:, 1:2])
    if NKV:
        modulation(0, [4, 5, 6, 7, 8, 9, 10, 11], "b", bm=bm2)
        reserved.discard(ibm2)
    else:
        modulation(0, [4, 5, 6, 7, 8, 9, 10, 11], "b")
    cast_all(["w_out0", "w_ff1_0", "w_ff2_0"])

    def sample_ktiles(N):
        kts = []
        for kt in range(34):
            kts.append(((lambda h, kt=kt: KT[:, h, kt * 128:(kt + 1) * 128]),
                        (lambda h, kt=kt: VV[:, kt, h * 129:(h + 1) * 129])))
        return [(0, N, kts)]

    groups = [(0, 4), (4, 4), (8, 4), (12, 3), (15, 2)]

    def s_l0(g):
        t0, nt = groups[g]
        hook = None
        if g >= 2:
            hook = lambda: l1_prologue(groups[g - 1][1], 0, (g - 1) % 2)
        layer0_group(lambda t_: xs[(t0 + t_) * 128:(t0 + t_ + 1) * 128, :], nt, 0, g % 2, True, t0 * 128,
                     sample_ktiles, None, pre_ln2=hook)

    def s_l1(g):
        t0, nt = groups[g]
        xnext = XR[:, (g + 1) % 2] if g + 1 < len(groups) else None
        hoisted = (g >= 1 and g + 1 < len(groups))
        layer1_group(nt, 0, g % 2, [(0, nt * 128)], xnext, True,
                     lambda t_: ys[(t0 + t_) * 128:(t0 + t_ + 1) * 128, :], skip_prologue=hoisted)

    if stop >= 3:
        s_l0(0)
    cast_all(["w_in1", "w_out1", "w_ff1_1", "w_ff2_1"])
    if stop >= 4:
        for g in range(1, len(groups)):
            s_l0(g)
            if g == 1:
                modulation(1, list(range(12)), "ab")
            s_l1(g - 1)
        s_l1(len(groups) - 1)

    def prompt_ktiles(N):
        res = []
        for bi in range(2):
            kts = []
            for j in range(2):
                kt = bi * 2 + j
                kts.append(((lambda h, kt=kt: KTP[:, h, kt * 128:(kt + 1) * 128]),
                            (lambda h, kt=kt: VV[:, kt, h * 129:(h + 1) * 129])))
            res.append((bi * 256, 256, kts))
        return res

    for pg in range(2 if stop >= 5 else 0):
        r0 = pg * 512
        layer0_group(lambda t_: xp[r0 + t_ * 128: r0 + (t_ + 1) * 128, :], 4, 1, 0, False, 0, prompt_ktiles,
                     lambda t_: (nk[r0 + t_ * 128: r0 + (t_ + 1) * 128, :], nv[r0 + t_ * 128: r0 + (t_ + 1) * 128, :]))
        layer1_group(4, 1, 0, [(0, 256), (256, 512)], None, False,
                     lambda t_: yp[r0 + t_ * 128: r0 + (t_ + 1) * 128, :])

    ops = P.ops
    for o in ops:
        for d in o.deps:
            ops[d].has_dep = True
    ENG = ["pe", "act", "dve", "pool", "sp"]
    semkeys = sorted({o.semkey for o in ops if o.dma})
    sems = {}
    for e_ in ENG:
        sems[e_] = es.enter_context(nc.semaphore("s_" + e_))
    for k in semkeys:
        sems["d_" + k] = es.enter_context(nc.semaphore("d_" + k))
    ecount = {e_: 0 for e_ in ENG}
    dcount = {k: 0 for k in semkeys}
    for o in ops:
        if o.dma:
            dcount[o.semkey] += 16
            o.cnt = dcount[o.semkey]
        elif o.has_dep:
            ecount[o.eng] += 1
            o.cnt = ecount[o.eng]
    dtotal = dict(dcount)
    out_keys = [k for k in semkeys if k.startswith("stg")]

    block = es.enter_context(nc.Block())

    def emit_engine(ename, e):
        waited = {}
        for o in ops:
            if o.eng != ename:
                continue
            need = {}
            for d in o.deps:
                p = ops[d]
                if p.dma:
                    sk = "d_" + p.semkey
                    val = dtotal[p.semkey] if p.whole else p.cnt
                else:
                    if p.eng == "pe" and ename == "pe":
                        continue
                    sk = p.eng
                    val = p.cnt
                if val > need.get(sk, 0):
                    need[sk] = val
            for sk, val in need.items():
                if waited.get(sk, 0) >= val:
                    continue
                e.wait_ge(sems[sk], val)
                waited[sk] = val
            ins = o.fn(e)
            if o.dma:
                ins.then_inc(sems["d_" + o.semkey], 16)
            elif o.has_dep:
                ins.then_inc(sems[ename], 1)
        if ename == "pool":
            for k in out_keys:
                e.wait_ge(sems["d_" + k], dtotal[k])

    @block.tensor
    def _(e):
        emit_engine("pe", e)

    @block.scalar
    def _(e):
        emit_engine("act", e)

    @block.vector
    def _(e):
        emit_engine("dve", e)

    @block.gpsimd
    def _(e):
        emit_engine("pool", e)

    @block.sync
    def _(e):
        emit_engine("sp", e)

    es.close()
    return nc


_NC_CACHE = {}


def _rope_tables(order):
    pos = (np.asarray(order)[:, None] * 128 + np.arange(128)[None, :]).reshape(-1)
    row = (pos // 64).astype(np.float32)
    col = (pos % 64).astype(np.float32)
    inv = (1.0 / (np.float32(10000.0) ** (np.arange(16, dtype=np.float32) / np.float32(16)))).astype(np.float32)
    ang = [row[:, None] * inv[None, :], col[:, None] * inv[None, :]]
    cosT = np.zeros((128, 4096), np.float32)
    sinT = np.zeros((128, 4096), np.float32)
    for p in range(128):
        pm = p % 64
        s = pm // 32
        j = (pm % 32) // 16
        f = pm % 16
        cosT[p] = np.cos(ang[s][:, f])
        sinT[p] = np.sin(ang[s][:, f]) * (-1.0 if j == 0 else 1.0)
    return cosT, sinT


def kernel(**inp):
    f = lambda a: np.ascontiguousarray(np.asarray(a, dtype=np.float32))
    x_prompt, x_sample = f(inp["x_prompt"]), f(inp["x_sample"])
    cache_k0, cache_v0 = f(inp["cache_k0"]), f(inp["cache_v0"])
    c, c_ctx = f(inp["c"]), f(inp["c_ctx"])
    if "nc" not in _NC_CACHE:
        _NC_CACHE["nc"] = build_nc()
    nc = _NC_CACHE["nc"]
    ident = np.eye(128, dtype=np.float32)
    perm = np.zeros((128, 128), np.float32)
    for m in range(128):
        perm[m ^ 16, m] = 1.0
    shared = {"c_ident": ident, "c_perm": perm}
    for name, _, _ in W_SPECS:
        shared[name] = f(inp[name])
    for name in ("w_mod0", "w_mod1", "conv_w1", "subln_g0", "sgu_w0", "sgu_b0", "lambda_q1_0", "lambda_k1_0",
                 "lambda_q2_0", "lambda_k2_0"):
        shared[name] = f(inp[name])
    for name, _ in VEC_IN:
        shared[name] = f(inp[name])
    in_maps = []
    orders = []
    for core in range(8):
        b, half = core // 2, core % 2
        win = list(range(0, 17)) if half == 0 else list(range(15, 32))
        others = [t for t in range(32) if t not in win]
        order = win + others
        orders.append(order)
        xt = x_sample[b].reshape(32, 128, 1024)[order].reshape(4096, 1024)
        cosT, sinT = _rope_tables(order)
        m = dict(shared)
        m["xs"] = np.ascontiguousarray(xt)
        m["xp"] = np.ascontiguousarray(x_prompt[4 * core:4 * core + 4].reshape(1024, 1024))
        m["ck"] = np.ascontiguousarray(cache_k0[b].reshape(256, 512))
        m["cv"] = np.ascontiguousarray(cache_v0[b].reshape(256, 512))
        m["cvec"] = np.ascontiguousarray(np.stack([c[b], c_ctx], 0))
        m["cosT"] = cosT
        m["sinT"] = sinT
        in_maps.append(m)
    res = run_bass_kernel_spmd(nc, in_maps, core_ids=list(range(8)))
    y_prompt = np.zeros((32, 256, 1024), np.float32)
    y_sample = np.zeros((4, 4096, 1024), np.float32)
    new_k = np.zeros((32, 256, 4, 2, 64), np.float32)
    new_v = np.zeros((32, 256, 4, 128), np.float32)
    for core in range(8):
        r = res.results[core]
        b, half = core // 2, core % 2
        ysc = np.asarray(r["ys"])
        if half == 0:
            y_sample[b, 0:2048] = ysc[0:2048]
        else:
            y_sample[b, 2048:4096] = ysc[128:2176]
        y_prompt[4 * core:4 * core + 4] = np.asarray(r["yp"]).reshape(4, 256, 1024)
        new_k[4 * core:4 * core + 4] = np.asarray(r["nk"]).reshape(4, 256, 4, 2, 64)
        new_v[4 * core:4 * core + 4] = np.asarray(r["nv"]).reshape(4, 256, 4, 128)
    return (y_prompt, y_sample, new_k, new_v)
```
